# Optimizing a Trainium2 kernel written in Bass

```python
import math
import jax, jax.numpy as jnp
from jax import lax
import numpy as np

D_MODEL = 1024
BATCH = 16
SEQ = 256
DEPTH = 2
DEC_BATCH = 4
DEC_SEQ = 2048
PAST_LEN = 256

GRID_W = 64
N_MIXERS = 2
N_ATTN_LAYERS = (DEPTH + 1) // 2
N_HGRN_LAYERS = DEPTH // 2
N_MOD = 9
EPS = 1e-6
DA_HEADS = D_MODEL // 128
DA_DK = 64
DA_DV = 128
ROPE_THETA = 10000.0
Q_BLOCK = 128
HG_HEADS = D_MODEL // 128
HG_DK = 128
HG_DV = D_MODEL // HG_HEADS
CHUNK = 32
FFN_HIDDEN = ((8 * D_MODEL // 3 + 127) // 128) * 128

kernel_name = "diff_hgrn2_macaron_prefix_dit"


def rmsnorm(x, g):
    xf = x.astype(jnp.float32)
    y = xf * lax.rsqrt(jnp.mean(xf * xf, axis=-1, keepdims=True) + EPS)
    return (y * g.astype(jnp.float32)).astype(x.dtype)


def ada_pre(x, mod, j, g):
    shift = mod[:, 3 * j][:, None, :]
    scale = mod[:, 3 * j + 1][:, None, :]
    return rmsnorm(x, g) * (1 + scale) + shift


def ada_post(x, y, mod, j, g, res_w):
    gate = mod[:, 3 * j + 2][:, None, :]
    return x + res_w * gate * rmsnorm(y, g)


def swiglu(h, w_in, w_out):
    a, b = jnp.split(h @ w_in, 2, axis=-1)
    return (jax.nn.silu(a) * b) @ w_out


def ffn_sublayer(x, mod, j, g_pre, g_post, w_in, w_out):
    return ada_post(x, swiglu(ada_pre(x, mod, j, g_pre), w_in, w_out), mod, j, g_post, 0.5)


def axial_rope(x):
    L = x.shape[1]
    rows = L // GRID_W
    row = jnp.repeat(jnp.arange(rows, dtype=jnp.float32), GRID_W)
    col = jnp.tile(jnp.arange(GRID_W, dtype=jnp.float32), rows)
    nf = DA_DK // 4
    inv = ROPE_THETA ** (-jnp.arange(nf, dtype=jnp.float32) / nf)
    ang = jnp.stack([row[:, None] * inv, col[:, None] * inv], axis=1)
    cos = jnp.cos(ang)[None, :, None, None]
    sin = jnp.sin(ang)[None, :, None, None]
    xr = x.astype(jnp.float32).reshape(*x.shape[:-1], 2, 2, nf)
    x1 = xr[..., 0, :]
    x2 = xr[..., 1, :]
    out = jnp.stack([x1 * cos - x2 * sin, x1 * sin + x2 * cos], axis=-2)
    return out.reshape(x.shape).astype(x.dtype)


def diff_qkv(h, w_in):
    B, L, _ = h.shape
    z = h @ w_in
    qd = DA_HEADS * 2 * DA_DK
    q = z[..., :qd].reshape(B, L, DA_HEADS, 2, DA_DK)
    k = z[..., qd:2 * qd].reshape(B, L, DA_HEADS, 2, DA_DK)
    v = z[..., 2 * qd:].reshape(B, L, DA_HEADS, DA_DV)
    return q, k, v


def diff_lambda(lam_p, lam_init):
    lp = lam_p.astype(jnp.float32)
    return jnp.exp(jnp.sum(lp[0] * lp[1])) - jnp.exp(jnp.sum(lp[2] * lp[3])) + lam_init


def diff_attention(q, k, v, lam):
    B, Lq = q.shape[:2]
    nb = Lq // Q_BLOCK
    qb = q.reshape(B, nb, Q_BLOCK, DA_HEADS, 2, DA_DK).transpose(1, 0, 2, 3, 4, 5)
    kf = k.astype(jnp.float32)
    vf = v.astype(jnp.float32)
    scale = DA_DK ** -0.5

    def block(qi):
        s = jnp.einsum('bqhcd,bkhcd->bhcqk', qi.astype(jnp.float32), kf) * scale
        p = jax.nn.softmax(s, axis=-1)
        a = p[:, :, 0] - lam * p[:, :, 1]
        return jnp.einsum('bhqk,bkhv->bqhv', a, vf)

    o = lax.map(block, qb)
    return o.transpose(1, 0, 2, 3, 4).reshape(B, Lq, DA_HEADS, DA_DV)


def diff_out(o, lam_init, gain, w_out, dtype):
    B, L = o.shape[:2]
    o = rmsnorm(o, gain) * (1 - lam_init)
    return o.reshape(B, L, DA_HEADS * DA_DV).astype(dtype) @ w_out


def hgrn_proj(h, w_in, lb):
    B, L, _ = h.shape
    z = (h @ w_in).astype(jnp.float32)
    q, ff, fb, i, g = [a.reshape(B, L, HG_HEADS, -1) for a in jnp.split(z, 5, axis=-1)]
    f_f = lb[0] + (1 - lb[0]) * jax.nn.sigmoid(ff)
    f_b = lb[1] + (1 - lb[1]) * jax.nn.sigmoid(fb)
    return q, i, g, (jnp.log(f_f), 1 - f_f), (jnp.log(f_b), 1 - f_b)


def gla_chunked(q, k, v, logf, S0):
    B, L, H, DK = q.shape
    DV = v.shape[-1]
    nc = L // CHUNK

    def chunks(a):
        return a.reshape(B, nc, CHUNK, H, a.shape[-1]).transpose(1, 0, 3, 2, 4)

    mask = jnp.tril(jnp.ones((CHUNK, CHUNK), dtype=bool))[:, :, None]

    def step(S, inp):
        qc, kc, vc, gc = inp
        b = jnp.cumsum(gc, axis=2)
        o_inter = jnp.einsum('bhtk,bhkv->bhtv', qc * jnp.exp(b), S)
        diff = b[:, :, :, None, :] - b[:, :, None, :, :]
        dec = jnp.exp(jnp.where(mask, diff, -jnp.inf))
        A = jnp.einsum('bhtk,bhtsk,bhsk->bhts', qc, dec, kc)
        o_intra = jnp.einsum('bhts,bhsv->bhtv', A, vc)
        b_last = b[:, :, -1]
        S_new = jnp.exp(b_last)[..., None] * S + jnp.einsum(
            'bhsk,bhsv->bhkv', kc * jnp.exp(b_last[:, :, None] - b), vc)
        return S_new, o_inter + o_intra

    S_fin, o = lax.scan(step, S0, (chunks(q), chunks(k), chunks(v), chunks(logf)))
    return o.transpose(1, 0, 3, 2, 4).reshape(B, L, H, DV), S_fin


def hgrn_bidir(q, i, dir_f, dir_b, S0f, S0b):
    o_f, S_f = gla_chunked(q, dir_f[1], i, dir_f[0], S0f)
    fl = lambda a: a[:, ::-1]
    o_b, S_b = gla_chunked(fl(q), fl(dir_b[1]), fl(i), fl(dir_b[0]), S0b)
    return o_f + fl(o_b), S_f, S_b


def hgrn_out(o, g, gain, w_out, dtype):
    B, L = o.shape[:2]
    o = rmsnorm(o, gain) * jax.nn.sigmoid(g)
    return o.reshape(B, L, HG_HEADS * HG_DV).astype(dtype) @ w_out


def setup_inputs(seed: int = 0) -> dict:
    key = jax.random.key(seed)
    ks = jax.random.split(key, 24)
    f32 = jnp.float32

    def nrm(k, shape, scale):
        return jax.random.normal(k, shape, f32) * scale

    D = D_MODEL
    return {
        "x_prompt": nrm(ks[0], (BATCH, SEQ, D), 1.0),
        "x_sample": nrm(ks[1], (DEC_BATCH, DEC_SEQ, D), 1.0),
        "cache_k": nrm(ks[2], (DEC_BATCH, N_ATTN_LAYERS, PAST_LEN, DA_HEADS, 2, DA_DK), 1.0),
        "cache_v": nrm(ks[3], (DEC_BATCH, N_ATTN_LAYERS, PAST_LEN, DA_HEADS, DA_DV), 1.0),
        "state_hgrn": nrm(ks[4], (DEC_BATCH, N_HGRN_LAYERS, 2, HG_HEADS, HG_DK, HG_DV), 0.3),
        "c": nrm(ks[5], (DEC_BATCH, D), 1.0),
        "c_ctx": nrm(ks[6], (D,), 1.0),
        "w_mod": nrm(ks[7], (DEPTH, D, N_MOD * D), 0.5 * D ** -0.5),
        "b_mod": nrm(ks[8], (DEPTH, N_MOD * D), 0.02),
        "norm_g": 1.0 + nrm(ks[9], (DEPTH, 6, D), 0.02),
        "ffn_w_in": nrm(ks[10], (DEPTH, 2, D, 2 * FFN_HIDDEN), D ** -0.5),
        "ffn_w_out": nrm(ks[11], (DEPTH, 2, FFN_HIDDEN, D), FFN_HIDDEN ** -0.5),
        "attn_w_in": nrm(ks[12], (N_ATTN_LAYERS, D, 2 * DA_HEADS * 2 * DA_DK + DA_HEADS * DA_DV), D ** -0.5),
        "attn_w_out": nrm(ks[13], (N_ATTN_LAYERS, DA_HEADS * DA_DV, D), (DA_HEADS * DA_DV) ** -0.5),
        "attn_lambda": nrm(ks[14], (N_ATTN_LAYERS, 4, DA_DK), 0.1),
        "attn_subln": 1.0 + nrm(ks[15], (N_ATTN_LAYERS, DA_DV), 0.02),
        "hgrn_w_in": nrm(ks[16], (N_HGRN_LAYERS, D, 5 * HG_HEADS * HG_DK), D ** -0.5),
        "hgrn_w_out": nrm(ks[17], (N_HGRN_LAYERS, HG_HEADS * HG_DV, D), (HG_HEADS * HG_DV) ** -0.5),
        "hgrn_lower_bounds": nrm(ks[18], (DEPTH, 2, HG_HEADS * HG_DK), 0.5),
        "hgrn_gnorm": 1.0 + nrm(ks[19], (N_HGRN_LAYERS, HG_DV), 0.02),
    }


def reference(x_prompt, x_sample, cache_k, cache_v, state_hgrn, c, c_ctx, w_mod, b_mod, norm_g,
              ffn_w_in, ffn_w_out, attn_w_in, attn_w_out, attn_lambda, attn_subln,
              hgrn_w_in, hgrn_w_out, hgrn_lower_bounds, hgrn_gnorm):
    lb_soft = jax.nn.softmax(hgrn_lower_bounds.astype(jnp.float32), axis=0)
    lb_all = jnp.cumsum(lb_soft, axis=0) - lb_soft[0]
    mod_ctx_all = jnp.einsum('d,lde->le', jax.nn.silu(c_ctx), w_mod) + b_mod
    mod_lat_all = jnp.einsum('bd,lde->lbe', jax.nn.silu(c), w_mod) + b_mod[:, None]

    x = x_prompt
    Bp = x.shape[0]
    ks_list, vs_list, st_list = [], [], []
    for l in range(DEPTH):
        mod = mod_ctx_all[l].reshape(1, N_MOD, D_MODEL)
        x = ffn_sublayer(x, mod, 0, norm_g[l, 0], norm_g[l, 1], ffn_w_in[l, 0], ffn_w_out[l, 0])
        h = ada_pre(x, mod, 1, norm_g[l, 2])
        if l % N_MIXERS == 0:
            a = l // N_MIXERS
            lam_init = 0.8 - 0.6 * math.exp(-0.3 * l)
            lam = diff_lambda(attn_lambda[a], lam_init)
            q, k, v = diff_qkv(h, attn_w_in[a])
            o = diff_attention(q, k, v, lam)
            y = diff_out(o, lam_init, attn_subln[a], attn_w_out[a], h.dtype)
            ks_list.append(k)
            vs_list.append(v)
        else:
            r = l // N_MIXERS
            lb = lb_all[l].reshape(2, HG_HEADS, HG_DK)
            q, i, g, df, db = hgrn_proj(h, hgrn_w_in[r], lb)
            zero = jnp.zeros((Bp, HG_HEADS, HG_DK, HG_DV), jnp.float32)
            o, S_f, S_b = hgrn_bidir(q, i, df, db, zero, zero)
            y = hgrn_out(o, g, hgrn_gnorm[r], hgrn_w_out[r], h.dtype)
            st_list.append(jnp.stack([S_f, S_b], axis=1).astype(x.dtype))
        x = ada_post(x, y, mod, 1, norm_g[l, 3], 1.0)
        x = ffn_sublayer(x, mod, 2, norm_g[l, 4], norm_g[l, 5], ffn_w_in[l, 1], ffn_w_out[l, 1])
    y_prompt = x
    new_cache_k = jnp.stack(ks_list, axis=1)
    new_cache_v = jnp.stack(vs_list, axis=1)
    new_state_hgrn = jnp.stack(st_list, axis=1)

    x = x_sample
    Bs = x.shape[0]
    for l in range(DEPTH):
        mod = mod_lat_all[l].reshape(Bs, N_MOD, D_MODEL)
        x = ffn_sublayer(x, mod, 0, norm_g[l, 0], norm_g[l, 1], ffn_w_in[l, 0], ffn_w_out[l, 0])
        h = ada_pre(x, mod, 1, norm_g[l, 2])
        if l % N_MIXERS == 0:
            a = l // N_MIXERS
            lam_init = 0.8 - 0.6 * math.exp(-0.3 * l)
            lam = diff_lambda(attn_lambda[a], lam_init)
            q, k, v = diff_qkv(h, attn_w_in[a])
            q = axial_rope(q)
            k = axial_rope(k)
            k_all = jnp.concatenate([k, cache_k[:, a].astype(k.dtype)], axis=1)
            v_all = jnp.concatenate([v, cache_v[:, a].astype(v.dtype)], axis=1)
            o = diff_attention(q, k_all, v_all, lam)
            y = diff_out(o, lam_init, attn_subln[a], attn_w_out[a], h.dtype)
        else:
            r = l // N_MIXERS
            lb = lb_all[l].reshape(2, HG_HEADS, HG_DK)
            q, i, g, df, db = hgrn_proj(h, hgrn_w_in[r], lb)
            S0 = state_hgrn[:, r].astype(jnp.float32)
            o, _, _ = hgrn_bidir(q, i, df, db, S0[:, 0], S0[:, 1])
            y = hgrn_out(o, g, hgrn_gnorm[r], hgrn_w_out[r], h.dtype)
        x = ada_post(x, y, mod, 1, norm_g[l, 3], 1.0)
        x = ffn_sublayer(x, mod, 2, norm_g[l, 4], norm_g[l, 5], ffn_w_in[l, 1], ffn_w_out[l, 1])
    y_sample = x

    return (y_prompt, y_sample, new_cache_k, new_cache_v, new_state_hgrn)
```

```python
from contextlib import ExitStack
import math
import os
import numpy as np
import concourse.bass as bass
import concourse.mybir as mybir
from concourse.bass_utils import run_bass_kernel_spmd

F32 = mybir.dt.float32
BF16 = mybir.dt.bfloat16
AF = mybir.ActivationFunctionType
ALU = mybir.AluOpType

SAME_ENGINE_SYNC = True
D = 1024
T = 1536
NTB = 3
FH = 2816
NJ = 22
EPS = 1e-6
WSLOT = 4096
WRING = 3
ARENA_BYTES = 103 * 1024 - 15360 + 512


class Res:
    __slots__ = ("name", "w", "r", "excl")

    def __init__(self, name="", excl=False):
        self.name = name
        self.w = None
        self.r = {}
        self.excl = excl


class DmaSem:
    def __init__(self, h):
        self.h = h
        self.count = 0


def _merge(dst, tok, key):
    old = dst.get(key)
    if old is None or old[2] < tok[2]:
        dst[key] = tok


def _tok_key(tok):
    return tok[1] if tok[0] == "eng" else ("dma", id(tok[1]))


class Sched:
    ENGS = ("pe", "act", "dve", "pool", "sp")

    def __init__(self, nc, stack):
        self.nc = nc
        self.stack = stack
        self.dry = False
        self.q = {e: [] for e in self.ENGS}
        self.esem = {}
        for e in ("pe", "act", "dve", "pool"):
            self.esem[e] = stack.enter_context(nc.semaphore("es_" + e))
        self.n_dsem = 0

    def dma_sem(self, name=None):
        self.n_dsem += 1
        h = self.stack.enter_context(self.nc.semaphore(name or ("ds%d" % self.n_dsem)))
        return DmaSem(h)

    def op(self, e, fn, reads=(), writes=(), dsem=None, inc=16):
        if self.dry:
            return None
        deps = set()
        for r in reads:
            if r.w is not None:
                deps.add(r.w)
            if r.excl:
                deps.update(r.r.values())
        for w in writes:
            if w.w is not None:
                deps.add(w.w)
            deps.update(w.r.values())
        idx = len(self.q[e])
        if dsem is not None:
            dsem.count += inc
            tok = ("dma", dsem, dsem.count)
        else:
            tok = ("eng", e, idx)
        self.q[e].append(dict(fn=fn, deps=deps, tok=tok, dsem=dsem, inc=inc))
        k = _tok_key(tok)
        for r in reads:
            r.r[k] = tok
        for w in writes:
            w.w = tok
            w.r = {}
        return tok

    def _skip(self, e, d):
        if d[0] != "eng" or d[1] != e:
            return False
        if e == "pe":
            return True
        return not SAME_ENGINE_SYNC

    def emit(self, block, final_waits=()):
        needed = {e: set() for e in self.ENGS}
        for e in self.ENGS:
            for rec in self.q[e]:
                for d in rec["deps"]:
                    if d[0] == "eng" and not self._skip(e, d):
                        needed[d[1]].add(d[2])
        val = {}
        for e in self.ENGS:
            c = 0
            v = {}
            for i in range(len(self.q[e])):
                if i in needed[e]:
                    c += 1
                    v[i] = c
            val[e] = v

        def body(e):
            def f(eng):
                waited = {}
                for rec in self.q[e]:
                    ws = {}
                    for d in rec["deps"]:
                        if self._skip(e, d):
                            continue
                        if d[0] == "eng":
                            key = ("eng", d[1])
                            sem = self.esem[d[1]]
                            v = val[d[1]][d[2]]
                        else:
                            key = ("dma", id(d[1]))
                            sem = d[1].h
                            v = d[2]
                        if ws.get(key, (None, 0))[1] < v:
                            ws[key] = (sem, v)
                    for key, (sem, v) in ws.items():
                        if waited.get(key, 0) >= v:
                            continue
                        eng.wait_ge(sem, v)
                        waited[key] = v
                    ins = rec["fn"](eng)
                    if rec["dsem"] is not None:
                        ins.then_inc(rec["dsem"].h, rec["inc"])
                    elif rec["tok"][2] in val[e]:
                        ins.then_inc(self.esem[e], 1)
                if e == "sp":
                    for ds in final_waits:
                        if ds.count > 0:
                            eng.wait_ge(ds.h, ds.count)
            return f

        block.tensor(body("pe"))
        block.scalar(body("act"))
        block.vector(body("dve"))
        block.gpsimd(body("pool"))
        block.sync(body("sp"))


class Arena:
    def __init__(self, nc, nbytes):
        self.t = nc.alloc_sbuf_tensor("arena", [128, nbytes // 2], BF16)
        self.nbytes = nbytes
        self.off = 0
        self.cur = []
        self.prev = []
        self.released = []

    def reset(self):
        self.prev = self.cur + self.released
        self.cur = []
        self.released = []
        self.off = 0

    def push(self):
        return (self.off, len(self.cur))

    def pop(self, mark):
        rel = self.cur[mark[1]:]
        self.cur = self.cur[:mark[1]]
        self.released = self.released + rel
        self.prev = self.prev + rel
        self.off = mark[0]

    def view(self, off, shape, dt):
        save = self.off
        self.off = off
        v = self.alloc(shape, dt)
        self.off = save
        return v

    def res_from(self, olds):
        r = Res()
        for o in olds:
            if o.w is not None:
                _merge(r.r, o.w, _tok_key(o.w))
            for k, tk in o.r.items():
                _merge(r.r, tk, k)
        self.cur.append(r)
        return r

    def _newres(self, name=""):
        r = Res(name)
        for o in self.prev:
            if o.w is not None:
                _merge(r.r, o.w, _tok_key(o.w))
            for k, tk in o.r.items():
                _merge(r.r, tk, k)
        self.cur.append(r)
        return r

    def alloc(self, shape, dt, name=""):
        n = 1
        for s in shape:
            n *= s
        nb = n * (4 if dt == F32 else 2)
        nb = (nb + 31) // 32 * 32
        assert self.off + nb <= self.nbytes, ("arena overflow", name, self.off, nb)
        v = self.t[:, self.off // 2:(self.off + nb) // 2]
        self.off += nb
        if dt == F32:
            v = v.bitcast(F32)
        v = v[:, 0:n]
        if len(shape) == 2:
            v = v.rearrange("p (a b) -> p a b", a=shape[0])
        elif len(shape) == 3:
            v = v.rearrange("p (a b c) -> p a b c", a=shape[0], b=shape[1])
        return v

    def res(self, name=""):
        return self._newres(name)

    def resgrid(self, *dims):
        if len(dims) == 1:
            return [self._newres() for _ in range(dims[0])]
        return [self.resgrid(*dims[1:]) for _ in range(dims[0])]


class Ring:
    def __init__(self, items):
        self.items = items
        self.i = 0

    def get(self):
        it = self.items[self.i % len(self.items)]
        self.i += 1
        return it


CST_LAM = 0
CST_LB = 256
CST_BMOD = 288
CST_NG = 432
CST_GN = 528
CST_SUB = 529
CST_C = 530
CST_M = 546
CST_RM = 548
CST_N = 550
CB_ID = 0
CB_ONES = 128
CB_MA = 256
CB_MB = 384
CB_N = 512


def build_program(stop="full"):
    nc = bass.Bass("TRN2", target_bir_lowering=False)
    RG = [[2 * i_, 2 * i_ + 1] for i_ in range(max(1, int(os.environ.get("K_CORES", "8")) // 2))]

    def din(name, shape, dt=F32):
        return nc.dram_tensor(name, list(shape), dt, kind="ExternalInput").ap()

    def dout(name, shape, dt=F32):
        return nc.dram_tensor(name, list(shape), dt, kind="ExternalOutput").ap()

    xT_in = din("xT_in", [128, 8, T])
    cst_in = din("cst", [128, CST_N])
    cb_in = din("cb", [128, CB_N])
    rope_in = din("rope", [128, 2, 1024])
    ck_in = din("ckT", [128, 8, 256])
    cv_in = din("cv", [128, 2, 1024])
    s0_in = din("s0", [128, 8, 128])
    wmod = din("wmod", [2, 18, 128, 8, 512])
    wfin = din("wfin", [2, 2, 11, 128, 8, 512])
    wfout = din("wfout", [2, 2, 8, 128, 22, 128])
    wain = din("wain", [10, 128, 8, 512])
    waout = din("waout", [2, 128, 8, 512])
    whin = din("whin", [8, 128, 8, 512])
    whi = din("whi", [8, 128, 8, 128])
    whout = din("whout", [2, 128, 8, 512])
    yT = dout("yT", [128, 8, T])
    nkT = dout("nkT", [128, 8, 512])
    nv = dout("nv", [128, 4, 1024])
    nS = dout("nS", [128, 2, 2, 8, 128])
    cinK = nc.dram_tensor("cinK", [1024, 1024], BF16)
    coutK = nc.dram_tensor("coutK", [2048, 1024], BF16)
    cinV = nc.dram_tensor("cinV", [1024, 1024], BF16)
    coutV = nc.dram_tensor("coutV", [2048, 1024], BF16)
    cin2 = nc.dram_tensor("cin2", [1024, 128], F32)
    cout2 = nc.dram_tensor("cout2", [2048, 128], F32)

    with ExitStack() as st:
        S = Sched(nc, st)
        xT = nc.alloc_sbuf_tensor("xT", [128, 8, T], F32)
        hT = nc.alloc_sbuf_tensor("hT", [128, 8, T], BF16)
        wr = [nc.alloc_sbuf_tensor("wr%d" % i, [128, WSLOT], BF16) for i in range(WRING)]
        cst = nc.alloc_sbuf_tensor("cst_sb", [128, CST_N], F32)
        cb = nc.alloc_sbuf_tensor("cb_sb", [128, CB_N], BF16)
        onesrow = nc.alloc_sbuf_tensor("onesrow", [128, T], BF16)
        scT = nc.alloc_sbuf_tensor("scT", [128, 8, 2], BF16)
        modv = nc.alloc_sbuf_tensor("modv", [128, 2, 72, 2], F32)
        gsT = nc.alloc_sbuf_tensor("gsT", [128, 2, 3, 2, 8], F32)
        ggT = nc.alloc_sbuf_tensor("ggT", [128, 2, 3, 2, 8], F32)
        sml = nc.alloc_sbuf_tensor("sml", [128, 64], F32)
        nb_rstd = nc.alloc_sbuf_tensor("nb_rstd", [128, NTB, 512], F32)
        nb_rt = nc.alloc_sbuf_tensor("nb_rt", [128, 512], F32)
        nb_tmp = nc.alloc_sbuf_tensor("nb_tmp", [128, 2, 512], F32)
        nb_sq = nc.alloc_sbuf_tensor("nb_sq", [128, 3, 512], BF16)
        psall = nc.alloc_psum_tensor("psall", [128, 4096], F32)
        psb = [psall[:, i * 512:(i + 1) * 512] for i in range(8)]
        ar = Arena(nc, ARENA_BYTES)

        ident = cb[:, CB_ID:CB_ID + 128]
        ones = cb[:, CB_ONES:CB_ONES + 128]
        maskA = cb[:, CB_MA:CB_MA + 128]
        maskB = cb[:, CB_MB:CB_MB + 128]

        sem_w = [S.dma_sem("w%d" % i) for i in range(WRING)]
        sem_in = [S.dma_sem("in%d" % i) for i in range(8)]
        sem_out = [S.dma_sem("out%d" % i) for i in range(6)]
        sem_cc = S.dma_sem("cc")
        sem_misc = [S.dma_sem("mi%d" % i) for i in range(8)]
        sem_kv = [S.dma_sem("kv%d" % i) for i in range(4)]

        class WS:
            pass
        W = WS()
        W.plan = []
        W.n = 0
        W.issued = 0
        W.res = [Res("w%d" % i) for i in range(WRING)]

        def w_issue(m):
            src, shp = W.plan[m]
            slot = m % WRING
            n = shp[0] * shp[1]
            dst = wr[slot][:, 0:n].rearrange("p (a b) -> p a b", a=shp[0])
            S.op("pool", lambda e, dst=dst, src=src: e.dma_start(out=dst, in_=src, max_dma_last_dim=2048),
                 writes=[W.res[slot]], dsem=sem_w[slot])

        def w_next(src, shp, keep=0):
            if S.dry:
                W.plan.append((src, shp))
                return wr[0][:, 0:shp[0] * shp[1]].rearrange("p (a b) -> p a b", a=shp[0]), W.res[0]
            n = W.n
            W.n += 1
            while W.issued < min(len(W.plan), n + WRING - keep):
                w_issue(W.issued)
                W.issued += 1
            slot = n % WRING
            return wr[slot][:, 0:shp[0] * shp[1]].rearrange("p (a b) -> p a b", a=shp[0]), W.res[slot]

        def MM(out, lhsT, rhs, start, stop, reads, wres, skip=False):
            S.op("pe", lambda e: e.matmul(out, lhsT=lhsT, rhs=rhs, start=start, stop=stop, skip_group_check=skip),
                 reads=reads, writes=[wres])

        def TR(out, in_, reads, wres):
            S.op("pe", lambda e: e.transpose(out, in_, ident), reads=reads, writes=[wres])

        def ACT(out, in_, func, reads, writes, scale=1.0, bias=0.0):
            S.op("act", lambda e: e.activation(out=out, in_=in_, func=func, bias=bias, scale=scale),
                 reads=reads, writes=writes)

        def TT(eng, out, in0, in1, op, reads, writes):
            S.op(eng, lambda e: e.tensor_tensor(out=out, in0=in0, in1=in1, op=op), reads=reads, writes=writes)

        def TS(eng, out, in0, s1, s2, op0, op1, reads, writes):
            S.op(eng, lambda e: e.tensor_scalar(out=out, in0=in0, scalar1=s1, scalar2=s2, op0=op0, op1=op1),
                 reads=reads, writes=writes)

        def STT(out, in0, scalar, in1, op0, op1, reads, writes):
            S.op("dve", lambda e: e.scalar_tensor_tensor(out=out, in0=in0, scalar=scalar, in1=in1, op0=op0, op1=op1),
                 reads=reads, writes=writes)

        def CP(eng, out, in_, reads, writes):
            if eng == "act":
                S.op("act", lambda e: e.activation(out=out, in_=in_, func=AF.Copy), reads=reads, writes=writes)
            else:
                S.op(eng, lambda e: e.tensor_copy(out=out, in_=in_), reads=reads, writes=writes)

        def RCP(out, in_, reads, writes):
            S.op("dve", lambda e: e.reciprocal(out=out, in_=in_), reads=reads, writes=writes)

        def DMA(q, out, in_, reads, writes, dsem):
            S.op(q, lambda e: e.dma_start(out=out, in_=in_), reads=reads, writes=writes, dsem=dsem)

        def tbs(tb):
            return slice(tb * 512, (tb + 1) * 512)

        R_x = [[Res() for _ in range(NTB)] for _ in range(8)]
        R_h = [[Res() for _ in range(NTB)] for _ in range(8)]
        R_ps = [Res("ps%d" % i, excl=True) for i in range(8)]
        R_cst = Res("cst")
        R_cb = Res("cb")
        R_mod = Res("mod")
        R_sml = Res("sml")
        R_sc = Res("sc")
        R_dram = {k: Res(k) for k in ["cinK", "coutK", "cinV", "coutV", "cin2", "cout2", "yT", "nkT", "nv", "nS"]}

        def body():
            gen = Ring([(psb[i], R_ps[i]) for i in range(4)])
            outsem = Ring(sem_out)
            DMA("sp", cst[:, :], cst_in[:, :], [], [R_cst], sem_in[0])
            S.op("pool", lambda e: e.dma_start(out=cb[:, :], in_=cb_in[:, :]), writes=[R_cb], dsem=sem_in[1])
            for tb in range(NTB):
                DMA("sp", xT[:, :, tbs(tb)], xT_in[:, :, tbs(tb)], [], [R_x[fo][tb] for fo in range(8)], sem_in[2 + tb])
            S.op("dve", lambda e: e.memset(onesrow[:, :], 1.0), writes=[R_cb])
            ACT(scT[:, :, :], cst[:, CST_C:CST_C + 16].rearrange("p (k v) -> p k v", k=8), AF.Silu, [R_cst], [R_sc])

            def mod_slab(l, s):
                pm, rpm = psb[7], R_ps[7]
                wv, wres = w_next(wmod[l, s], (8, 512))
                for cc in range(4):
                    ch = s * 4 + cc
                    for k in range(8):
                        MM(pm[:, ch * 2:ch * 2 + 2], wv[:, k, cc * 128:(cc + 1) * 128], scT[:, k, :],
                           k == 0, k == 7, [wres, R_sc], rpm)

            def mod_fin(l, j, part="both"):
                pm, rpm = psb[7], R_ps[7]
                c0, c1 = {"both": (0, 24), "pre": (0, 16), "post": (16, 24)}[part]
                bm = cst[:, CST_BMOD + l * 72 + 24 * j + c0:CST_BMOD + l * 72 + 24 * j + c1]
                for v in range(2):
                    TT("dve", modv[:, l, 24 * j + c0:24 * j + c1, v], pm[:, 48 * j + 2 * c0 + v:48 * j + 2 * c1:2], bm, ALU.add,
                       [rpm, R_cst], [R_mod])
                if True:
                    for v in range(2):
                        sc_ = modv[:, l, (3 * j + 1) * 8:(3 * j + 2) * 8, v]
                        gt_ = modv[:, l, (3 * j + 2) * 8:(3 * j + 3) * 8, v]
                        gpre = cst[:, CST_NG + (l * 6 + 2 * j) * 8:CST_NG + (l * 6 + 2 * j + 1) * 8]
                        gpost = cst[:, CST_NG + (l * 6 + 2 * j + 1) * 8:CST_NG + (l * 6 + 2 * j + 2) * 8]
                        rw = 1.0 if j == 1 else 0.5
                        if part in ("both", "pre"):
                            STT(gsT[:, l, j, v, :], sc_, 1.0, gpre, ALU.add, ALU.mult, [R_mod, R_cst], [R_mod])
                        if part in ("both", "post"):
                            STT(ggT[:, l, j, v, :], gt_, rw, gpost, ALU.mult, ALU.mult, [R_mod, R_cst], [R_mod])

            class NB:
                pass

            NBP = NB()
            NBP.rstd = nb_rstd
            NBP.R_rstd = [Res() for _ in range(NTB)]
            NBP.rt = nb_rt
            NBP.R_rt = Res()
            NBP.tmp = Ring([(nb_tmp[:, i, :], Res()) for i in range(2)])
            NBP.sq = Ring([(nb_sq[:, i, :], Res()) for i in range(3)])

            def norm_bufs():
                return NBP

            def rstd_from(b, tb, ssb, rss, n_feat):
                ACT(b.rt[:, :], ssb[:, :], AF.Ln, [rss], [b.R_rt], scale=1.0 / n_feat, bias=EPS)
                ACT(b.rstd[:, tb, :], b.rt[:, :], AF.Exp, [b.R_rt], [b.R_rstd[tb]], scale=-0.5)

            def prenorm(b, l, j):
                for tb in range(NTB):
                    prenorm_tb(b, l, j, tb)

            def prenorm_tb(b, l, j, tb):
                if True:
                    v = 0 if tb == 0 else 1
                    ssb, rss = psb[4 + tb], R_ps[4 + tb]
                    for fo in range(8):
                        sq, rsq = b.sq.get()
                        ACT(sq, xT[:, fo, tbs(tb)], AF.Square, [R_x[fo][tb]], [rsq])
                        MM(ssb[:, :], ones, sq, fo == 0, fo == 7, [rsq, R_cb], rss)
                    rstd_from(b, tb, ssb, rss, 1024.0)
                    for fo in range(8):
                        tm, rtm = b.tmp.get()
                        TT("dve" if fo % 2 == 0 else "pool", tm, xT[:, fo, tbs(tb)], b.rstd[:, tb, :], ALU.mult,
                           [R_x[fo][tb], b.R_rstd[tb]], [rtm])
                        ACT(hT[:, fo, tbs(tb)], tm, AF.Identity, [rtm, R_mod], [R_h[fo][tb]],
                            scale=gsT[:, l, j, v, fo:fo + 1], bias=modv[:, l, 3 * j * 8 + fo, v:v + 1])

            def post_chunk(b, bank, rbank, fo, tb, yb, R_yb):
                PC = int(os.environ.get("K_PC", "7"))
                sq, rsq = b.sq.get()
                if PC & 1:
                    ACT(sq, bank[:, :], AF.Square, [rbank], [rsq])
                if PC & 2:
                    CP("dve", yb[:, fo, tbs(tb)], bank[:, :], [rbank], [R_yb[fo][tb]])
                if PC & 4:
                    MM(psb[4 + tb][:, :], ones, sq, fo == 0, fo == 7, [rsq, R_cb], R_ps[4 + tb])

            def post_finish(b, l, j, yb, R_yb, nxt=None):
                for tb in range(NTB):
                    v = 0 if tb == 0 else 1
                    rstd_from(b, tb, psb[4 + tb], R_ps[4 + tb], 1024.0)
                    for fo in range(8):
                        tm, rtm = b.tmp.get()
                        TT("pool", tm, yb[:, fo, tbs(tb)], b.rstd[:, tb, :], ALU.mult, [R_yb[fo][tb], b.R_rstd[tb]], [rtm])
                        STT(xT[:, fo, tbs(tb)], tm, ggT[:, l, j, v, fo:fo + 1], xT[:, fo, tbs(tb)], ALU.mult, ALU.add,
                            [rtm, R_mod, R_x[fo][tb]], [R_x[fo][tb]])
                    if tb > 0 and nxt is not None:
                        prenorm_tb(b, nxt[0], nxt[1], tb - 1)
                if nxt is not None:
                    prenorm_tb(b, nxt[0], nxt[1], NTB - 1)

            def ffn(l, i, j, nxt=None, hooks=()):
                hooks = list(hooks)
                ar.reset()
                b = norm_bufs()
                uT = ar.alloc([NJ, T], BF16, "uT")
                R_u = ar.resgrid(NJ, NTB)
                sa_t = ar.alloc([2, 512], F32, "sa")
                sa = Ring([(sa_t[:, i2, :], ar.res()) for i2 in range(2)])
                DBG = int(os.environ.get("K_DBG", "9"))
                for s in range(11 if DBG >= 2 else 0):
                    wv, wres = w_next(wfin[l, i, s], (8, 512))
                    for jj in range(2):
                        hc = 2 * s + jj
                        for tb in range(NTB):
                            pa, rpa = gen.get()
                            pb, rpb = gen.get()
                            for k in range(8):
                                MM(pa[:, :], wv[:, k, jj * 128:(jj + 1) * 128], hT[:, k, tbs(tb)], k == 0, k == 7,
                                   [wres, R_h[k][tb]], rpa)
                            for k in range(8):
                                MM(pb[:, :], wv[:, k, 256 + jj * 128:256 + (jj + 1) * 128], hT[:, k, tbs(tb)], k == 0, k == 7,
                                   [wres, R_h[k][tb]], rpb)
                            s_, rs_ = sa.get()
                            ACT(s_, pa[:, :], AF.Silu, [rpa], [rs_])
                            TT("dve", uT[:, hc, tbs(tb)], s_, pb[:, :], ALU.mult, [rs_, rpb], [R_u[hc][tb]])
                    if hooks:
                        hooks.pop(0)()
                for fo in range(8 if DBG >= 3 else 0):
                    wv, wres = w_next(wfout[l, i, fo], (22, 128))
                    for tb in range(NTB):
                        pa, rpa = gen.get()
                        for hc in range(NJ):
                            MM(pa[:, :], wv[:, hc, :], uT[:, hc, tbs(tb)], hc == 0, hc == NJ - 1,
                               [wres, R_u[hc][tb]], rpa)
                        if DBG >= 4:
                            post_chunk(b, pa, rpa, fo, tb, hT, R_h)
                    if hooks:
                        hooks.pop(0)()
                while hooks:
                    hooks.pop(0)()
                if DBG >= 5:
                    post_finish(b, l, j, hT, R_h, nxt)

            def attention(nxt=None):
                l, j = 0, 1
                lam_init = 0.8 - 0.6 * math.exp(-0.3 * l)
                ar.reset()
                qTp = ar.alloc([8, 512], BF16, "qTp"); R_qp = ar.resgrid(8)
                kTp = ar.alloc([8, 512], BF16, "kTp"); R_kp = ar.resgrid(8)
                Vp = ar.alloc([4, 1024], BF16, "Vp"); R_vp = ar.resgrid(4, 2)
                b = norm_bufs()
                qTs = ar.alloc([8, 1024], BF16, "qTs"); R_qs = ar.resgrid(8, 2)
                ckT = ar.alloc([2, 8, 256], BF16, "ckT"); R_ck = ar.res()
                cvb = ar.alloc([2, 1024], BF16, "cvb"); R_cv = ar.res()
                mark = ar.push()
                rope = ar.alloc([2, 1024], F32, "rope"); R_rope = ar.res()
                r1_t = ar.alloc([2, 512], F32, "r1")
                r2_t = ar.alloc([2, 512], F32, "r2")
                r1 = Ring([(r1_t[:, i, :], ar.res()) for i in range(2)])
                r2 = Ring([(r2_t[:, i, :], ar.res()) for i in range(2)])
                kst_t = ar.alloc([2, 512], BF16, "kst")
                kst = Ring([(kst_t[:, i, :], ar.res(), sem_misc[i]) for i in range(2)])
                vst_t = ar.alloc([2, 512], BF16, "vst")
                vst = Ring([(vst_t[:, i, :], ar.res(), sem_misc[2 + i]) for i in range(2)])
                fo_t = ar.alloc([2, 512], F32, "fout")
                fst = Ring([(fo_t[:, i, :], ar.res(), sem_misc[4 + i]) for i in range(2)])
                tq_t = ar.alloc([64], F32, "lamt")
                R_tq = ar.res()
                cosT = rope[:, 0, :]
                sinT = rope[:, 1, :]

                DMA("sp", rope[:, :, :], rope_in[:, :, :], [], [R_rope], sem_in[7])
                S.op("dve", lambda e: e.memset(ckT[:, :, :, :], 0.0), writes=[R_ck])
                for c_ in range(2):
                    S.op("pool", lambda e, c_=c_: e.dma_start(out=ckT[c_ * 64:(c_ + 1) * 64, c_, :, :],
                                                              in_=ck_in[c_ * 64:(c_ + 1) * 64, :, :], max_dma_last_dim=1024),
                         writes=[R_ck], dsem=sem_misc[6])
                S.op("pool", lambda e: e.dma_start(out=cvb[:, :, :], in_=cv_in[:, :, :], max_dma_last_dim=2048),
                     writes=[R_cv], dsem=sem_misc[7])
                lamt = tq_t
                for i2 in range(2):
                    TT("dve", lamt[:, 0:64], cst[:, CST_LAM + i2 * 128:CST_LAM + i2 * 128 + 64],
                       cst[:, CST_LAM + i2 * 128 + 64:CST_LAM + i2 * 128 + 128], ALU.mult, [R_cst], [R_tq])
                    S.op("dve", lambda e, i2=i2: e.reduce_sum(out=sml[:, i2:i2 + 1], in_=lamt[:, 0:64], axis=mybir.AxisListType.X),
                         reads=[R_tq], writes=[R_sml])
                ACT(sml[:, 2:4], sml[:, 0:2], AF.Exp, [R_sml], [R_sml])
                TT("dve", sml[:, 4:5], sml[:, 3:4], sml[:, 2:3], ALU.subtract, [R_sml], [R_sml])
                TS("dve", sml[:, 4:5], sml[:, 4:5], -lam_init, None, ALU.add, ALU.bypass, [R_sml], [R_sml])
                TS("dve", sml[:, 5:6], cst[:, CST_SUB:CST_SUB + 1], 1.0 - lam_init, None, ALU.mult, ALU.bypass, [R_cst], [R_sml])
                neglam = sml[:, 4:5]
                gsub = sml[:, 5:6]

                for which in range(2):
                    for s in range(2):
                        wv, wres = w_next(wain[which * 4 + s], (8, 512))
                        wv2, wres2 = w_next(wain[which * 4 + 2 + s], (8, 512), keep=1)
                        for hh in range(4):
                            hd = 4 * s + hh
                            cs = slice(hh * 128, (hh + 1) * 128)
                            pa, rpa = gen.get()
                            for k in range(8):
                                MM(pa[:, :], wv[:, k, cs], hT[:, k, tbs(0)], k == 0, k == 7, [wres, R_h[k][0]], rpa)
                            if which == 0:
                                CP("act", qTp[:, hd, :], pa[:, :], [rpa], [R_qp[hd]])
                            else:
                                CP("act", kTp[:, hd, :], pa[:, :], [rpa], [R_kp[hd]])
                                fo_, rfo, sfo = fst.get()
                                CP("dve", fo_, pa[:, :], [rpa], [rfo])
                                DMA("sp", nkT[:, hd, :], fo_, [rfo], [R_dram["nkT"]], sfo)
                            for tb in (1, 2):
                                pa, rpa = gen.get()
                                pb, rpb = gen.get()
                                for k in range(8):
                                    MM(pa[:, :], wv[:, k, cs], hT[:, k, tbs(tb)], k == 0, k == 7, [wres, R_h[k][tb]], rpa)
                                for k in range(8):
                                    MM(pb[:, :], wv2[:, k, cs], hT[:, k, tbs(tb)], k == 0, k == 7, [wres2, R_h[k][tb]], rpb)
                                a1, ra1 = r1.get()
                                a2, ra2 = r2.get()
                                lc = slice((tb - 1) * 512, tb * 512)
                                TT("dve", a1, pb[:, :], sinT[:, lc], ALU.mult, [rpb, R_rope], [ra1])
                                TT("dve", a2, pa[:, :], cosT[:, lc], ALU.mult, [rpa, R_rope], [ra2])
                                if which == 0:
                                    TT("pool", qTs[:, hd, lc], a1, a2, ALU.add, [ra1, ra2], [R_qs[hd][tb - 1]])
                                else:
                                    ks, rks, sks = kst.get()
                                    TT("pool", ks, a1, a2, ALU.add, [ra1, ra2], [rks])
                                    DMA("sp", cinK[hd * 128:(hd + 1) * 128, lc], ks, [rks], [R_dram["cinK"]], sks)
                S.op("pool", lambda e: e.collective_compute(
                    "AllGather", ALU.bypass, replica_groups=RG,
                    ins=[cinK.ap().opt()], outs=[coutK.ap().opt()]),
                    reads=[R_dram["cinK"]], writes=[R_dram["coutK"]], dsem=sem_cc, inc=1)
                for s in range(2):
                    wv, wres = w_next(wain[8 + s], (8, 512))
                    for t in range(12):
                        tb = t // 4
                        pa, rpa = gen.get()
                        for k in range(8):
                            MM(pa[:, :], hT[:, k, t * 128:(t + 1) * 128], wv[:, k, :], k == 0, k == 7, [wres, R_h[k][tb]], rpa)
                        if t < 4:
                            CP("act", Vp[:, t, s * 512:(s + 1) * 512], pa[:, :], [rpa], [R_vp[t][s]])
                            fo_, rfo, sfo = fst.get()
                            CP("dve", fo_, pa[:, :], [rpa], [rfo])
                            DMA("sp", nv[:, t, s * 512:(s + 1) * 512], fo_, [rfo], [R_dram["nv"]], sfo)
                        else:
                            vs, rvs, svs = vst.get()
                            CP("act", vs, pa[:, :], [rpa], [rvs])
                            DMA("sp", cinV[(t - 4) * 128:(t - 3) * 128, s * 512:(s + 1) * 512], vs, [rvs],
                                [R_dram["cinV"]], svs)
                S.op("pool", lambda e: e.collective_compute(
                    "AllGather", ALU.bypass, replica_groups=RG,
                    ins=[cinV.ap().opt()], outs=[coutV.ap().opt()]),
                    reads=[R_dram["cinV"]], writes=[R_dram["coutV"]], dsem=sem_cc, inc=1)

                ar.pop(mark)
                kbuf_t = ar.alloc([2, 2, 2048], BF16, "kbuf")
                vbuf_t = ar.alloc([2, 2048], BF16, "vbuf")
                kvb = Ring([(kbuf_t[:, i, :, :], vbuf_t[:, i, :], ar.res(), ar.res(), sem_kv[2 * i], sem_kv[2 * i + 1]) for i in range(2)])
                S.op("dve", lambda e: e.memset(kbuf_t[:, :, :, :], 0.0), writes=[it_[2] for it_ in kvb.items])
                eT_t = ar.alloc([3, 2, 512], BF16, "eT")
                eTr = Ring([(eT_t[:, i, :, :], ar.res()) for i in range(3)])
                rz_t = ar.alloc([512], F32, "rz"); R_rz = ar.res()
                tq_t = ar.alloc([512], F32, "tq"); R_tq = ar.res()
                osb = ar.alloc([512], F32, "osb"); R_osb = ar.res()

                def attn_norm(n, hd, col0):
                    tb = col0 // 512
                    sq, rsq = b.sq.get()
                    ACT(sq[:, 0:n], osb[:, 0:n], AF.Square, [R_osb], [rsq])
                    pn, rpn = gen.get()
                    MM(pn[:, 0:n], ones, sq[:, 0:n], True, True, [rsq, R_cb], rpn)
                    ACT(b.rt[:, 0:n], pn[:, 0:n], AF.Ln, [rpn], [b.R_rt], scale=1.0 / 128.0, bias=EPS)
                    ACT(rz_t[:, 0:n], b.rt[:, 0:n], AF.Exp, [b.R_rt], [R_rz], scale=-0.5)
                    STT(hT[:, hd, col0:col0 + n], osb[:, 0:n], gsub, rz_t[:, 0:n], ALU.mult, ALU.mult,
                        [R_osb, R_sml, R_rz], [R_h[hd][tb]])

                def pstageA(seq, hd):
                    qc = slice(seq * 256, (seq + 1) * 256)
                    e2, re_ = eTr.get()
                    for kc in range(2):
                        kt = 2 * seq + kc
                        for c in range(2):
                            ps_ = slice(c * 64, (c + 1) * 64)
                            pa, rpa = gen.get()
                            MM(pa[:, 0:256], kTp[ps_, hd, kt * 128:(kt + 1) * 128], qTp[ps_, hd, qc],
                               True, True, [R_kp[hd], R_qp[hd]], rpa)
                            ACT(e2[:, kc, c * 256:(c + 1) * 256], pa[:, 0:256], AF.Exp, [rpa], [re_], scale=0.125)
                    return (seq, hd, e2, re_)

                def pstageB(it):
                    seq, hd, e2, re_ = it
                    po, rpo = psb[4], R_ps[4]
                    pz, rpz = psb[5], R_ps[5]
                    for kc in range(2):
                        kt = 2 * seq + kc
                        MM(po[:, :], Vp[:, kt, hd * 128:(hd + 1) * 128], e2[:, kc, :], kc == 0, kc == 1,
                           [re_, R_vp[kt][hd // 4]], rpo)
                    for kc in range(2):
                        MM(pz[:, :], ones, e2[:, kc, :], kc == 0, kc == 1, [re_, R_cb], rpz)
                    ACT(b.rt[:, :], pz[:, :], AF.Ln, [rpz], [b.R_rt])
                    ACT(rz_t[:, :], b.rt[:, :], AF.Exp, [b.R_rt], [R_rz], scale=-1.0)
                    TT("dve", tq_t[:, :], po[:, :], rz_t[:, :], ALU.mult, [rpo, R_rz], [R_tq])
                    STT(osb[:, 0:256], tq_t[:, 256:512], neglam, tq_t[:, 0:256], ALU.mult, ALU.add,
                        [R_tq, R_sml], [R_osb])
                    attn_norm(256, hd, seq * 256)

                prevb = None
                for seq in range(2):
                    for hd in range(8):
                        curb = pstageA(seq, hd)
                        if prevb is not None:
                            pstageB(prevb)
                        prevb = curb
                pstageB(prevb)

                coutKv = coutK.ap().rearrange("(r x p) n -> p r x n", r=2, p=128)
                coutVv = coutV.ap().rearrange("(r x p) n -> p r x n", r=2, p=128)
                kvs = {}

                def load_kv(hd):
                    kb, vb, rkb, rvb, skk, skv = kvb.get()
                    for c_ in range(2):
                        DMA("sp", kb[c_ * 64:(c_ + 1) * 64, c_, :].rearrange("p (r n) -> p r n", r=2),
                            coutKv[c_ * 64:(c_ + 1) * 64, :, hd, :], [R_dram["coutK"]], [rkb], skk)
                    for r_ in range(2):
                        DMA("sp", vb[:, r_ * 1024:(r_ + 1) * 1024].rearrange("p (t v) -> p t v", t=8),
                            coutVv[:, r_, :, hd * 128:(hd + 1) * 128], [R_dram["coutV"]], [rvb], skv)
                    kvs[hd] = (kb, vb, rkb, rvb)

                pairs = Ring([0, 1, 2])
                po, rpo, pz, rpz = psb[6], R_ps[6], psb[7], R_ps[7]

                def stage1(hd, qb, c, kp):
                    kb, vb, rkb, rvb = kvs[hd]
                    qcs = slice(qb * 512, (qb + 1) * 512)
                    pi = pairs.get()
                    vls = []
                    for u in range(2):
                        kc = 2 * kp + u
                        if kc < 16:
                            kl = kb[:, c, kc * 128:(kc + 1) * 128]
                            vl = vb[:, kc * 128:(kc + 1) * 128]
                            rk_, rv_ = rkb, rvb
                        else:
                            kl = ckT[:, c, hd, (kc - 16) * 128:(kc - 15) * 128]
                            vl = cvb[:, kc - 16, hd * 128:(hd + 1) * 128]
                            rk_, rv_ = R_ck, R_cv
                        MM(psb[2 * pi + u][:, :], kl, qTs[:, hd, qcs], True, True, [rk_, R_qs[hd][qb]], R_ps[2 * pi + u])
                        vls.append((vl, rv_))
                    e2, re_ = eTr.get()
                    ACT(e2, psall[:, 2 * pi * 512:(2 * pi + 2) * 512].rearrange("p (a b) -> p a b", a=2), AF.Exp,
                        [R_ps[2 * pi], R_ps[2 * pi + 1]], [re_], scale=0.125)
                    return (hd, qb, c, kp, e2, re_, vls)

                def stage2(it):
                    hd, qb, c, kp, e2, re_, vls = it
                    for u in range(2):
                        kc = 2 * kp + u
                        vl, rv_ = vls[u]
                        MM(po[:, :], vl, e2[:, u, :], kc == 0, kc == 17, [re_, rv_], rpo)
                        MM(pz[:, :], ones, e2[:, u, :], kc == 0, kc == 17, [re_, R_cb], rpz)
                    if kp == 8:
                        ACT(b.rt[:, :], pz[:, :], AF.Ln, [rpz], [b.R_rt])
                        ACT(rz_t[:, :], b.rt[:, :], AF.Exp, [b.R_rt], [R_rz], scale=-1.0)
                        if c == 0:
                            TT("dve", tq_t[:, :], po[:, :], rz_t[:, :], ALU.mult, [rpo, R_rz], [R_tq])
                        else:
                            TT("dve", osb[:, :], po[:, :], rz_t[:, :], ALU.mult, [rpo, R_rz], [R_osb])
                            STT(osb[:, :], osb[:, :], neglam, tq_t[:, :], ALU.mult, ALU.add, [R_tq, R_sml, R_osb], [R_osb])
                            attn_norm(512, hd, 512 + qb * 512)

                pend = []
                load_kv(0)
                for hd in range(8):
                    for qb in range(2):
                        for c in range(2):
                            for kp in range(9):
                                if hd + 1 < 8 and qb == 0 and c == 0 and kp == 4:
                                    load_kv(hd + 1)
                                pend.append(stage1(hd, qb, c, kp))
                                if len(pend) > 2:
                                    stage2(pend.pop(0))
                while pend:
                    stage2(pend.pop(0))

                yb = ar.view(0, [8, T], BF16)
                olds = R_qp + R_kp + [r_ for rr in R_vp for r_ in rr]
                R_yb = [[ar.res_from(olds) for _ in range(NTB)] for _ in range(8)]
                for s in range(2):
                    wv, wres = w_next(waout[s], (8, 512))
                    for ff in range(4):
                        fo = 4 * s + ff
                        for tb in range(NTB):
                            pa, rpa = gen.get()
                            for hd in range(8):
                                MM(pa[:, :], wv[:, hd, ff * 128:(ff + 1) * 128], hT[:, hd, tbs(tb)], hd == 0, hd == 7,
                                   [wres, R_h[hd][tb]], rpa)
                            post_chunk(b, pa, rpa, fo, tb, yb, R_yb)
                post_finish(b, l, j, yb, R_yb, nxt)

            def hgrn(nxt=None):
                l, j = 1, 1
                ar.reset()
                b = norm_bufs()
                on = ar.alloc([8, T], BF16, "on"); R_on = ar.resgrid(8, NTB)
                itok = ar.alloc([12, 128], BF16, "itok"); R_it = ar.res()
                T1 = ar.alloc([T], F32, "T1"); R_T1 = ar.res()
                T2 = ar.alloc([T], F32, "T2"); R_T2 = ar.res()
                PB = ar.alloc([T + 1], F32, "PB"); R_PB = ar.res()
                qb_ = ar.alloc([T], BF16, "qb"); R_qb = ar.res()
                sg = ar.alloc([T], BF16, "sg"); R_sg = ar.res()
                kx = ar.alloc([T], BF16, "kx"); R_kx = ar.res()
                qt = ar.alloc([1, T], BF16, "qt"); R_qt = [ar.res()] * 2
                kt = ar.alloc([1, T], BF16, "kt"); R_kt = [ar.res()] * 2
                AT = ar.alloc([12, 128], BF16, "AT"); R_AT = [ar.res()] * 2
                ktok_off = ar.off
                ktok = ar.alloc([2, 12, 128], BF16, "ktok"); R_ktok = [ar.res()] * 2
                sh_t = ar.alloc([2, 128], BF16, "shat")
                shr = Ring([(sh_t[:, i, :], ar.res()) for i in range(2)])
                Sst_t = ar.alloc([4, 128], F32, "Sst")
                sring = Ring([(Sst_t[:, i, :], ar.res()) for i in range(4)])
                td_t = ar.alloc([2, 4, 128], F32, "td")
                tdr = Ring([(td_t[:, i, :, :], ar.res()) for i in range(2)])
                csc = ar.alloc([2, 3, 24], F32, "csc"); R_csc = ar.resgrid(2)
                cdf = ar.alloc([3, 24], F32, "cdf"); R_cdf = ar.res()
                lbv = ar.alloc([2, 2, 8], F32, "lbv"); R_lb = ar.res()
                nol = ar.alloc([2, 8], F32, "nol")
                S0 = ar.alloc([8, 128], F32, "S0"); R_S0 = ar.res()
                Srv = ar.view(ktok_off, [8, 128], F32)
                R_Srv = R_ktok[0]
                SinB = ar.alloc([8, 128], F32, "SinB"); R_SinB = ar.res()
                Sout = ar.alloc([1, 128], F32, "Sout")
                sor = Ring([(Sout[:, i, :], ar.res(), sem_misc[i]) for i in range(1)])
                (osb, R_osb), (rz_t, R_rz) = b.tmp.items[0], b.tmp.items[1]

                DMA("sp", S0[:, :, :], s0_in[:, :, :], [], [R_S0], sem_in[5])
                S.op("dve", lambda e: e.memset(PB[:, 0:1], 0.0), writes=[R_PB])
                for X in range(2):
                    l0 = cst[:, CST_LB + X * 8:CST_LB + X * 8 + 8]
                    l1 = cst[:, CST_LB + 16 + X * 8:CST_LB + 16 + X * 8 + 8]
                    TT("dve", lbv[:, X, 1, :], l1, l0, ALU.subtract, [R_cst], [R_lb])
                    ACT(lbv[:, X, 0, :], lbv[:, X, 1, :], AF.Sigmoid, [R_lb], [R_lb])
                    TS("dve", lbv[:, X, 1, :], lbv[:, X, 0, :], -1.0, 1.0, ALU.mult, ALU.add, [R_lb], [R_lb])
                    TS("dve", nol[:, X, :], lbv[:, X, 1, :], -1.0, None, ALU.mult, ALU.bypass, [R_lb], [R_lb])
                gn = cst[:, CST_GN:CST_GN + 1]

                SEGS = [(0, 256, 0), (256, 512, 1), (512, 1536, 2)]

                class P:
                    pass

                def fp1(p):
                    hd, X, full = p.hd, p.X, p.full
                    p.wv, p.wres = w_next(whin[hd], (8, 512))
                    p.wi, p.wires = w_next(whi[hd], (8, 128), keep=1)
                    wv, wres = p.wv, p.wres
                    tb_list = range(NTB) if full else (1, 2)
                    t_lo = 0 if full else 512
                    tl = slice(t_lo, T)
                    p.t_lo, p.tl = t_lo, tl
                    if full:
                        for tb in tb_list:
                            pa, rpa = gen.get()
                            for k in range(8):
                                MM(pa[:, :], wv[:, k, 0:128], hT[:, k, tbs(tb)], k == 0, k == 7, [wres, R_h[k][tb]], rpa)
                            CP("dve", qb_[:, tbs(tb)], pa[:, :], [rpa], [R_qb])
                    yield
                    lb_ = lbv[:, X, 0, hd:hd + 1]
                    oml_ = lbv[:, X, 1, hd:hd + 1]
                    nol_ = nol[:, X, hd:hd + 1]
                    for tb in tb_list:
                        pa, rpa = gen.get()
                        for k in range(8):
                            MM(pa[:, :], wv[:, k, 128 * (1 + X):128 * (2 + X)], hT[:, k, tbs(tb)], k == 0, k == 7,
                               [wres, R_h[k][tb]], rpa)
                        ACT(T1[:, tbs(tb)], pa[:, :], AF.Sigmoid, [rpa], [R_T1])
                    yield
                    ACT(T2[:, tl], T1[:, tl], AF.Ln, [R_T1, R_lb], [R_T2], scale=oml_, bias=lb_)
                    TS("dve", kx[:, tl], T1[:, tl], nol_, oml_, ALU.mult, ALU.add, [R_T1, R_lb], [R_kx])
                    yield
                    S.op("dve", lambda e, t_lo=t_lo: e.memset(PB[:, t_lo:t_lo + 1], 0.0), reads=[R_cdf], writes=[R_PB])
                    S.op("dve", lambda e, tl=tl: e.tensor_tensor_scan(
                        out=PB[:, 1 + tl.start:1 + tl.stop], data0=onesrow[:, tl], data1=T2[:, tl], initial=0.0,
                        op0=ALU.mult, op1=ALU.add), reads=[R_T2, R_cb], writes=[R_PB])
                    yield
                    c_lo = t_lo // 64
                    nch = (T - t_lo) // 64
                    sh_ = 1 if X == 0 else 0
                    Pv = PB[:, sh_ + t_lo:sh_ + T].rearrange("p (c t) -> p c t", t=64)
                    ref = PB[:, sh_ + t_lo + 32:sh_ + t_lo + 32 + 64 * (nch - 1) + 1:64].unsqueeze(2).broadcast_to([128, nch, 64])
                    TT("dve", T2[:, tl].rearrange("p (c t) -> p c t", t=64), Pv, ref, ALU.subtract, [R_PB, R_T2], [R_T2])
                    yield

                    def pcol(off):
                        return PB[:, t_lo + off:t_lo + off + 64 * (nch - 1) + 1:64]
                    if X == 0:
                        pairs = [(pcol(64), pcol(0)), (pcol(64), pcol(33)), (pcol(33), pcol(0))]
                    else:
                        pairs = [(pcol(64), pcol(0)), (pcol(32), pcol(0)), (pcol(64), pcol(32))]
                    for i2, (pa_, pb_) in enumerate(pairs):
                        TT("dve", cdf[:, i2, c_lo:c_lo + nch], pa_, pb_, ALU.subtract, [R_PB], [R_cdf])
                    ACT(csc[:, p.slot, :, c_lo:c_lo + nch], cdf[:, :, c_lo:c_lo + nch], AF.Exp, [R_cdf], [R_csc[p.slot]])
                    yield
                    ACT(T1[:, tl], T2[:, tl], AF.Exp, [R_T2], [R_T1])
                    yield
                    ACT(T2[:, tl], T2[:, tl], AF.Exp, [R_T2], [R_T2], scale=-1.0)

                def fp2(p):
                    hd, X, full, tl = p.hd, p.X, p.full, p.tl
                    if X == 1:
                        for tb in range(NTB):
                            pa, rpa = gen.get()
                            for k in range(8):
                                MM(pa[:, :], p.wv[:, k, 384:512], hT[:, k, tbs(tb)], k == 0, k == 7, [p.wres, R_h[k][tb]], rpa)
                            ACT(sg[:, tbs(tb)], pa[:, :], AF.Sigmoid, [rpa], [R_sg])
                    if True:
                        wi, wires = p.wi, p.wires
                        for g in range(3):
                            if not full and g == 0:
                                continue
                            pa, rpa = gen.get()
                            for tt in range(4):
                                t = 4 * g + tt
                                for k in range(8):
                                    MM(pa[:, tt * 128:(tt + 1) * 128], hT[:, k, t * 128:(t + 1) * 128], wi[:, k, :], k == 0, k == 7,
                                       [wires, R_h[k][g]], rpa)
                            CP("act", itok[:, 4 * g:4 * g + 4, :], pa[:, :].rearrange("p (a b) -> p a b", a=4), [rpa], [R_it])
                    Eq, Ek = (T1, T2) if X == 0 else (T2, T1)
                    if full:
                        TT("dve", qt[:, 0, tl], qb_[:, tl], Eq[:, tl], ALU.mult, [R_qb, R_T1, R_T2], [R_qt[X]])
                    TT("dve", kt[:, 0, tl], kx[:, tl], Ek[:, tl], ALU.mult, [R_kx, R_T1, R_T2], [R_kt[X]])
                    mask = maskA if X == 0 else maskB
                    for g in range(3):
                        if not full and g == 0:
                            continue
                        if full:
                            pa, rpa = gen.get()
                            for tt in range(4):
                                t = 4 * g + tt
                                tok = slice(t * 128, (t + 1) * 128)
                                MM(pa[:, tt * 128:(tt + 1) * 128], kt[:, 0, tok], qt[:, 0, tok], True, True,
                                   [R_kt[X], R_qt[X]], rpa)
                            TT("dve", AT[:, 4 * g:4 * g + 4, :], pa[:, :].rearrange("p (a b) -> p a b", a=4),
                               mask.unsqueeze(1).broadcast_to([128, 4, 128]), ALU.mult, [rpa, R_cb], [R_AT[X]])
                        pa, rpa = gen.get()
                        pab = pa[:, 0:256].bitcast(BF16)
                        for tt in range(4):
                            t = 4 * g + tt
                            TR(pab[:, tt * 128:(tt + 1) * 128], kt[:, 0, t * 128:(t + 1) * 128], [R_kt[X], R_cb], rpa)
                        ACT(ktok[:, 0, 4 * g:4 * g + 4, :], pab.rearrange("p (a b) -> p a b", a=4), AF.Identity,
                            [rpa, R_cst], [R_ktok[X]], scale=cst[:, CST_RM:CST_RM + 1])
                        TS("dve", ktok[:, 1, 4 * g:4 * g + 4, :], pab.rearrange("p (a b) -> p a b", a=4),
                           cst[:, CST_RM + 1:CST_RM + 2], None, ALU.mult, ALU.bypass, [rpa, R_cst], [R_ktok[X]])

                o_started = [False] * NTB

                def back(p, filler=None):
                    hd, X, full = p.hd, p.X, p.full
                    if full:
                        for g in range(3):
                            o_started[g] = False
                        for g in range(3):
                            ob, rob = psb[4 + g], R_ps[4 + g]
                            for tt in range(4):
                                t = 4 * g + tt
                                MM(ob[:, tt * 128:(tt + 1) * 128], itok[:, t, :], AT[:, t, :], not o_started[g], False,
                                   [R_it, R_AT[X]], rob, skip=True)
                                o_started[g] = True
                    for (t0, t1, kind) in SEGS:
                        if not full and kind != 2:
                            continue
                        chunks = list(range(t0 // 64, t1 // 64))
                        if X == 1:
                            chunks = chunks[::-1]
                        if kind == 2:
                            Sst, R_S = (S0[:, hd, :], R_S0) if X == 0 else (SinB[:, hd, :], R_SinB)
                        else:
                            Sst, R_S = sring.get()
                            S.op("dve", lambda e, Sst=Sst: e.memset(Sst, 0.0), writes=[R_S])
                        groups = [chunks[gi:gi + 4] for gi in range(0, len(chunks), 4)]

                        def pre_mm(grp):
                            pd, rpd = gen.get()
                            asc = sorted(grp)
                            for c in grp:
                                t, hf = c // 2, c % 2
                                i2 = asc.index(c)
                                MM(pd[:, i2 * 128:(i2 + 1) * 128], ktok[:, hf, t, :], itok[:, t, :], True, True,
                                   [R_ktok[X], R_it], rpd)
                            return (pd, rpd, asc)

                        def pre_td(pm, grp):
                            pd, rpd, asc = pm
                            n_ = len(asc)
                            td4, rtd = tdr.get()
                            TT("dve", td4[:, 0:n_, :], pd[:, 0:n_ * 128].rearrange("p (a b) -> p a b", a=n_),
                               csc[:, p.slot, 1, asc[0]:asc[0] + n_].unsqueeze(2).broadcast_to([128, n_, 128]), ALU.mult,
                               [rpd, R_csc[p.slot]], [rtd])
                            return [(td4[:, asc.index(c), :], rtd) for c in grp]

                        nxt_tds = pre_td(pre_mm(groups[0]), groups[0])
                        for gidx, grp in enumerate(groups):
                            tds = nxt_tds
                            pm_n = None
                            if gidx + 1 < len(groups):
                                pm_n = pre_mm(groups[gidx + 1])
                            for i2, c in enumerate(grp):
                                tb = c // 8
                                if full:
                                    sh2, rsh2 = shr.get()
                                    ACT(sh2, Sst, AF.Identity, [R_S, R_csc[p.slot]], [rsh2], scale=csc[:, p.slot, 2, c:c + 1])
                                    ob, rob = psb[4 + tb], R_ps[4 + tb]
                                    oc = slice((c % 8) * 64, (c % 8) * 64 + 64)
                                    MM(ob[:, oc], sh2, qt[:, 0, c * 64:(c + 1) * 64], False, False, [rsh2, R_qt[X]], rob, skip=True)
                                td, rtd = tds[i2]
                                Snx, R_Snx = sring.get()
                                STT(Snx, Sst, csc[:, p.slot, 0, c:c + 1], td, ALU.mult, ALU.add,
                                    [R_S, R_csc[p.slot], rtd], [R_Snx])
                                Sst, R_S = Snx, R_Snx
                            if pm_n is not None:
                                nxt_tds = pre_td(pm_n, groups[gidx + 1])
                            if filler is not None:
                                next(filler, None)
                        if kind == 2 and X == 0:
                            so, rso, sso = sor.get()
                            CP("dve", so, Sst, [R_S], [rso])
                            DMA("sp", cin2[hd * 128:(hd + 1) * 128, :], so, [rso], [R_dram["cin2"]], sso)
                        if kind != 2 and full:
                            so, rso, sso = sor.get()
                            CP("dve", so, Sst, [R_S], [rso])
                            DMA("sp", nS[:, kind, X, hd, :], so, [rso], [R_dram["nS"]], sso)
                    if filler is not None:
                        for _ in filler:
                            pass
                    if full and X == 0:
                        for tb in range(NTB):
                            CP("dve", on[:, hd, tbs(tb)], psb[4 + tb][:, :], [R_ps[4 + tb]], [R_on[hd][tb]])
                    if full and X == 1:
                        for tb in range(NTB):
                            ob, rob = psb[4 + tb], R_ps[4 + tb]
                            ont = on[:, hd, tbs(tb)]
                            TT("dve", ont, ob[:, :], ont, ALU.add, [rob, R_on[hd][tb]], [R_on[hd][tb]])
                            sq, rsq = b.sq.get()
                            ACT(sq, ont, AF.Square, [R_on[hd][tb]], [rsq])
                            pn, rpn = gen.get()
                            MM(pn[:, :], ones, sq, True, True, [rsq, R_cb], rpn)
                            ACT(b.rstd[:, tb, :], pn[:, :], AF.Ln, [rpn], [b.R_rstd[tb]], scale=1.0 / 128.0, bias=EPS)
                            ACT(b.rstd[:, tb, :], b.rstd[:, tb, :], AF.Exp, [b.R_rstd[tb]], [b.R_rstd[tb]], scale=-0.5)
                        for tb in range(NTB):
                            ont = on[:, hd, tbs(tb)]
                            STT(ont, ont, gn, b.rstd[:, tb, :], ALU.mult, ALU.mult, [R_on[hd][tb], R_cst, b.R_rstd[tb]], [R_on[hd][tb]])
                            TT("dve", ont, ont, sg[:, tbs(tb)], ALU.mult, [R_on[hd][tb], R_sg], [R_on[hd][tb]])
                    if X == 0 and hd == 7:
                        S.op("pool", lambda e: e.collective_compute(
                            "AllGather", ALU.bypass, replica_groups=RG,
                            ins=[cin2.ap().opt()], outs=[cout2.ap().opt()]),
                            reads=[R_dram["cin2"]], writes=[R_dram["cout2"]], dsem=sem_cc, inc=1)
                        c2v = cout2.ap().rearrange("(r h p) n -> p r h n", r=2, p=128)
                        DMA("sp", SinB[:, :, :], c2v[:, 0, :, :], [R_dram["cout2"]], [R_SinB], sem_in[6])
                        DMA("sp", Srv[:, :, :], c2v[:, 1, :, :], [R_dram["cout2"]], [R_Srv], sem_in[7])
                        TS("dve", SinB[:, :, :], SinB[:, :, :], cst[:, CST_M:CST_M + 1], None, ALU.mult, ALU.bypass, [R_cst, R_SinB], [R_SinB])
                        STT(SinB[:, :, :], Srv[:, :, :], cst[:, CST_M + 1:CST_M + 2], SinB[:, :, :], ALU.mult, ALU.add,
                            [R_Srv, R_cst, R_SinB], [R_SinB])

                passes = []
                for hd in range(8):
                    passes.append((hd, 0, True))
                for hd in range(8):
                    passes.append((hd, 1, True))
                plist = []
                prev = None
                for n_, (hd, X, full) in enumerate(passes):
                    p = P()
                    p.hd, p.X, p.full, p.slot = hd, X, full, n_ % 2
                    plist.append(p)
                    prev = p
                for n_, p in enumerate(plist):
                    if n_ == 0:
                        for _ in fp1(p):
                            pass
                        fp2(p)
                    fil = None
                    if n_ + 1 < len(plist):
                        fil = fp1(plist[n_ + 1])
                        next(fil, None)
                    back(p, fil)
                    if n_ + 1 < len(plist):
                        fp2(plist[n_ + 1])
                for s in range(2):
                    wv, wres = w_next(whout[s], (8, 512))
                    for ff in range(4):
                        fo = 4 * s + ff
                        for tb in range(NTB):
                            pa, rpa = gen.get()
                            for hd in range(8):
                                MM(pa[:, :], wv[:, hd, ff * 128:(ff + 1) * 128], on[:, hd, tbs(tb)], hd == 0, hd == 7,
                                   [wres, R_on[hd][tb]], rpa)
                            post_chunk(b, pa, rpa, fo, tb, hT, R_h)
                post_finish(b, l, j, hT, R_h, nxt)

            stages = ["none", "mod", "ffn00", "attn", "ffn01", "ffn10", "hgrn", "full"]
            lim = stages.index(stop) - 2
            def mk(l, s):
                return lambda: mod_slab(l, s)

            def fin(l, js):
                def f():
                    for j_ in js:
                        mod_fin(l, j_)
                return f
            def mk2(l, s):
                def f():
                    mod_slab(l, s)
                    mod_slab(l, s + 1)
                return f
            if lim >= -1:
                for s_ in range(4):
                    mod_slab(0, s_)
                mod_fin(0, 0, "pre")
            if lim >= 0:
                prenorm(NBP, 0, 0)
                ffn(0, 0, 0, nxt=(0, 1) if lim >= 1 else None,
                    hooks=[mk2(0, 4), lambda: mod_fin(0, 0, "post")] + [mk2(0, s_) for s_ in range(6, 18, 2)] + [fin(0, [1, 2])])
            if lim >= 1:
                attention(nxt=(0, 2) if lim >= 2 else None)
            if lim >= 2:
                ffn(0, 1, 2, nxt=(1, 0) if lim >= 3 else None,
                    hooks=[mk2(1, s_) for s_ in range(0, 18, 2)] + [fin(1, [0, 1, 2])])
            if lim >= 3:
                ffn(1, 0, 0, nxt=(1, 1) if lim >= 4 else None)
            if lim >= 4:
                hgrn(nxt=(1, 2) if lim >= 5 else None)
            if lim >= 5:
                ffn(1, 1, 2)
            for tb in range(NTB):
                DMA("sp", yT[:, :, tbs(tb)], xT[:, :, tbs(tb)], [R_x[fo][tb] for fo in range(8)], [R_dram["yT"]], outsem.get())

        S.dry = True
        body()
        S.dry = False
        ar.off = 0
        ar.cur = []
        ar.prev = []
        body()
        with nc.Block() as block:
            S.emit(block, final_waits=sem_out + sem_misc)
    return nc


def _fm(v):
    v = np.asarray(v)
    n = v.shape[-1] // 128
    w = v.reshape(v.shape[:-1] + (n, 128))
    return np.moveaxis(w, -1, 0)


def _slab(wmat, cols):
    sub = wmat[:, cols]
    return np.ascontiguousarray(sub.reshape(8, 128, -1).transpose(1, 0, 2))


_NC_CACHE = {}


def _get_nc(stop):
    if stop not in _NC_CACHE:
        _NC_CACHE[stop] = build_program(stop)
    return _NC_CACHE[stop]


def kernel(x_prompt, x_sample, cache_k, cache_v, state_hgrn, c, c_ctx, w_mod, b_mod, norm_g,
           ffn_w_in, ffn_w_out, attn_w_in, attn_w_out, attn_lambda, attn_subln,
           hgrn_w_in, hgrn_w_out, hgrn_lower_bounds, hgrn_gnorm, _stop="full"):
    f32 = np.float32
    A = lambda a: np.asarray(a, dtype=f32)
    x_prompt, x_sample, cache_k, cache_v, state_hgrn = A(x_prompt), A(x_sample), A(cache_k), A(cache_v), A(state_hgrn)
    c, c_ctx, w_mod, b_mod, norm_g = A(c), A(c_ctx), A(w_mod), A(b_mod), A(norm_g)
    ffn_w_in, ffn_w_out, attn_w_in, attn_w_out = A(ffn_w_in), A(ffn_w_out), A(attn_w_in), A(attn_w_out)
    attn_lambda, attn_subln, hgrn_w_in, hgrn_w_out = A(attn_lambda), A(attn_subln), A(hgrn_w_in), A(hgrn_w_out)
    hgrn_lower_bounds, hgrn_gnorm = A(hgrn_lower_bounds), A(hgrn_gnorm)

    wmod_h = np.empty((2, 18, 128, 8, 512), f32)
    for l in range(2):
        for s in range(18):
            wmod_h[l, s] = _slab(w_mod[l], np.arange(s * 512, (s + 1) * 512))
    wfin_h = np.empty((2, 2, 11, 128, 8, 512), f32)
    wfout_h = np.empty((2, 2, 8, 128, 22, 128), f32)
    for l in range(2):
        for i in range(2):
            for s in range(11):
                cols = np.concatenate([np.arange(2 * s * 128, (2 * s + 2) * 128), FH + np.arange(2 * s * 128, (2 * s + 2) * 128)])
                wfin_h[l, i, s] = _slab(ffn_w_in[l, i], cols)
            wo = ffn_w_out[l, i].reshape(22, 128, 8, 128)
            wfout_h[l, i] = wo.transpose(2, 1, 0, 3)
    idx = np.arange(1024).reshape(8, 2, 2, 2, 16)
    swp = idx[:, :, :, ::-1, :].reshape(-1)
    wain_h = np.empty((10, 128, 8, 512), f32)
    for which in range(2):
        for s in range(2):
            cols = which * 1024 + np.arange(s * 512, (s + 1) * 512)
            wain_h[which * 4 + s] = _slab(attn_w_in[0], cols)
            cols2 = which * 1024 + swp[s * 512:(s + 1) * 512]
            wain_h[which * 4 + 2 + s] = _slab(attn_w_in[0], cols2)
    for s in range(2):
        wain_h[8 + s] = _slab(attn_w_in[0], 2048 + np.arange(s * 512, (s + 1) * 512))
    waout_h = np.stack([_slab(attn_w_out[0], np.arange(s * 512, (s + 1) * 512)) for s in range(2)])
    whout_h = np.stack([_slab(hgrn_w_out[0], np.arange(s * 512, (s + 1) * 512)) for s in range(2)])
    whi_h = np.stack([_slab(hgrn_w_in[0], 3072 + np.arange(hd * 128, (hd + 1) * 128)) for hd in range(8)])
    whin_par = []
    for par in range(2):
        arr = np.empty((8, 128, 8, 512), f32)
        for hd in range(8):
            r = np.arange(hd * 128, (hd + 1) * 128)
            gA, gB = (1024, 2048) if par == 0 else (2048, 1024)
            arr[hd] = _slab(hgrn_w_in[0], np.concatenate([r, gA + r, gB + r, 4096 + r]))
        whin_par.append(arr)
    cbh = np.zeros((128, CB_N), f32)
    cbh[:, CB_ID:CB_ID + 128] = np.eye(128, dtype=f32)
    cbh[:, CB_ONES:CB_ONES + 128] = 1.0
    pp = np.arange(128)[:, None]
    tt = np.arange(128)[None, :]
    same = (pp // 64) == (tt // 64)
    cbh[:, CB_MA:CB_MA + 128] = same & ((pp % 64) <= (tt % 64))
    cbh[:, CB_MB:CB_MB + 128] = same & ((pp % 64) >= (tt % 64))
    nf = 16
    inv = (np.float32(10000.0) ** (-np.arange(nf, dtype=f32) / np.float32(nf))).astype(f32)

    in_maps = []
    for core in range(8):
        pr, par = core // 2, core % 2
        segs = [x_prompt[2 * core], x_prompt[2 * core + 1], x_sample[pr, par * 1024:(par + 1) * 1024]]
        if par:
            segs = [s_[::-1] for s_ in segs]
        X = np.concatenate(segs, 0)
        xT_h = np.ascontiguousarray(X.T.reshape(8, 128, T).transpose(1, 0, 2))
        cst_h = np.zeros((128, CST_N), f32)
        cst_h[:, CST_LAM:CST_LAM + 256] = attn_lambda[0].reshape(1, 256)
        lbm = hgrn_lower_bounds[:, ::-1, :] if par else hgrn_lower_bounds
        cst_h[:, CST_LB:CST_LB + 32] = _fm(lbm).reshape(128, 32)
        cst_h[:, CST_BMOD:CST_BMOD + 144] = _fm(b_mod).reshape(128, 144)
        cst_h[:, CST_NG:CST_NG + 96] = _fm(norm_g).reshape(128, 96)
        cst_h[:, CST_GN] = hgrn_gnorm[0]
        cst_h[:, CST_SUB] = attn_subln[0]
        cst_h[:, CST_C:CST_C + 16] = np.stack([_fm(c_ctx), _fm(c[pr])], -1).reshape(128, 16)
        cst_h[:, CST_M] = 1.0 if par else 0.0
        cst_h[:, CST_M + 1] = 0.0 if par else 1.0
        cst_h[:64, CST_RM] = 1.0
        cst_h[64:, CST_RM + 1] = 1.0
        pos = par * 1024 + np.arange(1024)
        if par:
            pos = pos[::-1]
        row = (pos // 64).astype(f32)
        col = (pos % 64).astype(f32)
        p = np.arange(128) % 64
        axis, half, fr = p // 32, (p % 32) // 16, p % 16
        ang = np.where(axis[:, None] == 0, row[None, :], col[None, :]).astype(f32) * inv[fr][:, None]
        rope_h = np.stack([np.cos(ang), np.where(half[:, None] == 0, -np.sin(ang), np.sin(ang))], 1).astype(f32)
        ck = cache_k[pr, 0]
        ck_h = np.ascontiguousarray(ck.transpose(2, 3, 1, 0).reshape(128, 8, 256))
        cv_h = np.ascontiguousarray(cache_v[pr, 0].reshape(2, 128, 1024).transpose(1, 0, 2))
        s0_h = np.ascontiguousarray(state_hgrn[pr, 0, par].transpose(1, 0, 2))
        in_maps.append(dict(xT_in=xT_h, cst=cst_h, cb=cbh, rope=np.ascontiguousarray(rope_h), ckT=ck_h, cv=cv_h, s0=s0_h,
                            wmod=wmod_h, wfin=wfin_h, wfout=wfout_h, wain=wain_h, waout=waout_h,
                            whin=whin_par[par], whi=whi_h, whout=whout_h))
    nc = _get_nc(_stop)
    ncores = int(os.environ.get("K_CORES", "8"))
    res = run_bass_kernel_spmd(nc, in_maps[:ncores], core_ids=list(range(ncores)))
    return _assemble(res.results, ncores)


def _assemble(results, ncores=8):
    f32 = np.float32

    y_prompt = np.zeros((16, 256, 1024), f32)
    y_sample = np.zeros((4, 2048, 1024), f32)
    nck = np.zeros((16, 1, 256, 8, 2, 64), f32)
    ncv = np.zeros((16, 1, 256, 8, 128), f32)
    nst = np.zeros((16, 1, 2, 8, 128, 128), f32)
    for core in range(ncores):
        r = results[core]
        pr, par = core // 2, core % 2
        Y = np.asarray(r["yT"]).reshape(128, 8, T).transpose(2, 1, 0).reshape(T, 1024)
        K = np.asarray(r["nkT"]).reshape(128, 8, 512).transpose(2, 1, 0).reshape(512, 8, 2, 64)
        V = np.asarray(r["nv"]).reshape(128, 4, 1024).transpose(1, 0, 2).reshape(512, 8, 128)
        St = np.asarray(r["nS"]).reshape(128, 2, 2, 8, 128)
        for sq in range(2):
            ys, ks, vs = Y[sq * 256:(sq + 1) * 256], K[sq * 256:(sq + 1) * 256], V[sq * 256:(sq + 1) * 256]
            if par:
                ys, ks, vs = ys[::-1], ks[::-1], vs[::-1]
            y_prompt[2 * core + sq] = ys
            nck[2 * core + sq, 0] = ks
            ncv[2 * core + sq, 0] = vs
            for X in range(2):
                d = X if par == 0 else 1 - X
                nst[2 * core + sq, 0, d] = St[:, sq, X].transpose(1, 0, 2)
        ysm = Y[512:]
        if par:
            ysm = ysm[::-1]
        y_sample[pr, par * 1024:(par + 1) * 1024] = ysm
    return (y_prompt, y_sample, nck, ncv, nst)
```

```python
from contextlib import ExitStack
import math
import os
import numpy as np
import concourse.bass as bass
import concourse.mybir as mybir
from concourse.bass_utils import run_bass_kernel_spmd

F32 = mybir.dt.float32
BF16 = mybir.dt.bfloat16
AF = mybir.ActivationFunctionType
ALU = mybir.AluOpType

SAME_ENGINE_SYNC = True
D = 1024
T = 1536
NTB = 3
FH = 2816
NJ = 22
EPS = 1e-6
WSLOT = 4096
WRING = 3
ARENA_BYTES = 103 * 1024 - 15360 + 512


class Res:
    __slots__ = ("name", "w", "r", "excl")

    def __init__(self, name="", excl=False):
        self.name = name
        self.w = None
        self.r = {}
        self.excl = excl


class DmaSem:
    def __init__(self, h):
        self.h = h
        self.count = 0


def _merge(dst, tok, key):
    old = dst.get(key)
    if old is None or old[2] < tok[2]:
        dst[key] = tok


def _tok_key(tok):
    return tok[1] if tok[0] == "eng" else ("dma", id(tok[1]))


class Sched:
    ENGS = ("pe", "act", "dve", "pool", "sp")

    def __init__(self, nc, stack):
        self.nc = nc
        self.stack = stack
        self.dry = False
        self.q = {e: [] for e in self.ENGS}
        self.esem = {}
        for e in ("pe", "act", "dve", "pool"):
            self.esem[e] = stack.enter_context(nc.semaphore("es_" + e))
        self.n_dsem = 0

    def dma_sem(self, name=None):
        self.n_dsem += 1
        h = self.stack.enter_context(self.nc.semaphore(name or ("ds%d" % self.n_dsem)))
        return DmaSem(h)

    def op(self, e, fn, reads=(), writes=(), dsem=None, inc=16):
        if self.dry:
            return None
        deps = set()
        for r in reads:
            if r.w is not None:
                deps.add(r.w)
            if r.excl:
                deps.update(r.r.values())
        for w in writes:
            if w.w is not None:
                deps.add(w.w)
            deps.update(w.r.values())
        idx = len(self.q[e])
        if dsem is not None:
            dsem.count += inc
            tok = ("dma", dsem, dsem.count)
        else:
            tok = ("eng", e, idx)
        self.q[e].append(dict(fn=fn, deps=deps, tok=tok, dsem=dsem, inc=inc))
        k = _tok_key(tok)
        for r in reads:
            r.r[k] = tok
        for w in writes:
            w.w = tok
            w.r = {}
        return tok

    def _skip(self, e, d):
        if d[0] != "eng" or d[1] != e:
            return False
        if e == "pe":
            return True
        return not SAME_ENGINE_SYNC

    def emit(self, block, final_waits=()):
        needed = {e: set() for e in self.ENGS}
        for e in self.ENGS:
            for rec in self.q[e]:
                for d in rec["deps"]:
                    if d[0] == "eng" and not self._skip(e, d):
                        needed[d[1]].add(d[2])
        val = {}
        for e in self.ENGS:
            c = 0
            v = {}
            for i in range(len(self.q[e])):
                if i in needed[e]:
                    c += 1
                    v[i] = c
            val[e] = v

        def body(e):
            def f(eng):
                waited = {}
                for rec in self.q[e]:
                    ws = {}
                    for d in rec["deps"]:
                        if self._skip(e, d):
                            continue
                        if d[0] == "eng":
                            key = ("eng", d[1])
                            sem = self.esem[d[1]]
                            v = val[d[1]][d[2]]
                        else:
                            key = ("dma", id(d[1]))
                            sem = d[1].h
                            v = d[2]
                        if ws.get(key, (None, 0))[1] < v:
                            ws[key] = (sem, v)
                    for key, (sem, v) in ws.items():
                        if waited.get(key, 0) >= v:
                            continue
                        eng.wait_ge(sem, v)
                        waited[key] = v
                    ins = rec["fn"](eng)
                    if rec["dsem"] is not None:
                        ins.then_inc(rec["dsem"].h, rec["inc"])
                    elif rec["tok"][2] in val[e]:
                        ins.then_inc(self.esem[e], 1)
                if e == "sp":
                    for ds in final_waits:
                        if ds.count > 0:
                            eng.wait_ge(ds.h, ds.count)
            return f

        block.tensor(body("pe"))
        block.scalar(body("act"))
        block.vector(body("dve"))
        block.gpsimd(body("pool"))
        block.sync(body("sp"))


class Arena:
    def __init__(self, nc, nbytes):
        self.t = nc.alloc_sbuf_tensor("arena", [128, nbytes // 2], BF16)
        self.nbytes = nbytes
        self.off = 0
        self.cur = []
        self.prev = []
        self.released = []

    def reset(self):
        self.prev = self.cur + self.released
        self.cur = []
        self.released = []
        self.off = 0

    def push(self):
        return (self.off, len(self.cur))

    def pop(self, mark):
        rel = self.cur[mark[1]:]
        self.cur = self.cur[:mark[1]]
        self.released = self.released + rel
        self.prev = self.prev + rel
        self.off = mark[0]

    def view(self, off, shape, dt):
        save = self.off
        self.off = off
        v = self.alloc(shape, dt)
        self.off = save
        return v

    def res_from(self, olds):
        r = Res()
        for o in olds:
            if o.w is not None:
                _merge(r.r, o.w, _tok_key(o.w))
            for k, tk in o.r.items():
                _merge(r.r, tk, k)
        self.cur.append(r)
        return r

    def _newres(self, name=""):
        r = Res(name)
        for o in self.prev:
            if o.w is not None:
                _merge(r.r, o.w, _tok_key(o.w))
            for k, tk in o.r.items():
                _merge(r.r, tk, k)
        self.cur.append(r)
        return r

    def alloc(self, shape, dt, name=""):
        n = 1
        for s in shape:
            n *= s
        nb = n * (4 if dt == F32 else 2)
        nb = (nb + 31) // 32 * 32
        assert self.off + nb <= self.nbytes, ("arena overflow", name, self.off, nb)
        v = self.t[:, self.off // 2:(self.off + nb) // 2]
        self.off += nb
        if dt == F32:
            v = v.bitcast(F32)
        v = v[:, 0:n]
        if len(shape) == 2:
            v = v.rearrange("p (a b) -> p a b", a=shape[0])
        elif len(shape) == 3:
            v = v.rearrange("p (a b c) -> p a b c", a=shape[0], b=shape[1])
        return v

    def res(self, name=""):
        return self._newres(name)

    def resgrid(self, *dims):
        if len(dims) == 1:
            return [self._newres() for _ in range(dims[0])]
        return [self.resgrid(*dims[1:]) for _ in range(dims[0])]


class Ring:
    def __init__(self, items):
        self.items = items
        self.i = 0

    def get(self):
        it = self.items[self.i % len(self.items)]
        self.i += 1
        return it


CST_LAM = 0
CST_LB = 256
CST_BMOD = 288
CST_NG = 432
CST_GN = 528
CST_SUB = 529
CST_C = 530
CST_M = 546
CST_RM = 548
CST_N = 550
CB_ID = 0
CB_ONES = 128
CB_MA = 256
CB_MB = 384
CB_N = 512


def build_program(stop="full"):
    nc = bass.Bass("TRN2", target_bir_lowering=False)
    RG = [[2 * i_, 2 * i_ + 1] for i_ in range(max(1, int(os.environ.get("K_CORES", "8")) // 2))]

    def din(name, shape, dt=F32):
        return nc.dram_tensor(name, list(shape), dt, kind="ExternalInput").ap()

    def dout(name, shape, dt=F32):
        return nc.dram_tensor(name, list(shape), dt, kind="ExternalOutput").ap()

    xT_in = din("xT_in", [128, 8, T])
    cst_in = din("cst", [128, CST_N])
    cb_in = din("cb", [128, CB_N])
    rope_in = din("rope", [128, 2, 1024])
    ck_in = din("ckT", [128, 8, 256])
    cv_in = din("cv", [128, 2, 1024])
    s0_in = din("s0", [128, 8, 128])
    wmod = din("wmod", [2, 18, 128, 8, 512])
    wfin = din("wfin", [2, 2, 11, 128, 8, 512])
    wfout = din("wfout", [2, 2, 8, 128, 22, 128])
    wain = din("wain", [10, 128, 8, 512])
    waout = din("waout", [2, 128, 8, 512])
    whin = din("whin", [8, 128, 8, 512])
    whi = din("whi", [8, 128, 8, 128])
    whout = din("whout", [2, 128, 8, 512])
    yT = dout("yT", [128, 8, T])
    nkT = dout("nkT", [128, 8, 512])
    nv = dout("nv", [128, 4, 1024])
    nS = dout("nS", [128, 2, 2, 8, 128])
    cinK = nc.dram_tensor("cinK", [1024, 1024], BF16)
    coutK = nc.dram_tensor("coutK", [2048, 1024], BF16)
    cinV = nc.dram_tensor("cinV", [1024, 1024], BF16)
    coutV = nc.dram_tensor("coutV", [2048, 1024], BF16)
    cin2 = nc.dram_tensor("cin2", [1024, 128], F32)
    cout2 = nc.dram_tensor("cout2", [2048, 128], F32)

    with ExitStack() as st:
        S = Sched(nc, st)
        xT = nc.alloc_sbuf_tensor("xT", [128, 8, T], F32)
        hT = nc.alloc_sbuf_tensor("hT", [128, 8, T], BF16)
        wr = [nc.alloc_sbuf_tensor("wr%d" % i, [128, WSLOT], BF16) for i in range(WRING)]
        cst = nc.alloc_sbuf_tensor("cst_sb", [128, CST_N], F32)
        cb = nc.alloc_sbuf_tensor("cb_sb", [128, CB_N], BF16)
        onesrow = nc.alloc_sbuf_tensor("onesrow", [128, T], BF16)
        scT = nc.alloc_sbuf_tensor("scT", [128, 8, 2], BF16)
        modv = nc.alloc_sbuf_tensor("modv", [128, 2, 72, 2], F32)
        gsT = nc.alloc_sbuf_tensor("gsT", [128, 2, 3, 2, 8], F32)
        ggT = nc.alloc_sbuf_tensor("ggT", [128, 2, 3, 2, 8], F32)
        sml = nc.alloc_sbuf_tensor("sml", [128, 64], F32)
        nb_rstd = nc.alloc_sbuf_tensor("nb_rstd", [128, NTB, 512], F32)
        nb_rt = nc.alloc_sbuf_tensor("nb_rt", [128, 512], F32)
        nb_tmp = nc.alloc_sbuf_tensor("nb_tmp", [128, 2, 512], F32)
        nb_sq = nc.alloc_sbuf_tensor("nb_sq", [128, 3, 512], BF16)
        psall = nc.alloc_psum_tensor("psall", [128, 4096], F32)
        psb = [psall[:, i * 512:(i + 1) * 512] for i in range(8)]
        ar = Arena(nc, ARENA_BYTES)

        ident = cb[:, CB_ID:CB_ID + 128]
        ones = cb[:, CB_ONES:CB_ONES + 128]
        maskA = cb[:, CB_MA:CB_MA + 128]
        maskB = cb[:, CB_MB:CB_MB + 128]

        sem_w = [S.dma_sem("w%d" % i) for i in range(WRING)]
        sem_in = [S.dma_sem("in%d" % i) for i in range(8)]
        sem_out = [S.dma_sem("out%d" % i) for i in range(6)]
        sem_cc = S.dma_sem("cc")
        sem_misc = [S.dma_sem("mi%d" % i) for i in range(8)]
        sem_kv = [S.dma_sem("kv%d" % i) for i in range(4)]

        class WS:
            pass
        W = WS()
        W.plan = []
        W.n = 0
        W.issued = 0
        W.res = [Res("w%d" % i) for i in range(WRING)]

        def w_issue(m):
            src, shp = W.plan[m]
            slot = m % WRING
            n = shp[0] * shp[1]
            dst = wr[slot][:, 0:n].rearrange("p (a b) -> p a b", a=shp[0])
            S.op("pool", lambda e, dst=dst, src=src: e.dma_start(out=dst, in_=src, max_dma_last_dim=2048),
                 writes=[W.res[slot]], dsem=sem_w[slot])

        def w_next(src, shp, keep=0):
            if S.dry:
                W.plan.append((src, shp))
                return wr[0][:, 0:shp[0] * shp[1]].rearrange("p (a b) -> p a b", a=shp[0]), W.res[0]
            n = W.n
            W.n += 1
            while W.issued < min(len(W.plan), n + WRING - keep):
                w_issue(W.issued)
                W.issued += 1
            slot = n % WRING
            return wr[slot][:, 0:shp[0] * shp[1]].rearrange("p (a b) -> p a b", a=shp[0]), W.res[slot]

        def MM(out, lhsT, rhs, start, stop, reads, wres, skip=False):
            S.op("pe", lambda e: e.matmul(out, lhsT=lhsT, rhs=rhs, start=start, stop=stop, skip_group_check=skip),
                 reads=reads, writes=[wres])

        def TR(out, in_, reads, wres):
            S.op("pe", lambda e: e.transpose(out, in_, ident), reads=reads, writes=[wres])

        def ACT(out, in_, func, reads, writes, scale=1.0, bias=0.0):
            S.op("act", lambda e: e.activation(out=out, in_=in_, func=func, bias=bias, scale=scale),
                 reads=reads, writes=writes)

        def TT(eng, out, in0, in1, op, reads, writes):
            S.op(eng, lambda e: e.tensor_tensor(out=out, in0=in0, in1=in1, op=op), reads=reads, writes=writes)

        def TS(eng, out, in0, s1, s2, op0, op1, reads, writes):
            S.op(eng, lambda e: e.tensor_scalar(out=out, in0=in0, scalar1=s1, scalar2=s2, op0=op0, op1=op1),
                 reads=reads, writes=writes)

        def STT(out, in0, scalar, in1, op0, op1, reads, writes):
            S.op("dve", lambda e: e.scalar_tensor_tensor(out=out, in0=in0, scalar=scalar, in1=in1, op0=op0, op1=op1),
                 reads=reads, writes=writes)

        def CP(eng, out, in_, reads, writes):
            if eng == "act":
                S.op("act", lambda e: e.activation(out=out, in_=in_, func=AF.Copy), reads=reads, writes=writes)
            else:
                S.op(eng, lambda e: e.tensor_copy(out=out, in_=in_), reads=reads, writes=writes)

        def RCP(out, in_, reads, writes):
            S.op("dve", lambda e: e.reciprocal(out=out, in_=in_), reads=reads, writes=writes)

        def DMA(q, out, in_, reads, writes, dsem):
            S.op(q, lambda e: e.dma_start(out=out, in_=in_), reads=reads, writes=writes, dsem=dsem)

        def tbs(tb):
            return slice(tb * 512, (tb + 1) * 512)

        R_x = [[Res() for _ in range(NTB)] for _ in range(8)]
        R_h = [[Res() for _ in range(NTB)] for _ in range(8)]
        R_ps = [Res("ps%d" % i, excl=True) for i in range(8)]
        R_cst = Res("cst")
        R_cb = Res("cb")
        R_mod = Res("mod")
        R_sml = Res("sml")
        R_sc = Res("sc")
        R_dram = {k: Res(k) for k in ["cinK", "coutK", "cinV", "coutV", "cin2", "cout2", "yT", "nkT", "nv", "nS"]}

        def body():
            gen = Ring([(psb[i], R_ps[i]) for i in range(4)])
            outsem = Ring(sem_out)
            DMA("sp", cst[:, :], cst_in[:, :], [], [R_cst], sem_in[0])
            S.op("pool", lambda e: e.dma_start(out=cb[:, :], in_=cb_in[:, :]), writes=[R_cb], dsem=sem_in[1])
            for tb in range(NTB):
                DMA("sp", xT[:, :, tbs(tb)], xT_in[:, :, tbs(tb)], [], [R_x[fo][tb] for fo in range(8)], sem_in[2 + tb])
            S.op("dve", lambda e: e.memset(onesrow[:, :], 1.0), writes=[R_cb])
            ACT(scT[:, :, :], cst[:, CST_C:CST_C + 16].rearrange("p (k v) -> p k v", k=8), AF.Silu, [R_cst], [R_sc])

            def mod_slab(l, s):
                pm, rpm = psb[7], R_ps[7]
                wv, wres = w_next(wmod[l, s], (8, 512))
                for cc in range(4):
                    ch = s * 4 + cc
                    for k in range(8):
                        MM(pm[:, ch * 2:ch * 2 + 2], wv[:, k, cc * 128:(cc + 1) * 128], scT[:, k, :],
                           k == 0, k == 7, [wres, R_sc], rpm)

            def mod_fin(l, j, part="both"):
                pm, rpm = psb[7], R_ps[7]
                c0, c1 = {"both": (0, 24), "pre": (0, 16), "post": (16, 24)}[part]
                bm = cst[:, CST_BMOD + l * 72 + 24 * j + c0:CST_BMOD + l * 72 + 24 * j + c1]
                for v in range(2):
                    TT("dve", modv[:, l, 24 * j + c0:24 * j + c1, v], pm[:, 48 * j + 2 * c0 + v:48 * j + 2 * c1:2], bm, ALU.add,
                       [rpm, R_cst], [R_mod])
                if True:
                    for v in range(2):
                        sc_ = modv[:, l, (3 * j + 1) * 8:(3 * j + 2) * 8, v]
                        gt_ = modv[:, l, (3 * j + 2) * 8:(3 * j + 3) * 8, v]
                        gpre = cst[:, CST_NG + (l * 6 + 2 * j) * 8:CST_NG + (l * 6 + 2 * j + 1) * 8]
                        gpost = cst[:, CST_NG + (l * 6 + 2 * j + 1) * 8:CST_NG + (l * 6 + 2 * j + 2) * 8]
                        rw = 1.0 if j == 1 else 0.5
                        if part in ("both", "pre"):
                            STT(gsT[:, l, j, v, :], sc_, 1.0, gpre, ALU.add, ALU.mult, [R_mod, R_cst], [R_mod])
                        if part in ("both", "post"):
                            STT(ggT[:, l, j, v, :], gt_, rw, gpost, ALU.mult, ALU.mult, [R_mod, R_cst], [R_mod])

            class NB:
                pass

            NBP = NB()
            NBP.rstd = nb_rstd
            NBP.R_rstd = [Res() for _ in range(NTB)]
            NBP.rt = nb_rt
            NBP.R_rt = Res()
            NBP.tmp = Ring([(nb_tmp[:, i, :], Res()) for i in range(2)])
            NBP.sq = Ring([(nb_sq[:, i, :], Res()) for i in range(3)])

            def norm_bufs():
                return NBP

            def rstd_from(b, tb, ssb, rss, n_feat):
                ACT(b.rt[:, :], ssb[:, :], AF.Ln, [rss], [b.R_rt], scale=1.0 / n_feat, bias=EPS)
                ACT(b.rstd[:, tb, :], b.rt[:, :], AF.Exp, [b.R_rt], [b.R_rstd[tb]], scale=-0.5)

            def prenorm(b, l, j):
                for tb in range(NTB):
                    prenorm_tb(b, l, j, tb)

            def prenorm_tb(b, l, j, tb):
                if True:
                    v = 0 if tb == 0 else 1
                    ssb, rss = psb[4 + tb], R_ps[4 + tb]
                    for fo in range(8):
                        sq, rsq = b.sq.get()
                        ACT(sq, xT[:, fo, tbs(tb)], AF.Square, [R_x[fo][tb]], [rsq])
                        MM(ssb[:, :], ones, sq, fo == 0, fo == 7, [rsq, R_cb], rss)
                    rstd_from(b, tb, ssb, rss, 1024.0)
                    for fo in range(8):
                        tm, rtm = b.tmp.get()
                        TT("dve" if fo % 2 == 0 else "pool", tm, xT[:, fo, tbs(tb)], b.rstd[:, tb, :], ALU.mult,
                           [R_x[fo][tb], b.R_rstd[tb]], [rtm])
                        ACT(hT[:, fo, tbs(tb)], tm, AF.Identity, [rtm, R_mod], [R_h[fo][tb]],
                            scale=gsT[:, l, j, v, fo:fo + 1], bias=modv[:, l, 3 * j * 8 + fo, v:v + 1])

            def post_chunk(b, bank, rbank, fo, tb, yb, R_yb):
                PC = int(os.environ.get("K_PC", "7"))
                sq, rsq = b.sq.get()
                if PC & 1:
                    ACT(sq, bank[:, :], AF.Square, [rbank], [rsq])
                if PC & 2:
                    CP("dve", yb[:, fo, tbs(tb)], bank[:, :], [rbank], [R_yb[fo][tb]])
                if PC & 4:
                    MM(psb[4 + tb][:, :], ones, sq, fo == 0, fo == 7, [rsq, R_cb], R_ps[4 + tb])

            def post_finish(b, l, j, yb, R_yb, nxt=None):
                for tb in range(NTB):
                    v = 0 if tb == 0 else 1
                    rstd_from(b, tb, psb[4 + tb], R_ps[4 + tb], 1024.0)
                    for fo in range(8):
                        tm, rtm = b.tmp.get()
                        TT("pool", tm, yb[:, fo, tbs(tb)], b.rstd[:, tb, :], ALU.mult, [R_yb[fo][tb], b.R_rstd[tb]], [rtm])
                        STT(xT[:, fo, tbs(tb)], tm, ggT[:, l, j, v, fo:fo + 1], xT[:, fo, tbs(tb)], ALU.mult, ALU.add,
                            [rtm, R_mod, R_x[fo][tb]], [R_x[fo][tb]])
                    if tb > 0 and nxt is not None:
                        prenorm_tb(b, nxt[0], nxt[1], tb - 1)
                if nxt is not None:
                    prenorm_tb(b, nxt[0], nxt[1], NTB - 1)

            def ffn(l, i, j, nxt=None, hooks=()):
                hooks = list(hooks)
                ar.reset()
                b = norm_bufs()
                uT = ar.alloc([NJ, T], BF16, "uT")
                R_u = ar.resgrid(NJ, NTB)
                sa_t = ar.alloc([2, 512], F32, "sa")
                sa = Ring([(sa_t[:, i2, :], ar.res()) for i2 in range(2)])
                DBG = int(os.environ.get("K_DBG", "9"))
                for s in range(11 if DBG >= 2 else 0):
                    wv, wres = w_next(wfin[l, i, s], (8, 512))
                    for jj in range(2):
                        hc = 2 * s + jj
                        for tb in range(NTB):
                            pa, rpa = gen.get()
                            pb, rpb = gen.get()
                            for k in range(8):
                                MM(pa[:, :], wv[:, k, jj * 128:(jj + 1) * 128], hT[:, k, tbs(tb)], k == 0, k == 7,
                                   [wres, R_h[k][tb]], rpa)
                            for k in range(8):
                                MM(pb[:, :], wv[:, k, 256 + jj * 128:256 + (jj + 1) * 128], hT[:, k, tbs(tb)], k == 0, k == 7,
                                   [wres, R_h[k][tb]], rpb)
                            s_, rs_ = sa.get()
                            ACT(s_, pa[:, :], AF.Silu, [rpa], [rs_])
                            TT("dve", uT[:, hc, tbs(tb)], s_, pb[:, :], ALU.mult, [rs_, rpb], [R_u[hc][tb]])
                    if hooks:
                        hooks.pop(0)()
                for fo in range(8 if DBG >= 3 else 0):
                    wv, wres = w_next(wfout[l, i, fo], (22, 128))
                    for tb in range(NTB):
                        pa, rpa = gen.get()
                        for hc in range(NJ):
                            MM(pa[:, :], wv[:, hc, :], uT[:, hc, tbs(tb)], hc == 0, hc == NJ - 1,
                               [wres, R_u[hc][tb]], rpa)
                        if DBG >= 4:
                            post_chunk(b, pa, rpa, fo, tb, hT, R_h)
                    if hooks:
                        hooks.pop(0)()
                while hooks:
                    hooks.pop(0)()
                if DBG >= 5:
                    post_finish(b, l, j, hT, R_h, nxt)

            def attention(nxt=None):
                l, j = 0, 1
                lam_init = 0.8 - 0.6 * math.exp(-0.3 * l)
                ar.reset()
                qTp = ar.alloc([8, 512], BF16, "qTp"); R_qp = ar.resgrid(8)
                kTp = ar.alloc([8, 512], BF16, "kTp"); R_kp = ar.resgrid(8)
                Vp = ar.alloc([4, 1024], BF16, "Vp"); R_vp = ar.resgrid(4, 2)
                b = norm_bufs()
                qTs = ar.alloc([8, 1024], BF16, "qTs"); R_qs = ar.resgrid(8, 2)
                ckT = ar.alloc([2, 8, 256], BF16, "ckT"); R_ck = ar.res()
                cvb = ar.alloc([2, 1024], BF16, "cvb"); R_cv = ar.res()
                mark = ar.push()
                rope = ar.alloc([2, 1024], F32, "rope"); R_rope = ar.res()
                r1_t = ar.alloc([2, 512], F32, "r1")
                r2_t = ar.alloc([2, 512], F32, "r2")
                r1 = Ring([(r1_t[:, i, :], ar.res()) for i in range(2)])
                r2 = Ring([(r2_t[:, i, :], ar.res()) for i in range(2)])
                kst_t = ar.alloc([2, 512], BF16, "kst")
                kst = Ring([(kst_t[:, i, :], ar.res(), sem_misc[i]) for i in range(2)])
                vst_t = ar.alloc([2, 512], BF16, "vst")
                vst = Ring([(vst_t[:, i, :], ar.res(), sem_misc[2 + i]) for i in range(2)])
                fo_t = ar.alloc([2, 512], F32, "fout")
                fst = Ring([(fo_t[:, i, :], ar.res(), sem_misc[4 + i]) for i in range(2)])
                tq_t = ar.alloc([64], F32, "lamt")
                R_tq = ar.res()
                cosT = rope[:, 0, :]
                sinT = rope[:, 1, :]

                DMA("sp", rope[:, :, :], rope_in[:, :, :], [], [R_rope], sem_in[7])
                S.op("dve", lambda e: e.memset(ckT[:, :, :, :], 0.0), writes=[R_ck])
                for c_ in range(2):
                    S.op("pool", lambda e, c_=c_: e.dma_start(out=ckT[c_ * 64:(c_ + 1) * 64, c_, :, :],
                                                              in_=ck_in[c_ * 64:(c_ + 1) * 64, :, :], max_dma_last_dim=1024),
                         writes=[R_ck], dsem=sem_misc[6])
                S.op("pool", lambda e: e.dma_start(out=cvb[:, :, :], in_=cv_in[:, :, :], max_dma_last_dim=2048),
                     writes=[R_cv], dsem=sem_misc[7])
                lamt = tq_t
                for i2 in range(2):
                    TT("dve", lamt[:, 0:64], cst[:, CST_LAM + i2 * 128:CST_LAM + i2 * 128 + 64],
                       cst[:, CST_LAM + i2 * 128 + 64:CST_LAM + i2 * 128 + 128], ALU.mult, [R_cst], [R_tq])
                    S.op("dve", lambda e, i2=i2: e.reduce_sum(out=sml[:, i2:i2 + 1], in_=lamt[:, 0:64], axis=mybir.AxisListType.X),
                         reads=[R_tq], writes=[R_sml])
                ACT(sml[:, 2:4], sml[:, 0:2], AF.Exp, [R_sml], [R_sml])
                TT("dve", sml[:, 4:5], sml[:, 3:4], sml[:, 2:3], ALU.subtract, [R_sml], [R_sml])
                TS("dve", sml[:, 4:5], sml[:, 4:5], -lam_init, None, ALU.add, ALU.bypass, [R_sml], [R_sml])
                TS("dve", sml[:, 5:6], cst[:, CST_SUB:CST_SUB + 1], 1.0 - lam_init, None, ALU.mult, ALU.bypass, [R_cst], [R_sml])
                neglam = sml[:, 4:5]
                gsub = sml[:, 5:6]

                for which in range(2):
                    for s in range(2):
                        wv, wres = w_next(wain[which * 4 + s], (8, 512))
                        wv2, wres2 = w_next(wain[which * 4 + 2 + s], (8, 512), keep=1)
                        for hh in range(4):
                            hd = 4 * s + hh
                            cs = slice(hh * 128, (hh + 1) * 128)
                            pa, rpa = gen.get()
                            for k in range(8):
                                MM(pa[:, :], wv[:, k, cs], hT[:, k, tbs(0)], k == 0, k == 7, [wres, R_h[k][0]], rpa)
                            if which == 0:
                                CP("act", qTp[:, hd, :], pa[:, :], [rpa], [R_qp[hd]])
                            else:
                                CP("act", kTp[:, hd, :], pa[:, :], [rpa], [R_kp[hd]])
                                fo_, rfo, sfo = fst.get()
                                CP("dve", fo_, pa[:, :], [rpa], [rfo])
                                DMA("sp", nkT[:, hd, :], fo_, [rfo], [R_dram["nkT"]], sfo)
                            for tb in (1, 2):
                                pa, rpa = gen.get()
                                pb, rpb = gen.get()
                                for k in range(8):
                                    MM(pa[:, :], wv[:, k, cs], hT[:, k, tbs(tb)], k == 0, k == 7, [wres, R_h[k][tb]], rpa)
                                for k in range(8):
                                    MM(pb[:, :], wv2[:, k, cs], hT[:, k, tbs(tb)], k == 0, k == 7, [wres2, R_h[k][tb]], rpb)
                                a1, ra1 = r1.get()
                                a2, ra2 = r2.get()
                                lc = slice((tb - 1) * 512, tb * 512)
                                TT("dve", a1, pb[:, :], sinT[:, lc], ALU.mult, [rpb, R_rope], [ra1])
                                TT("dve", a2, pa[:, :], cosT[:, lc], ALU.mult, [rpa, R_rope], [ra2])
                                if which == 0:
                                    TT("pool", qTs[:, hd, lc], a1, a2, ALU.add, [ra1, ra2], [R_qs[hd][tb - 1]])
                                else:
                                    ks, rks, sks = kst.get()
                                    TT("pool", ks, a1, a2, ALU.add, [ra1, ra2], [rks])
                                    DMA("sp", cinK[hd * 128:(hd + 1) * 128, lc], ks, [rks], [R_dram["cinK"]], sks)
                S.op("pool", lambda e: e.collective_compute(
                    "AllGather", ALU.bypass, replica_groups=RG,
                    ins=[cinK.ap().opt()], outs=[coutK.ap().opt()]),
                    reads=[R_dram["cinK"]], writes=[R_dram["coutK"]], dsem=sem_cc, inc=1)
                for s in range(2):
                    wv, wres = w_next(wain[8 + s], (8, 512))
                    for t in range(12):
                        tb = t // 4
                        pa, rpa = gen.get()
                        for k in range(8):
                            MM(pa[:, :], hT[:, k, t * 128:(t + 1) * 128], wv[:, k, :], k == 0, k == 7, [wres, R_h[k][tb]], rpa)
                        if t < 4:
                            CP("act", Vp[:, t, s * 512:(s + 1) * 512], pa[:, :], [rpa], [R_vp[t][s]])
                            fo_, rfo, sfo = fst.get()
                            CP("dve", fo_, pa[:, :], [rpa], [rfo])
                            DMA("sp", nv[:, t, s * 512:(s + 1) * 512], fo_, [rfo], [R_dram["nv"]], sfo)
                        else:
                            vs, rvs, svs = vst.get()
                            CP("act", vs, pa[:, :], [rpa], [rvs])
                            DMA("sp", cinV[(t - 4) * 128:(t - 3) * 128, s * 512:(s + 1) * 512], vs, [rvs],
                                [R_dram["cinV"]], svs)
                S.op("pool", lambda e: e.collective_compute(
                    "AllGather", ALU.bypass, replica_groups=RG,
                    ins=[cinV.ap().opt()], outs=[coutV.ap().opt()]),
                    reads=[R_dram["cinV"]], writes=[R_dram["coutV"]], dsem=sem_cc, inc=1)

                ar.pop(mark)
                kbuf_t = ar.alloc([2, 2, 2048], BF16, "kbuf")
                vbuf_t = ar.alloc([2, 2048], BF16, "vbuf")
                kvb = Ring([(kbuf_t[:, i, :, :], vbuf_t[:, i, :], ar.res(), ar.res(), sem_kv[2 * i], sem_kv[2 * i + 1]) for i in range(2)])
                S.op("dve", lambda e: e.memset(kbuf_t[:, :, :, :], 0.0), writes=[it_[2] for it_ in kvb.items])
                eT_t = ar.alloc([3, 2, 512], BF16, "eT")
                eTr = Ring([(eT_t[:, i, :, :], ar.res()) for i in range(3)])
                rz_t = ar.alloc([512], F32, "rz"); R_rz = ar.res()
                tq_t = ar.alloc([512], F32, "tq"); R_tq = ar.res()
                osb = ar.alloc([512], F32, "osb"); R_osb = ar.res()

                def attn_norm(n, hd, col0):
                    tb = col0 // 512
                    sq, rsq = b.sq.get()
                    ACT(sq[:, 0:n], osb[:, 0:n], AF.Square, [R_osb], [rsq])
                    pn, rpn = gen.get()
                    MM(pn[:, 0:n], ones, sq[:, 0:n], True, True, [rsq, R_cb], rpn)
                    ACT(b.rt[:, 0:n], pn[:, 0:n], AF.Ln, [rpn], [b.R_rt], scale=1.0 / 128.0, bias=EPS)
                    ACT(rz_t[:, 0:n], b.rt[:, 0:n], AF.Exp, [b.R_rt], [R_rz], scale=-0.5)
                    STT(hT[:, hd, col0:col0 + n], osb[:, 0:n], gsub, rz_t[:, 0:n], ALU.mult, ALU.mult,
                        [R_osb, R_sml, R_rz], [R_h[hd][tb]])

                def pstageA(seq, hd):
                    qc = slice(seq * 256, (seq + 1) * 256)
                    e2, re_ = eTr.get()
                    for kc in range(2):
                        kt = 2 * seq + kc
                        for c in range(2):
                            ps_ = slice(c * 64, (c + 1) * 64)
                            pa, rpa = gen.get()
                            MM(pa[:, 0:256], kTp[ps_, hd, kt * 128:(kt + 1) * 128], qTp[ps_, hd, qc],
                               True, True, [R_kp[hd], R_qp[hd]], rpa)
                            ACT(e2[:, kc, c * 256:(c + 1) * 256], pa[:, 0:256], AF.Exp, [rpa], [re_], scale=0.125)
                    return (seq, hd, e2, re_)

                def pstageB(it):
                    seq, hd, e2, re_ = it
                    po, rpo = psb[4], R_ps[4]
                    pz, rpz = psb[5], R_ps[5]
                    for kc in range(2):
                        kt = 2 * seq + kc
                        MM(po[:, :], Vp[:, kt, hd * 128:(hd + 1) * 128], e2[:, kc, :], kc == 0, kc == 1,
                           [re_, R_vp[kt][hd // 4]], rpo)
                    for kc in range(2):
                        MM(pz[:, :], ones, e2[:, kc, :], kc == 0, kc == 1, [re_, R_cb], rpz)
                    ACT(b.rt[:, :], pz[:, :], AF.Ln, [rpz], [b.R_rt])
                    ACT(rz_t[:, :], b.rt[:, :], AF.Exp, [b.R_rt], [R_rz], scale=-1.0)
                    TT("dve", tq_t[:, :], po[:, :], rz_t[:, :], ALU.mult, [rpo, R_rz], [R_tq])
                    STT(osb[:, 0:256], tq_t[:, 256:512], neglam, tq_t[:, 0:256], ALU.mult, ALU.add,
                        [R_tq, R_sml], [R_osb])
                    attn_norm(256, hd, seq * 256)

                prevb = None
                for seq in range(2):
                    for hd in range(8):
                        curb = pstageA(seq, hd)
                        if prevb is not None:
                            pstageB(prevb)
                        prevb = curb
                pstageB(prevb)

                coutKv = coutK.ap().rearrange("(r x p) n -> p r x n", r=2, p=128)
                coutVv = coutV.ap().rearrange("(r x p) n -> p r x n", r=2, p=128)
                kvs = {}

                def load_kv(hd):
                    kb, vb, rkb, rvb, skk, skv = kvb.get()
                    for c_ in range(2):
                        DMA("sp", kb[c_ * 64:(c_ + 1) * 64, c_, :].rearrange("p (r n) -> p r n", r=2),
                            coutKv[c_ * 64:(c_ + 1) * 64, :, hd, :], [R_dram["coutK"]], [rkb], skk)
                    for r_ in range(2):
                        DMA("sp", vb[:, r_ * 1024:(r_ + 1) * 1024].rearrange("p (t v) -> p t v", t=8),
                            coutVv[:, r_, :, hd * 128:(hd + 1) * 128], [R_dram["coutV"]], [rvb], skv)
                    kvs[hd] = (kb, vb, rkb, rvb)

                pairs = Ring([0, 1, 2])
                po, rpo, pz, rpz = psb[6], R_ps[6], psb[7], R_ps[7]

                def stage1(hd, qb, c, kp):
                    kb, vb, rkb, rvb = kvs[hd]
                    qcs = slice(qb * 512, (qb + 1) * 512)
                    pi = pairs.get()
                    vls = []
                    for u in range(2):
                        kc = 2 * kp + u
                        if kc < 16:
                            kl = kb[:, c, kc * 128:(kc + 1) * 128]
                            vl = vb[:, kc * 128:(kc + 1) * 128]
                            rk_, rv_ = rkb, rvb
                        else:
                            kl = ckT[:, c, hd, (kc - 16) * 128:(kc - 15) * 128]
                            vl = cvb[:, kc - 16, hd * 128:(hd + 1) * 128]
                            rk_, rv_ = R_ck, R_cv
                        MM(psb[2 * pi + u][:, :], kl, qTs[:, hd, qcs], True, True, [rk_, R_qs[hd][qb]], R_ps[2 * pi + u])
                        vls.append((vl, rv_))
                    e2, re_ = eTr.get()
                    ACT(e2, psall[:, 2 * pi * 512:(2 * pi + 2) * 512].rearrange("p (a b) -> p a b", a=2), AF.Exp,
                        [R_ps[2 * pi], R_ps[2 * pi + 1]], [re_], scale=0.125)
                    return (hd, qb, c, kp, e2, re_, vls)

                def stage2(it):
                    hd, qb, c, kp, e2, re_, vls = it
                    for u in range(2):
                        kc = 2 * kp + u
                        vl, rv_ = vls[u]
                        MM(po[:, :], vl, e2[:, u, :], kc == 0, kc == 17, [re_, rv_], rpo)
                        MM(pz[:, :], ones, e2[:, u, :], kc == 0, kc == 17, [re_, R_cb], rpz)
                    if kp == 8:
                        ACT(b.rt[:, :], pz[:, :], AF.Ln, [rpz], [b.R_rt])
                        ACT(rz_t[:, :], b.rt[:, :], AF.Exp, [b.R_rt], [R_rz], scale=-1.0)
                        if c == 0:
                            TT("dve", tq_t[:, :], po[:, :], rz_t[:, :], ALU.mult, [rpo, R_rz], [R_tq])
                        else:
                            TT("dve", osb[:, :], po[:, :], rz_t[:, :], ALU.mult, [rpo, R_rz], [R_osb])
                            STT(osb[:, :], osb[:, :], neglam, tq_t[:, :], ALU.mult, ALU.add, [R_tq, R_sml, R_osb], [R_osb])
                            attn_norm(512, hd, 512 + qb * 512)

                pend = []
                load_kv(0)
                for hd in range(8):
                    for qb in range(2):
                        for c in range(2):
                            for kp in range(9):
                                if hd + 1 < 8 and qb == 0 and c == 0 and kp == 4:
                                    load_kv(hd + 1)
                                pend.append(stage1(hd, qb, c, kp))
                                if len(pend) > 2:
                                    stage2(pend.pop(0))
                while pend:
                    stage2(pend.pop(0))

                yb = ar.view(0, [8, T], BF16)
                olds = R_qp + R_kp + [r_ for rr in R_vp for r_ in rr]
                R_yb = [[ar.res_from(olds) for _ in range(NTB)] for _ in range(8)]
                for s in range(2):
                    wv, wres = w_next(waout[s], (8, 512))
                    for ff in range(4):
                        fo = 4 * s + ff
                        for tb in range(NTB):
                            pa, rpa = gen.get()
                            for hd in range(8):
                                MM(pa[:, :], wv[:, hd, ff * 128:(ff + 1) * 128], hT[:, hd, tbs(tb)], hd == 0, hd == 7,
                                   [wres, R_h[hd][tb]], rpa)
                            post_chunk(b, pa, rpa, fo, tb, yb, R_yb)
                post_finish(b, l, j, yb, R_yb, nxt)

            def hgrn(nxt=None):
                l, j = 1, 1
                ar.reset()
                b = norm_bufs()
                on = ar.alloc([8, T], BF16, "on"); R_on = ar.resgrid(8, NTB)
                itok = ar.alloc([12, 128], BF16, "itok"); R_it = ar.res()
                T1 = ar.alloc([T], F32, "T1"); R_T1 = ar.res()
                T2 = ar.alloc([T], F32, "T2"); R_T2 = ar.res()
                PB = ar.alloc([T + 1], F32, "PB"); R_PB = ar.res()
                qb_ = ar.alloc([T], BF16, "qb"); R_qb = ar.res()
                sg = ar.alloc([T], BF16, "sg"); R_sg = ar.res()
                kx = ar.alloc([T], BF16, "kx"); R_kx = ar.res()
                qt = ar.alloc([1, T], BF16, "qt"); R_qt = [ar.res()] * 2
                kt = ar.alloc([1, T], BF16, "kt"); R_kt = [ar.res()] * 2
                AT = ar.alloc([12, 128], BF16, "AT"); R_AT = [ar.res()] * 2
                ktok_off = ar.off
                ktok = ar.alloc([2, 12, 128], BF16, "ktok"); R_ktok = [ar.res()] * 2
                sh_t = ar.alloc([2, 128], BF16, "shat")
                shr = Ring([(sh_t[:, i, :], ar.res()) for i in range(2)])
                Sst_t = ar.alloc([4, 128], F32, "Sst")
                sring = Ring([(Sst_t[:, i, :], ar.res()) for i in range(4)])
                td_t = ar.alloc([2, 4, 128], F32, "td")
                tdr = Ring([(td_t[:, i, :, :], ar.res()) for i in range(2)])
                csc = ar.alloc([2, 3, 24], F32, "csc"); R_csc = ar.resgrid(2)
                cdf = ar.alloc([3, 24], F32, "cdf"); R_cdf = ar.res()
                lbv = ar.alloc([2, 2, 8], F32, "lbv"); R_lb = ar.res()
                nol = ar.alloc([2, 8], F32, "nol")
                S0 = ar.alloc([8, 128], F32, "S0"); R_S0 = ar.res()
                Srv = ar.view(ktok_off, [8, 128], F32)
                R_Srv = R_ktok[0]
                SinB = ar.alloc([8, 128], F32, "SinB"); R_SinB = ar.res()
                Sout = ar.alloc([1, 128], F32, "Sout")
                sor = Ring([(Sout[:, i, :], ar.res(), sem_misc[i]) for i in range(1)])
                (osb, R_osb), (rz_t, R_rz) = b.tmp.items[0], b.tmp.items[1]

                DMA("sp", S0[:, :, :], s0_in[:, :, :], [], [R_S0], sem_in[5])
                S.op("dve", lambda e: e.memset(PB[:, 0:1], 0.0), writes=[R_PB])
                for X in range(2):
                    l0 = cst[:, CST_LB + X * 8:CST_LB + X * 8 + 8]
                    l1 = cst[:, CST_LB + 16 + X * 8:CST_LB + 16 + X * 8 + 8]
                    TT("dve", lbv[:, X, 1, :], l1, l0, ALU.subtract, [R_cst], [R_lb])
                    ACT(lbv[:, X, 0, :], lbv[:, X, 1, :], AF.Sigmoid, [R_lb], [R_lb])
                    TS("dve", lbv[:, X, 1, :], lbv[:, X, 0, :], -1.0, 1.0, ALU.mult, ALU.add, [R_lb], [R_lb])
                    TS("dve", nol[:, X, :], lbv[:, X, 1, :], -1.0, None, ALU.mult, ALU.bypass, [R_lb], [R_lb])
                gn = cst[:, CST_GN:CST_GN + 1]

                SEGS = [(0, 256, 0), (256, 512, 1), (512, 1536, 2)]

                class P:
                    pass

                def fp1(p):
                    hd, X, full = p.hd, p.X, p.full
                    p.wv, p.wres = w_next(whin[hd], (8, 512))
                    p.wi, p.wires = w_next(whi[hd], (8, 128), keep=1)
                    wv, wres = p.wv, p.wres
                    tb_list = range(NTB) if full else (1, 2)
                    t_lo = 0 if full else 512
                    tl = slice(t_lo, T)
                    p.t_lo, p.tl = t_lo, tl
                    if full:
                        for tb in tb_list:
                            pa, rpa = gen.get()
                            for k in range(8):
                                MM(pa[:, :], wv[:, k, 0:128], hT[:, k, tbs(tb)], k == 0, k == 7, [wres, R_h[k][tb]], rpa)
                            CP("dve", qb_[:, tbs(tb)], pa[:, :], [rpa], [R_qb])
                    yield
                    lb_ = lbv[:, X, 0, hd:hd + 1]
                    oml_ = lbv[:, X, 1, hd:hd + 1]
                    nol_ = nol[:, X, hd:hd + 1]
                    for tb in tb_list:
                        pa, rpa = gen.get()
                        for k in range(8):
                            MM(pa[:, :], wv[:, k, 128 * (1 + X):128 * (2 + X)], hT[:, k, tbs(tb)], k == 0, k == 7,
                               [wres, R_h[k][tb]], rpa)
                        ACT(T1[:, tbs(tb)], pa[:, :], AF.Sigmoid, [rpa], [R_T1])
                    yield
                    ACT(T2[:, tl], T1[:, tl], AF.Ln, [R_T1, R_lb], [R_T2], scale=oml_, bias=lb_)
                    TS("dve", kx[:, tl], T1[:, tl], nol_, oml_, ALU.mult, ALU.add, [R_T1, R_lb], [R_kx])
                    yield
                    S.op("dve", lambda e, t_lo=t_lo: e.memset(PB[:, t_lo:t_lo + 1], 0.0), reads=[R_cdf], writes=[R_PB])
                    S.op("dve", lambda e, tl=tl: e.tensor_tensor_scan(
                        out=PB[:, 1 + tl.start:1 + tl.stop], data0=onesrow[:, tl], data1=T2[:, tl], initial=0.0,
                        op0=ALU.mult, op1=ALU.add), reads=[R_T2, R_cb], writes=[R_PB])
                    yield
                    c_lo = t_lo // 64
                    nch = (T - t_lo) // 64
                    sh_ = 1 if X == 0 else 0
                    Pv = PB[:, sh_ + t_lo:sh_ + T].rearrange("p (c t) -> p c t", t=64)
                    ref = PB[:, sh_ + t_lo + 32:sh_ + t_lo + 32 + 64 * (nch - 1) + 1:64].unsqueeze(2).broadcast_to([128, nch, 64])
                    TT("dve", T2[:, tl].rearrange("p (c t) -> p c t", t=64), Pv, ref, ALU.subtract, [R_PB, R_T2], [R_T2])
                    yield

                    def pcol(off):
                        return PB[:, t_lo + off:t_lo + off + 64 * (nch - 1) + 1:64]
                    if X == 0:
                        pairs = [(pcol(64), pcol(0)), (pcol(64), pcol(33)), (pcol(33), pcol(0))]
                    else:
                        pairs = [(pcol(64), pcol(0)), (pcol(32), pcol(0)), (pcol(64), pcol(32))]
                    for i2, (pa_, pb_) in enumerate(pairs):
                        TT("dve", cdf[:, i2, c_lo:c_lo + nch], pa_, pb_, ALU.subtract, [R_PB], [R_cdf])
                    ACT(csc[:, p.slot, :, c_lo:c_lo + nch], cdf[:, :, c_lo:c_lo + nch], AF.Exp, [R_cdf], [R_csc[p.slot]])
                    yield
                    ACT(T1[:, tl], T2[:, tl], AF.Exp, [R_T2], [R_T1])
                    yield
                    ACT(T2[:, tl], T2[:, tl], AF.Exp, [R_T2], [R_T2], scale=-1.0)

                def fp2(p):
                    hd, X, full, tl = p.hd, p.X, p.full, p.tl
                    if X == 1:
                        for tb in range(NTB):
                            pa, rpa = gen.get()
                            for k in range(8):
                                MM(pa[:, :], p.wv[:, k, 384:512], hT[:, k, tbs(tb)], k == 0, k == 7, [p.wres, R_h[k][tb]], rpa)
                            ACT(sg[:, tbs(tb)], pa[:, :], AF.Sigmoid, [rpa], [R_sg])
                    if True:
                        wi, wires = p.wi, p.wires
                        for g in range(3):
                            if not full and g == 0:
                                continue
                            pa, rpa = gen.get()
                            for tt in range(4):
                                t = 4 * g + tt
                                for k in range(8):
                                    MM(pa[:, tt * 128:(tt + 1) * 128], hT[:, k, t * 128:(t + 1) * 128], wi[:, k, :], k == 0, k == 7,
                                       [wires, R_h[k][g]], rpa)
                            CP("act", itok[:, 4 * g:4 * g + 4, :], pa[:, :].rearrange("p (a b) -> p a b", a=4), [rpa], [R_it])
                    Eq, Ek = (T1, T2) if X == 0 else (T2, T1)
                    if full:
                        TT("dve", qt[:, 0, tl], qb_[:, tl], Eq[:, tl], ALU.mult, [R_qb, R_T1, R_T2], [R_qt[X]])
                    TT("dve", kt[:, 0, tl], kx[:, tl], Ek[:, tl], ALU.mult, [R_kx, R_T1, R_T2], [R_kt[X]])
                    mask = maskA if X == 0 else maskB
                    for g in range(3):
                        if not full and g == 0:
                            continue
                        if full:
                            pa, rpa = gen.get()
                            for tt in range(4):
                                t = 4 * g + tt
                                tok = slice(t * 128, (t + 1) * 128)
                                MM(pa[:, tt * 128:(tt + 1) * 128], kt[:, 0, tok], qt[:, 0, tok], True, True,
                                   [R_kt[X], R_qt[X]], rpa)
                            TT("dve", AT[:, 4 * g:4 * g + 4, :], pa[:, :].rearrange("p (a b) -> p a b", a=4),
                               mask.unsqueeze(1).broadcast_to([128, 4, 128]), ALU.mult, [rpa, R_cb], [R_AT[X]])
                        pa, rpa = gen.get()
                        pab = pa[:, 0:256].bitcast(BF16)
                        for tt in range(4):
                            t = 4 * g + tt
                            TR(pab[:, tt * 128:(tt + 1) * 128], kt[:, 0, t * 128:(t + 1) * 128], [R_kt[X], R_cb], rpa)
                        ACT(ktok[:, 0, 4 * g:4 * g + 4, :], pab.rearrange("p (a b) -> p a b", a=4), AF.Identity,
                            [rpa, R_cst], [R_ktok[X]], scale=cst[:, CST_RM:CST_RM + 1])
                        TS("dve", ktok[:, 1, 4 * g:4 * g + 4, :], pab.rearrange("p (a b) -> p a b", a=4),
                           cst[:, CST_RM + 1:CST_RM + 2], None, ALU.mult, ALU.bypass, [rpa, R_cst], [R_ktok[X]])

                o_started = [False] * NTB

                def back(p, filler=None):
                    hd, X, full = p.hd, p.X, p.full
                    if full:
                        for g in range(3):
                            o_started[g] = False
                        for g in range(3):
                            ob, rob = psb[4 + g], R_ps[4 + g]
                            for tt in range(4):
                                t = 4 * g + tt
                                MM(ob[:, tt * 128:(tt + 1) * 128], itok[:, t, :], AT[:, t, :], not o_started[g], False,
                                   [R_it, R_AT[X]], rob, skip=True)
                                o_started[g] = True
                    for (t0, t1, kind) in SEGS:
                        if not full and kind != 2:
                            continue
                        chunks = list(range(t0 // 64, t1 // 64))
                        if X == 1:
                            chunks = chunks[::-1]
                        if kind == 2:
                            Sst, R_S = (S0[:, hd, :], R_S0) if X == 0 else (SinB[:, hd, :], R_SinB)
                        else:
                            Sst, R_S = sring.get()
                            S.op("dve", lambda e, Sst=Sst: e.memset(Sst, 0.0), writes=[R_S])
                        groups = [chunks[gi:gi + 4] for gi in range(0, len(chunks), 4)]

                        def pre_mm(grp):
                            pd, rpd = gen.get()
                            asc = sorted(grp)
                            for c in grp:
                                t, hf = c // 2, c % 2
                                i2 = asc.index(c)
                                MM(pd[:, i2 * 128:(i2 + 1) * 128], ktok[:, hf, t, :], itok[:, t, :], True, True,
                                   [R_ktok[X], R_it], rpd)
                            return (pd, rpd, asc)

                        def pre_td(pm, grp):
                            pd, rpd, asc = pm
                            n_ = len(asc)
                            td4, rtd = tdr.get()
                            TT("dve", td4[:, 0:n_, :], pd[:, 0:n_ * 128].rearrange("p (a b) -> p a b", a=n_),
                               csc[:, p.slot, 1, asc[0]:asc[0] + n_].unsqueeze(2).broadcast_to([128, n_, 128]), ALU.mult,
                               [rpd, R_csc[p.slot]], [rtd])
                            return [(td4[:, asc.index(c), :], rtd) for c in grp]

                        nxt_tds = pre_td(pre_mm(groups[0]), groups[0])
                        for gidx, grp in enumerate(groups):
                            tds = nxt_tds
                            pm_n = None
                            if gidx + 1 < len(groups):
                                pm_n = pre_mm(groups[gidx + 1])
                            for i2, c in enumerate(grp):
                                tb = c // 8
                                if full:
                                    sh2, rsh2 = shr.get()
                                    ACT(sh2, Sst, AF.Identity, [R_S, R_csc[p.slot]], [rsh2], scale=csc[:, p.slot, 2, c:c + 1])
                                    ob, rob = psb[4 + tb], R_ps[4 + tb]
                                    oc = slice((c % 8) * 64, (c % 8) * 64 + 64)
                                    MM(ob[:, oc], sh2, qt[:, 0, c * 64:(c + 1) * 64], False, False, [rsh2, R_qt[X]], rob, skip=True)
                                td, rtd = tds[i2]
                                Snx, R_Snx = sring.get()
                                STT(Snx, Sst, csc[:, p.slot, 0, c:c + 1], td, ALU.mult, ALU.add,
                                    [R_S, R_csc[p.slot], rtd], [R_Snx])
                                Sst, R_S = Snx, R_Snx
                            if pm_n is not None:
                                nxt_tds = pre_td(pm_n, groups[gidx + 1])
                            if filler is not None:
                                next(filler, None)
                        if kind == 2 and X == 0:
                            so, rso, sso = sor.get()
                            CP("dve", so, Sst, [R_S], [rso])
                            DMA("sp", cin2[hd * 128:(hd + 1) * 128, :], so, [rso], [R_dram["cin2"]], sso)
                        if kind != 2 and full:
                            so, rso, sso = sor.get()
                            CP("dve", so, Sst, [R_S], [rso])
                            DMA("sp", nS[:, kind, X, hd, :], so, [rso], [R_dram["nS"]], sso)
                    if filler is not None:
                        for _ in filler:
                            pass
                    if full and X == 0:
                        for tb in range(NTB):
                            CP("dve", on[:, hd, tbs(tb)], psb[4 + tb][:, :], [R_ps[4 + tb]], [R_on[hd][tb]])
                    if full and X == 1:
                        for tb in range(NTB):
                            ob, rob = psb[4 + tb], R_ps[4 + tb]
                            ont = on[:, hd, tbs(tb)]
                            TT("dve", ont, ob[:, :], ont, ALU.add, [rob, R_on[hd][tb]], [R_on[hd][tb]])
                            sq, rsq = b.sq.get()
                            ACT(sq, ont, AF.Square, [R_on[hd][tb]], [rsq])
                            pn, rpn = gen.get()
                            MM(pn[:, :], ones, sq, True, True, [rsq, R_cb], rpn)
                            ACT(b.rstd[:, tb, :], pn[:, :], AF.Ln, [rpn], [b.R_rstd[tb]], scale=1.0 / 128.0, bias=EPS)
                            ACT(b.rstd[:, tb, :], b.rstd[:, tb, :], AF.Exp, [b.R_rstd[tb]], [b.R_rstd[tb]], scale=-0.5)
                        for tb in range(NTB):
                            ont = on[:, hd, tbs(tb)]
                            STT(ont, ont, gn, b.rstd[:, tb, :], ALU.mult, ALU.mult, [R_on[hd][tb], R_cst, b.R_rstd[tb]], [R_on[hd][tb]])
                            TT("dve", ont, ont, sg[:, tbs(tb)], ALU.mult, [R_on[hd][tb], R_sg], [R_on[hd][tb]])
                    if X == 0 and hd == 7:
                        S.op("pool", lambda e: e.collective_compute(
                            "AllGather", ALU.bypass, replica_groups=RG,
                            ins=[cin2.ap().opt()], outs=[cout2.ap().opt()]),
                            reads=[R_dram["cin2"]], writes=[R_dram["cout2"]], dsem=sem_cc, inc=1)
                        c2v = cout2.ap().rearrange("(r h p) n -> p r h n", r=2, p=128)
                        DMA("sp", SinB[:, :, :], c2v[:, 0, :, :], [R_dram["cout2"]], [R_SinB], sem_in[6])
                        DMA("sp", Srv[:, :, :], c2v[:, 1, :, :], [R_dram["cout2"]], [R_Srv], sem_in[7])
                        TS("dve", SinB[:, :, :], SinB[:, :, :], cst[:, CST_M:CST_M + 1], None, ALU.mult, ALU.bypass, [R_cst, R_SinB], [R_SinB])
                        STT(SinB[:, :, :], Srv[:, :, :], cst[:, CST_M + 1:CST_M + 2], SinB[:, :, :], ALU.mult, ALU.add,
                            [R_Srv, R_cst, R_SinB], [R_SinB])

                passes = []
                for hd in range(8):
                    passes.append((hd, 0, True))
                for hd in range(8):
                    passes.append((hd, 1, True))
                plist = []
                prev = None
                for n_, (hd, X, full) in enumerate(passes):
                    p = P()
                    p.hd, p.X, p.full, p.slot = hd, X, full, n_ % 2
                    plist.append(p)
                    prev = p
                for n_, p in enumerate(plist):
                    if n_ == 0:
                        for _ in fp1(p):
                            pass
                        fp2(p)
                    fil = None
                    if n_ + 1 < len(plist):
                        fil = fp1(plist[n_ + 1])
                        next(fil, None)
                    back(p, fil)
                    if n_ + 1 < len(plist):
                        fp2(plist[n_ + 1])
                for s in range(2):
                    wv, wres = w_next(whout[s], (8, 512))
                    for ff in range(4):
                        fo = 4 * s + ff
                        for tb in range(NTB):
                            pa, rpa = gen.get()
                            for hd in range(8):
                                MM(pa[:, :], wv[:, hd, ff * 128:(ff + 1) * 128], on[:, hd, tbs(tb)], hd == 0, hd == 7,
                                   [wres, R_on[hd][tb]], rpa)
                            post_chunk(b, pa, rpa, fo, tb, hT, R_h)
                post_finish(b, l, j, hT, R_h, nxt)

            stages = ["none", "mod", "ffn00", "attn", "ffn01", "ffn10", "hgrn", "full"]
            lim = stages.index(stop) - 2
            def mk(l, s):
                return lambda: mod_slab(l, s)

            def fin(l, js):
                def f():
                    for j_ in js:
                        mod_fin(l, j_)
                return f
            if lim >= -1:
                for s_ in range(4):
                    mod_slab(0, s_)
                mod_fin(0, 0, "pre")
            if lim >= 0:
                prenorm(NBP, 0, 0)
                ffn(0, 0, 0, nxt=(0, 1) if lim >= 1 else None,
                    hooks=[mk(0, 4), mk(0, 5), lambda: mod_fin(0, 0, "post")] + [mk(0, s_) for s_ in range(6, 18)] + [fin(0, [1, 2])])
            if lim >= 1:
                attention(nxt=(0, 2) if lim >= 2 else None)
            if lim >= 2:
                ffn(0, 1, 2, nxt=(1, 0) if lim >= 3 else None,
                    hooks=[mk(1, s_) for s_ in range(18)] + [fin(1, [0, 1, 2])])
            if lim >= 3:
                ffn(1, 0, 0, nxt=(1, 1) if lim >= 4 else None)
            if lim >= 4:
                hgrn(nxt=(1, 2) if lim >= 5 else None)
            if lim >= 5:
                ffn(1, 1, 2)
            for tb in range(NTB):
                DMA("sp", yT[:, :, tbs(tb)], xT[:, :, tbs(tb)], [R_x[fo][tb] for fo in range(8)], [R_dram["yT"]], outsem.get())

        S.dry = True
        body()
        S.dry = False
        ar.off = 0
        ar.cur = []
        ar.prev = []
        body()
        with nc.Block() as block:
            S.emit(block, final_waits=sem_out + sem_misc)
    return nc


def _fm(v):
    v = np.asarray(v)
    n = v.shape[-1] // 128
    w = v.reshape(v.shape[:-1] + (n, 128))
    return np.moveaxis(w, -1, 0)


def _slab(wmat, cols):
    sub = wmat[:, cols]
    return np.ascontiguousarray(sub.reshape(8, 128, -1).transpose(1, 0, 2))


_NC_CACHE = {}


def _get_nc(stop):
    if stop not in _NC_CACHE:
        _NC_CACHE[stop] = build_program(stop)
    return _NC_CACHE[stop]


def kernel(x_prompt, x_sample, cache_k, cache_v, state_hgrn, c, c_ctx, w_mod, b_mod, norm_g,
           ffn_w_in, ffn_w_out, attn_w_in, attn_w_out, attn_lambda, attn_subln,
           hgrn_w_in, hgrn_w_out, hgrn_lower_bounds, hgrn_gnorm, _stop="full"):
    f32 = np.float32
    A = lambda a: np.asarray(a, dtype=f32)
    x_prompt, x_sample, cache_k, cache_v, state_hgrn = A(x_prompt), A(x_sample), A(cache_k), A(cache_v), A(state_hgrn)
    c, c_ctx, w_mod, b_mod, norm_g = A(c), A(c_ctx), A(w_mod), A(b_mod), A(norm_g)
    ffn_w_in, ffn_w_out, attn_w_in, attn_w_out = A(ffn_w_in), A(ffn_w_out), A(attn_w_in), A(attn_w_out)
    attn_lambda, attn_subln, hgrn_w_in, hgrn_w_out = A(attn_lambda), A(attn_subln), A(hgrn_w_in), A(hgrn_w_out)
    hgrn_lower_bounds, hgrn_gnorm = A(hgrn_lower_bounds), A(hgrn_gnorm)

    wmod_h = np.empty((2, 18, 128, 8, 512), f32)
    for l in range(2):
        for s in range(18):
            wmod_h[l, s] = _slab(w_mod[l], np.arange(s * 512, (s + 1) * 512))
    wfin_h = np.empty((2, 2, 11, 128, 8, 512), f32)
    wfout_h = np.empty((2, 2, 8, 128, 22, 128), f32)
    for l in range(2):
        for i in range(2):
            for s in range(11):
                cols = np.concatenate([np.arange(2 * s * 128, (2 * s + 2) * 128), FH + np.arange(2 * s * 128, (2 * s + 2) * 128)])
                wfin_h[l, i, s] = _slab(ffn_w_in[l, i], cols)
            wo = ffn_w_out[l, i].reshape(22, 128, 8, 128)
            wfout_h[l, i] = wo.transpose(2, 1, 0, 3)
    idx = np.arange(1024).reshape(8, 2, 2, 2, 16)
    swp = idx[:, :, :, ::-1, :].reshape(-1)
    wain_h = np.empty((10, 128, 8, 512), f32)
    for which in range(2):
        for s in range(2):
            cols = which * 1024 + np.arange(s * 512, (s + 1) * 512)
            wain_h[which * 4 + s] = _slab(attn_w_in[0], cols)
            cols2 = which * 1024 + swp[s * 512:(s + 1) * 512]
            wain_h[which * 4 + 2 + s] = _slab(attn_w_in[0], cols2)
    for s in range(2):
        wain_h[8 + s] = _slab(attn_w_in[0], 2048 + np.arange(s * 512, (s + 1) * 512))
    waout_h = np.stack([_slab(attn_w_out[0], np.arange(s * 512, (s + 1) * 512)) for s in range(2)])
    whout_h = np.stack([_slab(hgrn_w_out[0], np.arange(s * 512, (s + 1) * 512)) for s in range(2)])
    whi_h = np.stack([_slab(hgrn_w_in[0], 3072 + np.arange(hd * 128, (hd + 1) * 128)) for hd in range(8)])
    whin_par = []
    for par in range(2):
        arr = np.empty((8, 128, 8, 512), f32)
        for hd in range(8):
            r = np.arange(hd * 128, (hd + 1) * 128)
            gA, gB = (1024, 2048) if par == 0 else (2048, 1024)
            arr[hd] = _slab(hgrn_w_in[0], np.concatenate([r, gA + r, gB + r, 4096 + r]))
        whin_par.append(arr)
    cbh = np.zeros((128, CB_N), f32)
    cbh[:, CB_ID:CB_ID + 128] = np.eye(128, dtype=f32)
    cbh[:, CB_ONES:CB_ONES + 128] = 1.0
    pp = np.arange(128)[:, None]
    tt = np.arange(128)[None, :]
    same = (pp // 64) == (tt // 64)
    cbh[:, CB_MA:CB_MA + 128] = same & ((pp % 64) <= (tt % 64))
    cbh[:, CB_MB:CB_MB + 128] = same & ((pp % 64) >= (tt % 64))
    nf = 16
    inv = (np.float32(10000.0) ** (-np.arange(nf, dtype=f32) / np.float32(nf))).astype(f32)

    in_maps = []
    for core in range(8):
        pr, par = core // 2, core % 2
        segs = [x_prompt[2 * core], x_prompt[2 * core + 1], x_sample[pr, par * 1024:(par + 1) * 1024]]
        if par:
            segs = [s_[::-1] for s_ in segs]
        X = np.concatenate(segs, 0)
        xT_h = np.ascontiguousarray(X.T.reshape(8, 128, T).transpose(1, 0, 2))
        cst_h = np.zeros((128, CST_N), f32)
        cst_h[:, CST_LAM:CST_LAM + 256] = attn_lambda[0].reshape(1, 256)
        lbm = hgrn_lower_bounds[:, ::-1, :] if par else hgrn_lower_bounds
        cst_h[:, CST_LB:CST_LB + 32] = _fm(lbm).reshape(128, 32)
        cst_h[:, CST_BMOD:CST_BMOD + 144] = _fm(b_mod).reshape(128, 144)
        cst_h[:, CST_NG:CST_NG + 96] = _fm(norm_g).reshape(128, 96)
        cst_h[:, CST_GN] = hgrn_gnorm[0]
        cst_h[:, CST_SUB] = attn_subln[0]
        cst_h[:, CST_C:CST_C + 16] = np.stack([_fm(c_ctx), _fm(c[pr])], -1).reshape(128, 16)
        cst_h[:, CST_M] = 1.0 if par else 0.0
        cst_h[:, CST_M + 1] = 0.0 if par else 1.0
        cst_h[:64, CST_RM] = 1.0
        cst_h[64:, CST_RM + 1] = 1.0
        pos = par * 1024 + np.arange(1024)
        if par:
            pos = pos[::-1]
        row = (pos // 64).astype(f32)
        col = (pos % 64).astype(f32)
        p = np.arange(128) % 64
        axis, half, fr = p // 32, (p % 32) // 16, p % 16
        ang = np.where(axis[:, None] == 0, row[None, :], col[None, :]).astype(f32) * inv[fr][:, None]
        rope_h = np.stack([np.cos(ang), np.where(half[:, None] == 0, -np.sin(ang), np.sin(ang))], 1).astype(f32)
        ck = cache_k[pr, 0]
        ck_h = np.ascontiguousarray(ck.transpose(2, 3, 1, 0).reshape(128, 8, 256))
        cv_h = np.ascontiguousarray(cache_v[pr, 0].reshape(2, 128, 1024).transpose(1, 0, 2))
        s0_h = np.ascontiguousarray(state_hgrn[pr, 0, par].transpose(1, 0, 2))
        in_maps.append(dict(xT_in=xT_h, cst=cst_h, cb=cbh, rope=np.ascontiguousarray(rope_h), ckT=ck_h, cv=cv_h, s0=s0_h,
                            wmod=wmod_h, wfin=wfin_h, wfout=wfout_h, wain=wain_h, waout=waout_h,
                            whin=whin_par[par], whi=whi_h, whout=whout_h))
    nc = _get_nc(_stop)
    ncores = int(os.environ.get("K_CORES", "8"))
    res = run_bass_kernel_spmd(nc, in_maps[:ncores], core_ids=list(range(ncores)))
    return _assemble(res.results, ncores)


def _assemble(results, ncores=8):
    f32 = np.float32

    y_prompt = np.zeros((16, 256, 1024), f32)
    y_sample = np.zeros((4, 2048, 1024), f32)
    nck = np.zeros((16, 1, 256, 8, 2, 64), f32)
    ncv = np.zeros((16, 1, 256, 8, 128), f32)
    nst = np.zeros((16, 1, 2, 8, 128, 128), f32)
    for core in range(ncores):
        r = results[core]
        pr, par = core // 2, core % 2
        Y = np.asarray(r["yT"]).reshape(128, 8, T).transpose(2, 1, 0).reshape(T, 1024)
        K = np.asarray(r["nkT"]).reshape(128, 8, 512).transpose(2, 1, 0).reshape(512, 8, 2, 64)
        V = np.asarray(r["nv"]).reshape(128, 4, 1024).transpose(1, 0, 2).reshape(512, 8, 128)
        St = np.asarray(r["nS"]).reshape(128, 2, 2, 8, 128)
        for sq in range(2):
            ys, ks, vs = Y[sq * 256:(sq + 1) * 256], K[sq * 256:(sq + 1) * 256], V[sq * 256:(sq + 1) * 256]
            if par:
                ys, ks, vs = ys[::-1], ks[::-1], vs[::-1]
            y_prompt[2 * core + sq] = ys
            nck[2 * core + sq, 0] = ks
            ncv[2 * core + sq, 0] = vs
            for X in range(2):
                d = X if par == 0 else 1 - X
                nst[2 * core + sq, 0, d] = St[:, sq, X].transpose(1, 0, 2)
        ysm = Y[512:]
        if par:
            ysm = ysm[::-1]
        y_sample[pr, par * 1024:(par + 1) * 1024] = ysm
    return (y_prompt, y_sample, nck, ncv, nst)
```

```python
from contextlib import ExitStack
import math
import os
import numpy as np
import concourse.bass as bass
import concourse.mybir as mybir
from concourse.bass_utils import run_bass_kernel_spmd

F32 = mybir.dt.float32
BF16 = mybir.dt.bfloat16
AF = mybir.ActivationFunctionType
ALU = mybir.AluOpType

SAME_ENGINE_SYNC = True
D = 1024
T = 1536
NTB = 3
FH = 2816
NJ = 22
EPS = 1e-6
WSLOT = 4096
WRING = 3
ARENA_BYTES = 103 * 1024 - 15360 + 512


class Res:
    __slots__ = ("name", "w", "r", "excl")

    def __init__(self, name="", excl=False):
        self.name = name
        self.w = None
        self.r = {}
        self.excl = excl


class DmaSem:
    def __init__(self, h):
        self.h = h
        self.count = 0


def _merge(dst, tok, key):
    old = dst.get(key)
    if old is None or old[2] < tok[2]:
        dst[key] = tok


def _tok_key(tok):
    return tok[1] if tok[0] == "eng" else ("dma", id(tok[1]))


class Sched:
    ENGS = ("pe", "act", "dve", "pool", "sp")

    def __init__(self, nc, stack):
        self.nc = nc
        self.stack = stack
        self.dry = False
        self.q = {e: [] for e in self.ENGS}
        self.esem = {}
        for e in ("pe", "act", "dve", "pool"):
            self.esem[e] = stack.enter_context(nc.semaphore("es_" + e))
        self.n_dsem = 0

    def dma_sem(self, name=None):
        self.n_dsem += 1
        h = self.stack.enter_context(self.nc.semaphore(name or ("ds%d" % self.n_dsem)))
        return DmaSem(h)

    def op(self, e, fn, reads=(), writes=(), dsem=None, inc=16):
        if self.dry:
            return None
        deps = set()
        for r in reads:
            if r.w is not None:
                deps.add(r.w)
            if r.excl:
                deps.update(r.r.values())
        for w in writes:
            if w.w is not None:
                deps.add(w.w)
            deps.update(w.r.values())
        idx = len(self.q[e])
        if dsem is not None:
            dsem.count += inc
            tok = ("dma", dsem, dsem.count)
        else:
            tok = ("eng", e, idx)
        self.q[e].append(dict(fn=fn, deps=deps, tok=tok, dsem=dsem, inc=inc))
        k = _tok_key(tok)
        for r in reads:
            r.r[k] = tok
        for w in writes:
            w.w = tok
            w.r = {}
        return tok

    def _skip(self, e, d):
        if d[0] != "eng" or d[1] != e:
            return False
        if e == "pe":
            return True
        return not SAME_ENGINE_SYNC

    def emit(self, block, final_waits=()):
        needed = {e: set() for e in self.ENGS}
        for e in self.ENGS:
            for rec in self.q[e]:
                for d in rec["deps"]:
                    if d[0] == "eng" and not self._skip(e, d):
                        needed[d[1]].add(d[2])
        val = {}
        for e in self.ENGS:
            c = 0
            v = {}
            for i in range(len(self.q[e])):
                if i in needed[e]:
                    c += 1
                    v[i] = c
            val[e] = v

        def body(e):
            def f(eng):
                waited = {}
                for rec in self.q[e]:
                    ws = {}
                    for d in rec["deps"]:
                        if self._skip(e, d):
                            continue
                        if d[0] == "eng":
                            key = ("eng", d[1])
                            sem = self.esem[d[1]]
                            v = val[d[1]][d[2]]
                        else:
                            key = ("dma", id(d[1]))
                            sem = d[1].h
                            v = d[2]
                        if ws.get(key, (None, 0))[1] < v:
                            ws[key] = (sem, v)
                    for key, (sem, v) in ws.items():
                        if waited.get(key, 0) >= v:
                            continue
                        eng.wait_ge(sem, v)
                        waited[key] = v
                    ins = rec["fn"](eng)
                    if rec["dsem"] is not None:
                        ins.then_inc(rec["dsem"].h, rec["inc"])
                    elif rec["tok"][2] in val[e]:
                        ins.then_inc(self.esem[e], 1)
                if e == "sp":
                    for ds in final_waits:
                        if ds.count > 0:
                            eng.wait_ge(ds.h, ds.count)
            return f

        block.tensor(body("pe"))
        block.scalar(body("act"))
        block.vector(body("dve"))
        block.gpsimd(body("pool"))
        block.sync(body("sp"))


class Arena:
    def __init__(self, nc, nbytes):
        self.t = nc.alloc_sbuf_tensor("arena", [128, nbytes // 2], BF16)
        self.nbytes = nbytes
        self.off = 0
        self.cur = []
        self.prev = []
        self.released = []

    def reset(self):
        self.prev = self.cur + self.released
        self.cur = []
        self.released = []
        self.off = 0

    def push(self):
        return (self.off, len(self.cur))

    def pop(self, mark):
        rel = self.cur[mark[1]:]
        self.cur = self.cur[:mark[1]]
        self.released = self.released + rel
        self.prev = self.prev + rel
        self.off = mark[0]

    def view(self, off, shape, dt):
        save = self.off
        self.off = off
        v = self.alloc(shape, dt)
        self.off = save
        return v

    def res_from(self, olds):
        r = Res()
        for o in olds:
            if o.w is not None:
                _merge(r.r, o.w, _tok_key(o.w))
            for k, tk in o.r.items():
                _merge(r.r, tk, k)
        self.cur.append(r)
        return r

    def _newres(self, name=""):
        r = Res(name)
        for o in self.prev:
            if o.w is not None:
                _merge(r.r, o.w, _tok_key(o.w))
            for k, tk in o.r.items():
                _merge(r.r, tk, k)
        self.cur.append(r)
        return r

    def alloc(self, shape, dt, name=""):
        n = 1
        for s in shape:
            n *= s
        nb = n * (4 if dt == F32 else 2)
        nb = (nb + 31) // 32 * 32
        assert self.off + nb <= self.nbytes, ("arena overflow", name, self.off, nb)
        v = self.t[:, self.off // 2:(self.off + nb) // 2]
        self.off += nb
        if dt == F32:
            v = v.bitcast(F32)
        v = v[:, 0:n]
        if len(shape) == 2:
            v = v.rearrange("p (a b) -> p a b", a=shape[0])
        elif len(shape) == 3:
            v = v.rearrange("p (a b c) -> p a b c", a=shape[0], b=shape[1])
        return v

    def res(self, name=""):
        return self._newres(name)

    def resgrid(self, *dims):
        if len(dims) == 1:
            return [self._newres() for _ in range(dims[0])]
        return [self.resgrid(*dims[1:]) for _ in range(dims[0])]


class Ring:
    def __init__(self, items):
        self.items = items
        self.i = 0

    def get(self):
        it = self.items[self.i % len(self.items)]
        self.i += 1
        return it


CST_LAM = 0
CST_LB = 256
CST_BMOD = 288
CST_NG = 432
CST_GN = 528
CST_SUB = 529
CST_C = 530
CST_M = 546
CST_RM = 548
CST_N = 550
CB_ID = 0
CB_ONES = 128
CB_MA = 256
CB_MB = 384
CB_N = 512


def build_program(stop="full"):
    nc = bass.Bass("TRN2", target_bir_lowering=False)
    RG = [[2 * i_, 2 * i_ + 1] for i_ in range(max(1, int(os.environ.get("K_CORES", "8")) // 2))]

    def din(name, shape, dt=F32):
        return nc.dram_tensor(name, list(shape), dt, kind="ExternalInput").ap()

    def dout(name, shape, dt=F32):
        return nc.dram_tensor(name, list(shape), dt, kind="ExternalOutput").ap()

    xT_in = din("xT_in", [128, 8, T])
    cst_in = din("cst", [128, CST_N])
    cb_in = din("cb", [128, CB_N])
    rope_in = din("rope", [128, 2, 1024])
    ck_in = din("ckT", [128, 8, 256])
    cv_in = din("cv", [128, 2, 1024])
    s0_in = din("s0", [128, 8, 128])
    wmod = din("wmod", [2, 18, 128, 8, 512])
    wfin = din("wfin", [2, 2, 11, 128, 8, 512])
    wfout = din("wfout", [2, 2, 8, 128, 22, 128])
    wain = din("wain", [10, 128, 8, 512])
    waout = din("waout", [2, 128, 8, 512])
    whin = din("whin", [8, 128, 8, 512])
    whi = din("whi", [8, 128, 8, 128])
    whout = din("whout", [2, 128, 8, 512])
    yT = dout("yT", [128, 8, T])
    nkT = dout("nkT", [128, 8, 512])
    nv = dout("nv", [128, 4, 1024])
    nS = dout("nS", [128, 2, 2, 8, 128])
    cinK = nc.dram_tensor("cinK", [1024, 1024], BF16)
    coutK = nc.dram_tensor("coutK", [2048, 1024], BF16)
    cinV = nc.dram_tensor("cinV", [1024, 1024], BF16)
    coutV = nc.dram_tensor("coutV", [2048, 1024], BF16)
    cin2 = nc.dram_tensor("cin2", [1024, 128], F32)
    cout2 = nc.dram_tensor("cout2", [2048, 128], F32)

    with ExitStack() as st:
        S = Sched(nc, st)
        xT = nc.alloc_sbuf_tensor("xT", [128, 8, T], F32)
        hT = nc.alloc_sbuf_tensor("hT", [128, 8, T], BF16)
        wr = [nc.alloc_sbuf_tensor("wr%d" % i, [128, WSLOT], BF16) for i in range(WRING)]
        cst = nc.alloc_sbuf_tensor("cst_sb", [128, CST_N], F32)
        cb = nc.alloc_sbuf_tensor("cb_sb", [128, CB_N], BF16)
        onesrow = nc.alloc_sbuf_tensor("onesrow", [128, T], BF16)
        scT = nc.alloc_sbuf_tensor("scT", [128, 8, 2], BF16)
        modv = nc.alloc_sbuf_tensor("modv", [128, 2, 72, 2], F32)
        gsT = nc.alloc_sbuf_tensor("gsT", [128, 2, 3, 2, 8], F32)
        ggT = nc.alloc_sbuf_tensor("ggT", [128, 2, 3, 2, 8], F32)
        sml = nc.alloc_sbuf_tensor("sml", [128, 64], F32)
        nb_rstd = nc.alloc_sbuf_tensor("nb_rstd", [128, NTB, 512], F32)
        nb_rt = nc.alloc_sbuf_tensor("nb_rt", [128, 512], F32)
        nb_tmp = nc.alloc_sbuf_tensor("nb_tmp", [128, 2, 512], F32)
        nb_sq = nc.alloc_sbuf_tensor("nb_sq", [128, 3, 512], BF16)
        psall = nc.alloc_psum_tensor("psall", [128, 4096], F32)
        psb = [psall[:, i * 512:(i + 1) * 512] for i in range(8)]
        ar = Arena(nc, ARENA_BYTES)

        ident = cb[:, CB_ID:CB_ID + 128]
        ones = cb[:, CB_ONES:CB_ONES + 128]
        maskA = cb[:, CB_MA:CB_MA + 128]
        maskB = cb[:, CB_MB:CB_MB + 128]

        sem_w = [S.dma_sem("w%d" % i) for i in range(WRING)]
        sem_in = [S.dma_sem("in%d" % i) for i in range(8)]
        sem_out = [S.dma_sem("out%d" % i) for i in range(6)]
        sem_cc = S.dma_sem("cc")
        sem_misc = [S.dma_sem("mi%d" % i) for i in range(8)]
        sem_kv = [S.dma_sem("kv%d" % i) for i in range(4)]

        class WS:
            pass
        W = WS()
        W.plan = []
        W.n = 0
        W.issued = 0
        W.res = [Res("w%d" % i) for i in range(WRING)]

        def w_issue(m):
            src, shp = W.plan[m]
            slot = m % WRING
            n = shp[0] * shp[1]
            dst = wr[slot][:, 0:n].rearrange("p (a b) -> p a b", a=shp[0])
            S.op("pool", lambda e, dst=dst, src=src: e.dma_start(out=dst, in_=src, max_dma_last_dim=2048),
                 writes=[W.res[slot]], dsem=sem_w[slot])

        def w_next(src, shp, keep=0):
            if S.dry:
                W.plan.append((src, shp))
                return wr[0][:, 0:shp[0] * shp[1]].rearrange("p (a b) -> p a b", a=shp[0]), W.res[0]
            n = W.n
            W.n += 1
            while W.issued < min(len(W.plan), n + WRING - keep):
                w_issue(W.issued)
                W.issued += 1
            slot = n % WRING
            return wr[slot][:, 0:shp[0] * shp[1]].rearrange("p (a b) -> p a b", a=shp[0]), W.res[slot]

        def MM(out, lhsT, rhs, start, stop, reads, wres, skip=False):
            S.op("pe", lambda e: e.matmul(out, lhsT=lhsT, rhs=rhs, start=start, stop=stop, skip_group_check=skip),
                 reads=reads, writes=[wres])

        def TR(out, in_, reads, wres):
            S.op("pe", lambda e: e.transpose(out, in_, ident), reads=reads, writes=[wres])

        def ACT(out, in_, func, reads, writes, scale=1.0, bias=0.0):
            S.op("act", lambda e: e.activation(out=out, in_=in_, func=func, bias=bias, scale=scale),
                 reads=reads, writes=writes)

        def TT(eng, out, in0, in1, op, reads, writes):
            S.op(eng, lambda e: e.tensor_tensor(out=out, in0=in0, in1=in1, op=op), reads=reads, writes=writes)

        def TS(eng, out, in0, s1, s2, op0, op1, reads, writes):
            S.op(eng, lambda e: e.tensor_scalar(out=out, in0=in0, scalar1=s1, scalar2=s2, op0=op0, op1=op1),
                 reads=reads, writes=writes)

        def STT(out, in0, scalar, in1, op0, op1, reads, writes):
            S.op("dve", lambda e: e.scalar_tensor_tensor(out=out, in0=in0, scalar=scalar, in1=in1, op0=op0, op1=op1),
                 reads=reads, writes=writes)

        def CP(eng, out, in_, reads, writes):
            if eng == "act":
                S.op("act", lambda e: e.activation(out=out, in_=in_, func=AF.Copy), reads=reads, writes=writes)
            else:
                S.op(eng, lambda e: e.tensor_copy(out=out, in_=in_), reads=reads, writes=writes)

        def RCP(out, in_, reads, writes):
            S.op("dve", lambda e: e.reciprocal(out=out, in_=in_), reads=reads, writes=writes)

        def DMA(q, out, in_, reads, writes, dsem):
            S.op(q, lambda e: e.dma_start(out=out, in_=in_), reads=reads, writes=writes, dsem=dsem)

        def tbs(tb):
            return slice(tb * 512, (tb + 1) * 512)

        R_x = [[Res() for _ in range(NTB)] for _ in range(8)]
        R_h = [[Res() for _ in range(NTB)] for _ in range(8)]
        R_ps = [Res("ps%d" % i, excl=True) for i in range(8)]
        R_cst = Res("cst")
        R_cb = Res("cb")
        R_mod = Res("mod")
        R_sml = Res("sml")
        R_sc = Res("sc")
        R_dram = {k: Res(k) for k in ["cinK", "coutK", "cinV", "coutV", "cin2", "cout2", "yT", "nkT", "nv", "nS"]}

        def body():
            gen = Ring([(psb[i], R_ps[i]) for i in range(4)])
            outsem = Ring(sem_out)
            DMA("sp", cst[:, :], cst_in[:, :], [], [R_cst], sem_in[0])
            S.op("pool", lambda e: e.dma_start(out=cb[:, :], in_=cb_in[:, :]), writes=[R_cb], dsem=sem_in[1])
            for tb in range(NTB):
                DMA("sp", xT[:, :, tbs(tb)], xT_in[:, :, tbs(tb)], [], [R_x[fo][tb] for fo in range(8)], sem_in[2 + tb])
            S.op("dve", lambda e: e.memset(onesrow[:, :], 1.0), writes=[R_cb])
            ACT(scT[:, :, :], cst[:, CST_C:CST_C + 16].rearrange("p (k v) -> p k v", k=8), AF.Silu, [R_cst], [R_sc])

            def mod_slab(l, s):
                pm, rpm = psb[7], R_ps[7]
                wv, wres = w_next(wmod[l, s], (8, 512))
                for cc in range(4):
                    ch = s * 4 + cc
                    for k in range(8):
                        MM(pm[:, ch * 2:ch * 2 + 2], wv[:, k, cc * 128:(cc + 1) * 128], scT[:, k, :],
                           k == 0, k == 7, [wres, R_sc], rpm)

            def mod_fin(l, j, part="both"):
                pm, rpm = psb[7], R_ps[7]
                c0, c1 = {"both": (0, 24), "pre": (0, 16), "post": (16, 24)}[part]
                bm = cst[:, CST_BMOD + l * 72 + 24 * j + c0:CST_BMOD + l * 72 + 24 * j + c1]
                for v in range(2):
                    TT("dve", modv[:, l, 24 * j + c0:24 * j + c1, v], pm[:, 48 * j + 2 * c0 + v:48 * j + 2 * c1:2], bm, ALU.add,
                       [rpm, R_cst], [R_mod])
                if True:
                    for v in range(2):
                        sc_ = modv[:, l, (3 * j + 1) * 8:(3 * j + 2) * 8, v]
                        gt_ = modv[:, l, (3 * j + 2) * 8:(3 * j + 3) * 8, v]
                        gpre = cst[:, CST_NG + (l * 6 + 2 * j) * 8:CST_NG + (l * 6 + 2 * j + 1) * 8]
                        gpost = cst[:, CST_NG + (l * 6 + 2 * j + 1) * 8:CST_NG + (l * 6 + 2 * j + 2) * 8]
                        rw = 1.0 if j == 1 else 0.5
                        if part in ("both", "pre"):
                            STT(gsT[:, l, j, v, :], sc_, 1.0, gpre, ALU.add, ALU.mult, [R_mod, R_cst], [R_mod])
                        if part in ("both", "post"):
                            STT(ggT[:, l, j, v, :], gt_, rw, gpost, ALU.mult, ALU.mult, [R_mod, R_cst], [R_mod])

            class NB:
                pass

            NBP = NB()
            NBP.rstd = nb_rstd
            NBP.R_rstd = [Res() for _ in range(NTB)]
            NBP.rt = nb_rt
            NBP.R_rt = Res()
            NBP.tmp = Ring([(nb_tmp[:, i, :], Res()) for i in range(2)])
            NBP.sq = Ring([(nb_sq[:, i, :], Res()) for i in range(3)])

            def norm_bufs():
                return NBP

            def rstd_from(b, tb, ssb, rss, n_feat):
                ACT(b.rt[:, :], ssb[:, :], AF.Ln, [rss], [b.R_rt], scale=1.0 / n_feat, bias=EPS)
                ACT(b.rstd[:, tb, :], b.rt[:, :], AF.Exp, [b.R_rt], [b.R_rstd[tb]], scale=-0.5)

            def prenorm(b, l, j):
                for tb in range(NTB):
                    prenorm_tb(b, l, j, tb)

            def prenorm_tb(b, l, j, tb):
                if True:
                    v = 0 if tb == 0 else 1
                    ssb, rss = psb[4 + tb], R_ps[4 + tb]
                    for fo in range(8):
                        sq, rsq = b.sq.get()
                        ACT(sq, xT[:, fo, tbs(tb)], AF.Square, [R_x[fo][tb]], [rsq])
                        MM(ssb[:, :], ones, sq, fo == 0, fo == 7, [rsq, R_cb], rss)
                    rstd_from(b, tb, ssb, rss, 1024.0)
                    for fo in range(8):
                        tm, rtm = b.tmp.get()
                        TT("dve" if fo % 2 == 0 else "pool", tm, xT[:, fo, tbs(tb)], b.rstd[:, tb, :], ALU.mult,
                           [R_x[fo][tb], b.R_rstd[tb]], [rtm])
                        ACT(hT[:, fo, tbs(tb)], tm, AF.Identity, [rtm, R_mod], [R_h[fo][tb]],
                            scale=gsT[:, l, j, v, fo:fo + 1], bias=modv[:, l, 3 * j * 8 + fo, v:v + 1])

            def post_chunk(b, bank, rbank, fo, tb, yb, R_yb):
                PC = int(os.environ.get("K_PC", "7"))
                sq, rsq = b.sq.get()
                if PC & 1:
                    ACT(sq, bank[:, :], AF.Square, [rbank], [rsq])
                if PC & 2:
                    CP("dve", yb[:, fo, tbs(tb)], bank[:, :], [rbank], [R_yb[fo][tb]])
                if PC & 4:
                    MM(psb[4 + tb][:, :], ones, sq, fo == 0, fo == 7, [rsq, R_cb], R_ps[4 + tb])

            def post_finish(b, l, j, yb, R_yb, nxt=None):
                for tb in range(NTB):
                    v = 0 if tb == 0 else 1
                    rstd_from(b, tb, psb[4 + tb], R_ps[4 + tb], 1024.0)
                    for fo in range(8):
                        tm, rtm = b.tmp.get()
                        TT("pool", tm, yb[:, fo, tbs(tb)], b.rstd[:, tb, :], ALU.mult, [R_yb[fo][tb], b.R_rstd[tb]], [rtm])
                        STT(xT[:, fo, tbs(tb)], tm, ggT[:, l, j, v, fo:fo + 1], xT[:, fo, tbs(tb)], ALU.mult, ALU.add,
                            [rtm, R_mod, R_x[fo][tb]], [R_x[fo][tb]])
                    if tb > 0 and nxt is not None:
                        prenorm_tb(b, nxt[0], nxt[1], tb - 1)
                if nxt is not None:
                    prenorm_tb(b, nxt[0], nxt[1], NTB - 1)

            def ffn(l, i, j, nxt=None, hooks=()):
                hooks = list(hooks)
                ar.reset()
                b = norm_bufs()
                uT = ar.alloc([NJ, T], BF16, "uT")
                R_u = ar.resgrid(NJ, NTB)
                sa_t = ar.alloc([2, 512], F32, "sa")
                sa = Ring([(sa_t[:, i2, :], ar.res()) for i2 in range(2)])
                DBG = int(os.environ.get("K_DBG", "9"))
                for s in range(11 if DBG >= 2 else 0):
                    wv, wres = w_next(wfin[l, i, s], (8, 512))
                    for jj in range(2):
                        hc = 2 * s + jj
                        for tb in range(NTB):
                            pa, rpa = gen.get()
                            pb, rpb = gen.get()
                            for k in range(8):
                                MM(pa[:, :], wv[:, k, jj * 128:(jj + 1) * 128], hT[:, k, tbs(tb)], k == 0, k == 7,
                                   [wres, R_h[k][tb]], rpa)
                            for k in range(8):
                                MM(pb[:, :], wv[:, k, 256 + jj * 128:256 + (jj + 1) * 128], hT[:, k, tbs(tb)], k == 0, k == 7,
                                   [wres, R_h[k][tb]], rpb)
                            s_, rs_ = sa.get()
                            ACT(s_, pa[:, :], AF.Silu, [rpa], [rs_])
                            TT("dve", uT[:, hc, tbs(tb)], s_, pb[:, :], ALU.mult, [rs_, rpb], [R_u[hc][tb]])
                    if hooks:
                        hooks.pop(0)()
                for fo in range(8 if DBG >= 3 else 0):
                    wv, wres = w_next(wfout[l, i, fo], (22, 128))
                    for tb in range(NTB):
                        pa, rpa = gen.get()
                        for hc in range(NJ):
                            MM(pa[:, :], wv[:, hc, :], uT[:, hc, tbs(tb)], hc == 0, hc == NJ - 1,
                               [wres, R_u[hc][tb]], rpa)
                        if DBG >= 4:
                            post_chunk(b, pa, rpa, fo, tb, hT, R_h)
                    if hooks:
                        hooks.pop(0)()
                while hooks:
                    hooks.pop(0)()
                if DBG >= 5:
                    post_finish(b, l, j, hT, R_h, nxt)

            def attention(nxt=None):
                l, j = 0, 1
                lam_init = 0.8 - 0.6 * math.exp(-0.3 * l)
                ar.reset()
                qTp = ar.alloc([8, 512], BF16, "qTp"); R_qp = ar.resgrid(8)
                kTp = ar.alloc([8, 512], BF16, "kTp"); R_kp = ar.resgrid(8)
                Vp = ar.alloc([4, 1024], BF16, "Vp"); R_vp = ar.resgrid(4, 2)
                b = norm_bufs()
                qTs = ar.alloc([8, 1024], BF16, "qTs"); R_qs = ar.resgrid(8, 2)
                ckT = ar.alloc([2, 8, 256], BF16, "ckT"); R_ck = ar.res()
                cvb = ar.alloc([2, 1024], BF16, "cvb"); R_cv = ar.res()
                mark = ar.push()
                rope = ar.alloc([2, 1024], F32, "rope"); R_rope = ar.res()
                r1_t = ar.alloc([2, 512], F32, "r1")
                r2_t = ar.alloc([2, 512], F32, "r2")
                r1 = Ring([(r1_t[:, i, :], ar.res()) for i in range(2)])
                r2 = Ring([(r2_t[:, i, :], ar.res()) for i in range(2)])
                kst_t = ar.alloc([2, 512], BF16, "kst")
                kst = Ring([(kst_t[:, i, :], ar.res(), sem_misc[i]) for i in range(2)])
                vst_t = ar.alloc([2, 512], BF16, "vst")
                vst = Ring([(vst_t[:, i, :], ar.res(), sem_misc[2 + i]) for i in range(2)])
                fo_t = ar.alloc([2, 512], F32, "fout")
                fst = Ring([(fo_t[:, i, :], ar.res(), sem_misc[4 + i]) for i in range(2)])
                tq_t = ar.alloc([64], F32, "lamt")
                R_tq = ar.res()
                cosT = rope[:, 0, :]
                sinT = rope[:, 1, :]

                DMA("sp", rope[:, :, :], rope_in[:, :, :], [], [R_rope], sem_in[7])
                S.op("dve", lambda e: e.memset(ckT[:, :, :, :], 0.0), writes=[R_ck])
                for c_ in range(2):
                    S.op("pool", lambda e, c_=c_: e.dma_start(out=ckT[c_ * 64:(c_ + 1) * 64, c_, :, :],
                                                              in_=ck_in[c_ * 64:(c_ + 1) * 64, :, :], max_dma_last_dim=1024),
                         writes=[R_ck], dsem=sem_misc[6])
                S.op("pool", lambda e: e.dma_start(out=cvb[:, :, :], in_=cv_in[:, :, :], max_dma_last_dim=2048),
                     writes=[R_cv], dsem=sem_misc[7])
                lamt = tq_t
                for i2 in range(2):
                    TT("dve", lamt[:, 0:64], cst[:, CST_LAM + i2 * 128:CST_LAM + i2 * 128 + 64],
                       cst[:, CST_LAM + i2 * 128 + 64:CST_LAM + i2 * 128 + 128], ALU.mult, [R_cst], [R_tq])
                    S.op("dve", lambda e, i2=i2: e.reduce_sum(out=sml[:, i2:i2 + 1], in_=lamt[:, 0:64], axis=mybir.AxisListType.X),
                         reads=[R_tq], writes=[R_sml])
                ACT(sml[:, 2:4], sml[:, 0:2], AF.Exp, [R_sml], [R_sml])
                TT("dve", sml[:, 4:5], sml[:, 3:4], sml[:, 2:3], ALU.subtract, [R_sml], [R_sml])
                TS("dve", sml[:, 4:5], sml[:, 4:5], -lam_init, None, ALU.add, ALU.bypass, [R_sml], [R_sml])
                TS("dve", sml[:, 5:6], cst[:, CST_SUB:CST_SUB + 1], 1.0 - lam_init, None, ALU.mult, ALU.bypass, [R_cst], [R_sml])
                neglam = sml[:, 4:5]
                gsub = sml[:, 5:6]

                for which in range(2):
                    for s in range(2):
                        wv, wres = w_next(wain[which * 4 + s], (8, 512))
                        wv2, wres2 = w_next(wain[which * 4 + 2 + s], (8, 512), keep=1)
                        for hh in range(4):
                            hd = 4 * s + hh
                            cs = slice(hh * 128, (hh + 1) * 128)
                            pa, rpa = gen.get()
                            for k in range(8):
                                MM(pa[:, :], wv[:, k, cs], hT[:, k, tbs(0)], k == 0, k == 7, [wres, R_h[k][0]], rpa)
                            if which == 0:
                                CP("act", qTp[:, hd, :], pa[:, :], [rpa], [R_qp[hd]])
                            else:
                                CP("act", kTp[:, hd, :], pa[:, :], [rpa], [R_kp[hd]])
                                fo_, rfo, sfo = fst.get()
                                CP("dve", fo_, pa[:, :], [rpa], [rfo])
                                DMA("sp", nkT[:, hd, :], fo_, [rfo], [R_dram["nkT"]], sfo)
                            for tb in (1, 2):
                                pa, rpa = gen.get()
                                pb, rpb = gen.get()
                                for k in range(8):
                                    MM(pa[:, :], wv[:, k, cs], hT[:, k, tbs(tb)], k == 0, k == 7, [wres, R_h[k][tb]], rpa)
                                for k in range(8):
                                    MM(pb[:, :], wv2[:, k, cs], hT[:, k, tbs(tb)], k == 0, k == 7, [wres2, R_h[k][tb]], rpb)
                                a1, ra1 = r1.get()
                                a2, ra2 = r2.get()
                                lc = slice((tb - 1) * 512, tb * 512)
                                TT("dve", a1, pb[:, :], sinT[:, lc], ALU.mult, [rpb, R_rope], [ra1])
                                TT("dve", a2, pa[:, :], cosT[:, lc], ALU.mult, [rpa, R_rope], [ra2])
                                if which == 0:
                                    TT("pool", qTs[:, hd, lc], a1, a2, ALU.add, [ra1, ra2], [R_qs[hd][tb - 1]])
                                else:
                                    ks, rks, sks = kst.get()
                                    TT("pool", ks, a1, a2, ALU.add, [ra1, ra2], [rks])
                                    DMA("sp", cinK[hd * 128:(hd + 1) * 128, lc], ks, [rks], [R_dram["cinK"]], sks)
                S.op("pool", lambda e: e.collective_compute(
                    "AllGather", ALU.bypass, replica_groups=RG,
                    ins=[cinK.ap().opt()], outs=[coutK.ap().opt()]),
                    reads=[R_dram["cinK"]], writes=[R_dram["coutK"]], dsem=sem_cc, inc=1)
                for s in range(2):
                    wv, wres = w_next(wain[8 + s], (8, 512))
                    for t in range(12):
                        tb = t // 4
                        pa, rpa = gen.get()
                        for k in range(8):
                            MM(pa[:, :], hT[:, k, t * 128:(t + 1) * 128], wv[:, k, :], k == 0, k == 7, [wres, R_h[k][tb]], rpa)
                        if t < 4:
                            CP("act", Vp[:, t, s * 512:(s + 1) * 512], pa[:, :], [rpa], [R_vp[t][s]])
                            fo_, rfo, sfo = fst.get()
                            CP("dve", fo_, pa[:, :], [rpa], [rfo])
                            DMA("sp", nv[:, t, s * 512:(s + 1) * 512], fo_, [rfo], [R_dram["nv"]], sfo)
                        else:
                            vs, rvs, svs = vst.get()
                            CP("act", vs, pa[:, :], [rpa], [rvs])
                            DMA("sp", cinV[(t - 4) * 128:(t - 3) * 128, s * 512:(s + 1) * 512], vs, [rvs],
                                [R_dram["cinV"]], svs)
                S.op("pool", lambda e: e.collective_compute(
                    "AllGather", ALU.bypass, replica_groups=RG,
                    ins=[cinV.ap().opt()], outs=[coutV.ap().opt()]),
                    reads=[R_dram["cinV"]], writes=[R_dram["coutV"]], dsem=sem_cc, inc=1)

                ar.pop(mark)
                kbuf_t = ar.alloc([2, 2, 2048], BF16, "kbuf")
                vbuf_t = ar.alloc([2, 2048], BF16, "vbuf")
                kvb = Ring([(kbuf_t[:, i, :, :], vbuf_t[:, i, :], ar.res(), ar.res(), sem_kv[2 * i], sem_kv[2 * i + 1]) for i in range(2)])
                S.op("dve", lambda e: e.memset(kbuf_t[:, :, :, :], 0.0), writes=[it_[2] for it_ in kvb.items])
                eT_t = ar.alloc([3, 2, 512], BF16, "eT")
                eTr = Ring([(eT_t[:, i, :, :], ar.res()) for i in range(3)])
                (osb, R_osb), (rz_t, R_rz) = b.tmp.items[0], b.tmp.items[1]
                tq_t = ar.alloc([512], F32, "tq"); R_tq = ar.res()
                es_t = ar.alloc([3, 512], BF16, "esum")
                esr = Ring([(es_t[:, i, :], ar.res()) for i in range(3)])

                def attn_norm(n, hd, col0):
                    tb = col0 // 512
                    sq, rsq = b.sq.get()
                    ACT(sq[:, 0:n], osb[:, 0:n], AF.Square, [R_osb], [rsq])
                    pn, rpn = gen.get()
                    MM(pn[:, 0:n], ones, sq[:, 0:n], True, True, [rsq, R_cb], rpn)
                    ACT(b.rt[:, 0:n], pn[:, 0:n], AF.Ln, [rpn], [b.R_rt], scale=1.0 / 128.0, bias=EPS)
                    ACT(rz_t[:, 0:n], b.rt[:, 0:n], AF.Exp, [b.R_rt], [R_rz], scale=-0.5)
                    STT(hT[:, hd, col0:col0 + n], osb[:, 0:n], gsub, rz_t[:, 0:n], ALU.mult, ALU.mult,
                        [R_osb, R_sml, R_rz], [R_h[hd][tb]])

                def pstageA(seq, hd):
                    qc = slice(seq * 256, (seq + 1) * 256)
                    e2, re_ = eTr.get()
                    for kc in range(2):
                        kt = 2 * seq + kc
                        for c in range(2):
                            ps_ = slice(c * 64, (c + 1) * 64)
                            pa, rpa = gen.get()
                            MM(pa[:, 0:256], kTp[ps_, hd, kt * 128:(kt + 1) * 128], qTp[ps_, hd, qc],
                               True, True, [R_kp[hd], R_qp[hd]], rpa)
                            ACT(e2[:, kc, c * 256:(c + 1) * 256], pa[:, 0:256], AF.Exp, [rpa], [re_], scale=0.125)
                    return (seq, hd, e2, re_)

                def pstageB(it):
                    seq, hd, e2, re_ = it
                    po, rpo = psb[4], R_ps[4]
                    pz, rpz = psb[5], R_ps[5]
                    for kc in range(2):
                        kt = 2 * seq + kc
                        MM(po[:, :], Vp[:, kt, hd * 128:(hd + 1) * 128], e2[:, kc, :], kc == 0, kc == 1,
                           [re_, R_vp[kt][hd // 4]], rpo)
                    for kc in range(2):
                        MM(pz[:, :], ones, e2[:, kc, :], kc == 0, kc == 1, [re_, R_cb], rpz)
                    ACT(b.rt[:, :], pz[:, :], AF.Ln, [rpz], [b.R_rt])
                    ACT(rz_t[:, :], b.rt[:, :], AF.Exp, [b.R_rt], [R_rz], scale=-1.0)
                    TT("dve", tq_t[:, :], po[:, :], rz_t[:, :], ALU.mult, [rpo, R_rz], [R_tq])
                    STT(osb[:, 0:256], tq_t[:, 256:512], neglam, tq_t[:, 0:256], ALU.mult, ALU.add,
                        [R_tq, R_sml], [R_osb])
                    attn_norm(256, hd, seq * 256)

                prevb = None
                for seq in range(2):
                    for hd in range(8):
                        curb = pstageA(seq, hd)
                        if prevb is not None:
                            pstageB(prevb)
                        prevb = curb
                pstageB(prevb)

                coutKv = coutK.ap().rearrange("(r x p) n -> p r x n", r=2, p=128)
                coutVv = coutV.ap().rearrange("(r x p) n -> p r x n", r=2, p=128)
                kvs = {}

                def load_kv(hd):
                    kb, vb, rkb, rvb, skk, skv = kvb.get()
                    for c_ in range(2):
                        DMA("sp", kb[c_ * 64:(c_ + 1) * 64, c_, :].rearrange("p (r n) -> p r n", r=2),
                            coutKv[c_ * 64:(c_ + 1) * 64, :, hd, :], [R_dram["coutK"]], [rkb], skk)
                    for r_ in range(2):
                        DMA("sp", vb[:, r_ * 1024:(r_ + 1) * 1024].rearrange("p (t v) -> p t v", t=8),
                            coutVv[:, r_, :, hd * 128:(hd + 1) * 128], [R_dram["coutV"]], [rvb], skv)
                    kvs[hd] = (kb, vb, rkb, rvb)

                pairs = Ring([0, 1, 2])
                po, rpo, pz, rpz = psb[6], R_ps[6], psb[7], R_ps[7]

                def stage1(hd, qb, c, kp):
                    kb, vb, rkb, rvb = kvs[hd]
                    qcs = slice(qb * 512, (qb + 1) * 512)
                    pi = pairs.get()
                    vls = []
                    for u in range(2):
                        kc = 2 * kp + u
                        if kc < 16:
                            kl = kb[:, c, kc * 128:(kc + 1) * 128]
                            vl = vb[:, kc * 128:(kc + 1) * 128]
                            rk_, rv_ = rkb, rvb
                        else:
                            kl = ckT[:, c, hd, (kc - 16) * 128:(kc - 15) * 128]
                            vl = cvb[:, kc - 16, hd * 128:(hd + 1) * 128]
                            rk_, rv_ = R_ck, R_cv
                        MM(psb[2 * pi + u][:, :], kl, qTs[:, hd, qcs], True, True, [rk_, R_qs[hd][qb]], R_ps[2 * pi + u])
                        vls.append((vl, rv_))
                    e2, re_ = eTr.get()
                    ACT(e2, psall[:, 2 * pi * 512:(2 * pi + 2) * 512].rearrange("p (a b) -> p a b", a=2), AF.Exp,
                        [R_ps[2 * pi], R_ps[2 * pi + 1]], [re_], scale=0.125)
                    es, res_ = esr.get()
                    TT("dve", es, e2[:, 0, :], e2[:, 1, :], ALU.add, [re_], [res_])
                    return (hd, qb, c, kp, e2, re_, vls, es, res_)

                def stage2(it):
                    hd, qb, c, kp, e2, re_, vls, es, res_ = it
                    for u in range(2):
                        kc = 2 * kp + u
                        vl, rv_ = vls[u]
                        MM(po[:, :], vl, e2[:, u, :], kc == 0, kc == 17, [re_, rv_], rpo)
                    MM(pz[:, :], ones, es, kp == 0, kp == 8, [res_, R_cb], rpz)
                    if kp == 8:
                        ACT(b.rt[:, :], pz[:, :], AF.Ln, [rpz], [b.R_rt])
                        ACT(rz_t[:, :], b.rt[:, :], AF.Exp, [b.R_rt], [R_rz], scale=-1.0)
                        if c == 0:
                            TT("dve", tq_t[:, :], po[:, :], rz_t[:, :], ALU.mult, [rpo, R_rz], [R_tq])
                        else:
                            TT("dve", osb[:, :], po[:, :], rz_t[:, :], ALU.mult, [rpo, R_rz], [R_osb])
                            STT(osb[:, :], osb[:, :], neglam, tq_t[:, :], ALU.mult, ALU.add, [R_tq, R_sml, R_osb], [R_osb])
                            attn_norm(512, hd, 512 + qb * 512)

                pend = []
                load_kv(0)
                for hd in range(8):
                    for qb in range(2):
                        for c in range(2):
                            for kp in range(9):
                                if hd + 1 < 8 and qb == 0 and c == 0 and kp == 4:
                                    load_kv(hd + 1)
                                pend.append(stage1(hd, qb, c, kp))
                                if len(pend) > 2:
                                    stage2(pend.pop(0))
                while pend:
                    stage2(pend.pop(0))

                yb = ar.view(0, [8, T], BF16)
                olds = R_qp + R_kp + [r_ for rr in R_vp for r_ in rr]
                R_yb = [[ar.res_from(olds) for _ in range(NTB)] for _ in range(8)]
                for s in range(2):
                    wv, wres = w_next(waout[s], (8, 512))
                    for ff in range(4):
                        fo = 4 * s + ff
                        for tb in range(NTB):
                            pa, rpa = gen.get()
                            for hd in range(8):
                                MM(pa[:, :], wv[:, hd, ff * 128:(ff + 1) * 128], hT[:, hd, tbs(tb)], hd == 0, hd == 7,
                                   [wres, R_h[hd][tb]], rpa)
                            post_chunk(b, pa, rpa, fo, tb, yb, R_yb)
                post_finish(b, l, j, yb, R_yb, nxt)

            def hgrn(nxt=None):
                l, j = 1, 1
                ar.reset()
                b = norm_bufs()
                on = ar.alloc([8, T], BF16, "on"); R_on = ar.resgrid(8, NTB)
                itok = ar.alloc([12, 128], BF16, "itok"); R_it = ar.res()
                T1 = ar.alloc([T], F32, "T1"); R_T1 = ar.res()
                T2 = ar.alloc([T], F32, "T2"); R_T2 = ar.res()
                PB = ar.alloc([T + 1], F32, "PB"); R_PB = ar.res()
                qb_ = ar.alloc([T], BF16, "qb"); R_qb = ar.res()
                sg = ar.alloc([T], BF16, "sg"); R_sg = ar.res()
                kx = ar.alloc([T], BF16, "kx"); R_kx = ar.res()
                qt = ar.alloc([1, T], BF16, "qt"); R_qt = [ar.res()] * 2
                kt = ar.alloc([1, T], BF16, "kt"); R_kt = [ar.res()] * 2
                AT = ar.alloc([12, 128], BF16, "AT"); R_AT = [ar.res()] * 2
                ktok_off = ar.off
                ktok = ar.alloc([2, 12, 128], BF16, "ktok"); R_ktok = [ar.res()] * 2
                sh_t = ar.alloc([2, 128], BF16, "shat")
                shr = Ring([(sh_t[:, i, :], ar.res()) for i in range(2)])
                Sst_t = ar.alloc([4, 128], F32, "Sst")
                sring = Ring([(Sst_t[:, i, :], ar.res()) for i in range(4)])
                td_t = ar.alloc([2, 4, 128], F32, "td")
                tdr = Ring([(td_t[:, i, :, :], ar.res()) for i in range(2)])
                csc = ar.alloc([2, 3, 24], F32, "csc"); R_csc = ar.resgrid(2)
                cdf = ar.alloc([3, 24], F32, "cdf"); R_cdf = ar.res()
                lbv = ar.alloc([2, 2, 8], F32, "lbv"); R_lb = ar.res()
                nol = ar.alloc([2, 8], F32, "nol")
                S0 = ar.alloc([8, 128], F32, "S0"); R_S0 = ar.res()
                Srv = ar.view(ktok_off, [8, 128], F32)
                R_Srv = R_ktok[0]
                SinB = ar.alloc([8, 128], F32, "SinB"); R_SinB = ar.res()
                Sout = ar.alloc([1, 128], F32, "Sout")
                sor = Ring([(Sout[:, i, :], ar.res(), sem_misc[i]) for i in range(1)])
                (osb, R_osb), (rz_t, R_rz) = b.tmp.items[0], b.tmp.items[1]

                DMA("sp", S0[:, :, :], s0_in[:, :, :], [], [R_S0], sem_in[5])
                S.op("dve", lambda e: e.memset(PB[:, 0:1], 0.0), writes=[R_PB])
                for X in range(2):
                    l0 = cst[:, CST_LB + X * 8:CST_LB + X * 8 + 8]
                    l1 = cst[:, CST_LB + 16 + X * 8:CST_LB + 16 + X * 8 + 8]
                    TT("dve", lbv[:, X, 1, :], l1, l0, ALU.subtract, [R_cst], [R_lb])
                    ACT(lbv[:, X, 0, :], lbv[:, X, 1, :], AF.Sigmoid, [R_lb], [R_lb])
                    TS("dve", lbv[:, X, 1, :], lbv[:, X, 0, :], -1.0, 1.0, ALU.mult, ALU.add, [R_lb], [R_lb])
                    TS("dve", nol[:, X, :], lbv[:, X, 1, :], -1.0, None, ALU.mult, ALU.bypass, [R_lb], [R_lb])
                gn = cst[:, CST_GN:CST_GN + 1]

                SEGS = [(0, 256, 0), (256, 512, 1), (512, 1536, 2)]

                class P:
                    pass

                def fp1(p):
                    hd, X, full = p.hd, p.X, p.full
                    p.wv, p.wres = w_next(whin[hd], (8, 512))
                    p.wi, p.wires = w_next(whi[hd], (8, 128), keep=1)
                    wv, wres = p.wv, p.wres
                    tb_list = range(NTB) if full else (1, 2)
                    t_lo = 0 if full else 512
                    tl = slice(t_lo, T)
                    p.t_lo, p.tl = t_lo, tl
                    if full:
                        for tb in tb_list:
                            pa, rpa = gen.get()
                            for k in range(8):
                                MM(pa[:, :], wv[:, k, 0:128], hT[:, k, tbs(tb)], k == 0, k == 7, [wres, R_h[k][tb]], rpa)
                            CP("dve", qb_[:, tbs(tb)], pa[:, :], [rpa], [R_qb])
                    yield
                    lb_ = lbv[:, X, 0, hd:hd + 1]
                    oml_ = lbv[:, X, 1, hd:hd + 1]
                    nol_ = nol[:, X, hd:hd + 1]
                    for tb in tb_list:
                        pa, rpa = gen.get()
                        for k in range(8):
                            MM(pa[:, :], wv[:, k, 128 * (1 + X):128 * (2 + X)], hT[:, k, tbs(tb)], k == 0, k == 7,
                               [wres, R_h[k][tb]], rpa)
                        ACT(T1[:, tbs(tb)], pa[:, :], AF.Sigmoid, [rpa], [R_T1])
                    yield
                    ACT(T2[:, tl], T1[:, tl], AF.Ln, [R_T1, R_lb], [R_T2], scale=oml_, bias=lb_)
                    TS("dve", kx[:, tl], T1[:, tl], nol_, oml_, ALU.mult, ALU.add, [R_T1, R_lb], [R_kx])
                    yield
                    S.op("dve", lambda e, t_lo=t_lo: e.memset(PB[:, t_lo:t_lo + 1], 0.0), reads=[R_cdf], writes=[R_PB])
                    S.op("dve", lambda e, tl=tl: e.tensor_tensor_scan(
                        out=PB[:, 1 + tl.start:1 + tl.stop], data0=onesrow[:, tl], data1=T2[:, tl], initial=0.0,
                        op0=ALU.mult, op1=ALU.add), reads=[R_T2, R_cb], writes=[R_PB])
                    yield
                    c_lo = t_lo // 64
                    nch = (T - t_lo) // 64
                    sh_ = 1 if X == 0 else 0
                    Pv = PB[:, sh_ + t_lo:sh_ + T].rearrange("p (c t) -> p c t", t=64)
                    ref = PB[:, sh_ + t_lo + 32:sh_ + t_lo + 32 + 64 * (nch - 1) + 1:64].unsqueeze(2).broadcast_to([128, nch, 64])
                    TT("dve", T2[:, tl].rearrange("p (c t) -> p c t", t=64), Pv, ref, ALU.subtract, [R_PB, R_T2], [R_T2])
                    yield

                    def pcol(off):
                        return PB[:, t_lo + off:t_lo + off + 64 * (nch - 1) + 1:64]
                    if X == 0:
                        pairs = [(pcol(64), pcol(0)), (pcol(64), pcol(33)), (pcol(33), pcol(0))]
                    else:
                        pairs = [(pcol(64), pcol(0)), (pcol(32), pcol(0)), (pcol(64), pcol(32))]
                    for i2, (pa_, pb_) in enumerate(pairs):
                        TT("dve", cdf[:, i2, c_lo:c_lo + nch], pa_, pb_, ALU.subtract, [R_PB], [R_cdf])
                    ACT(csc[:, p.slot, :, c_lo:c_lo + nch], cdf[:, :, c_lo:c_lo + nch], AF.Exp, [R_cdf], [R_csc[p.slot]])
                    yield
                    ACT(T1[:, tl], T2[:, tl], AF.Exp, [R_T2], [R_T1])
                    yield
                    ACT(T2[:, tl], T2[:, tl], AF.Exp, [R_T2], [R_T2], scale=-1.0)

                def fp2(p):
                    hd, X, full, tl = p.hd, p.X, p.full, p.tl
                    if X == 1:
                        for tb in range(NTB):
                            pa, rpa = gen.get()
                            for k in range(8):
                                MM(pa[:, :], p.wv[:, k, 384:512], hT[:, k, tbs(tb)], k == 0, k == 7, [p.wres, R_h[k][tb]], rpa)
                            ACT(sg[:, tbs(tb)], pa[:, :], AF.Sigmoid, [rpa], [R_sg])
                    if True:
                        wi, wires = p.wi, p.wires
                        for g in range(3):
                            if not full and g == 0:
                                continue
                            pa, rpa = gen.get()
                            for tt in range(4):
                                t = 4 * g + tt
                                for k in range(8):
                                    MM(pa[:, tt * 128:(tt + 1) * 128], hT[:, k, t * 128:(t + 1) * 128], wi[:, k, :], k == 0, k == 7,
                                       [wires, R_h[k][g]], rpa)
                            CP("act", itok[:, 4 * g:4 * g + 4, :], pa[:, :].rearrange("p (a b) -> p a b", a=4), [rpa], [R_it])
                    Eq, Ek = (T1, T2) if X == 0 else (T2, T1)
                    if full:
                        TT("dve", qt[:, 0, tl], qb_[:, tl], Eq[:, tl], ALU.mult, [R_qb, R_T1, R_T2], [R_qt[X]])
                    TT("dve", kt[:, 0, tl], kx[:, tl], Ek[:, tl], ALU.mult, [R_kx, R_T1, R_T2], [R_kt[X]])
                    mask = maskA if X == 0 else maskB
                    for g in range(3):
                        if not full and g == 0:
                            continue
                        if full:
                            pa, rpa = gen.get()
                            for tt in range(4):
                                t = 4 * g + tt
                                tok = slice(t * 128, (t + 1) * 128)
                                MM(pa[:, tt * 128:(tt + 1) * 128], kt[:, 0, tok], qt[:, 0, tok], True, True,
                                   [R_kt[X], R_qt[X]], rpa)
                            TT("dve", AT[:, 4 * g:4 * g + 4, :], pa[:, :].rearrange("p (a b) -> p a b", a=4),
                               mask.unsqueeze(1).broadcast_to([128, 4, 128]), ALU.mult, [rpa, R_cb], [R_AT[X]])
                        pa, rpa = gen.get()
                        pab = pa[:, 0:256].bitcast(BF16)
                        for tt in range(4):
                            t = 4 * g + tt
                            TR(pab[:, tt * 128:(tt + 1) * 128], kt[:, 0, t * 128:(t + 1) * 128], [R_kt[X], R_cb], rpa)
                        ACT(ktok[:, 0, 4 * g:4 * g + 4, :], pab.rearrange("p (a b) -> p a b", a=4), AF.Identity,
                            [rpa, R_cst], [R_ktok[X]], scale=cst[:, CST_RM:CST_RM + 1])
                        TS("dve", ktok[:, 1, 4 * g:4 * g + 4, :], pab.rearrange("p (a b) -> p a b", a=4),
                           cst[:, CST_RM + 1:CST_RM + 2], None, ALU.mult, ALU.bypass, [rpa, R_cst], [R_ktok[X]])

                o_started = [False] * NTB

                def back(p, filler=None):
                    hd, X, full = p.hd, p.X, p.full
                    if full:
                        for g in range(3):
                            o_started[g] = False
                        for g in range(3):
                            ob, rob = psb[4 + g], R_ps[4 + g]
                            for tt in range(4):
                                t = 4 * g + tt
                                MM(ob[:, tt * 128:(tt + 1) * 128], itok[:, t, :], AT[:, t, :], not o_started[g], False,
                                   [R_it, R_AT[X]], rob, skip=True)
                                o_started[g] = True
                    for (t0, t1, kind) in SEGS:
                        if not full and kind != 2:
                            continue
                        chunks = list(range(t0 // 64, t1 // 64))
                        if X == 1:
                            chunks = chunks[::-1]
                        if kind == 2:
                            Sst, R_S = (S0[:, hd, :], R_S0) if X == 0 else (SinB[:, hd, :], R_SinB)
                        else:
                            Sst, R_S = sring.get()
                            S.op("dve", lambda e, Sst=Sst: e.memset(Sst, 0.0), writes=[R_S])
                        groups = [chunks[gi:gi + 4] for gi in range(0, len(chunks), 4)]

                        def pre_mm(grp):
                            pd, rpd = gen.get()
                            asc = sorted(grp)
                            for c in grp:
                                t, hf = c // 2, c % 2
                                i2 = asc.index(c)
                                MM(pd[:, i2 * 128:(i2 + 1) * 128], ktok[:, hf, t, :], itok[:, t, :], True, True,
                                   [R_ktok[X], R_it], rpd)
                            return (pd, rpd, asc)

                        def pre_td(pm, grp):
                            pd, rpd, asc = pm
                            n_ = len(asc)
                            td4, rtd = tdr.get()
                            TT("dve", td4[:, 0:n_, :], pd[:, 0:n_ * 128].rearrange("p (a b) -> p a b", a=n_),
                               csc[:, p.slot, 1, asc[0]:asc[0] + n_].unsqueeze(2).broadcast_to([128, n_, 128]), ALU.mult,
                               [rpd, R_csc[p.slot]], [rtd])
                            return [(td4[:, asc.index(c), :], rtd) for c in grp]

                        nxt_tds = pre_td(pre_mm(groups[0]), groups[0])
                        for gidx, grp in enumerate(groups):
                            tds = nxt_tds
                            pm_n = None
                            if gidx + 1 < len(groups):
                                pm_n = pre_mm(groups[gidx + 1])
                            for i2, c in enumerate(grp):
                                tb = c // 8
                                if full:
                                    sh2, rsh2 = shr.get()
                                    ACT(sh2, Sst, AF.Identity, [R_S, R_csc[p.slot]], [rsh2], scale=csc[:, p.slot, 2, c:c + 1])
                                    ob, rob = psb[4 + tb], R_ps[4 + tb]
                                    oc = slice((c % 8) * 64, (c % 8) * 64 + 64)
                                    MM(ob[:, oc], sh2, qt[:, 0, c * 64:(c + 1) * 64], False, False, [rsh2, R_qt[X]], rob, skip=True)
                                td, rtd = tds[i2]
                                Snx, R_Snx = sring.get()
                                STT(Snx, Sst, csc[:, p.slot, 0, c:c + 1], td, ALU.mult, ALU.add,
                                    [R_S, R_csc[p.slot], rtd], [R_Snx])
                                Sst, R_S = Snx, R_Snx
                            if pm_n is not None:
                                nxt_tds = pre_td(pm_n, groups[gidx + 1])
                            if filler is not None:
                                next(filler, None)
                        if kind == 2 and X == 0:
                            so, rso, sso = sor.get()
                            CP("dve", so, Sst, [R_S], [rso])
                            DMA("sp", cin2[hd * 128:(hd + 1) * 128, :], so, [rso], [R_dram["cin2"]], sso)
                        if kind != 2 and full:
                            so, rso, sso = sor.get()
                            CP("dve", so, Sst, [R_S], [rso])
                            DMA("sp", nS[:, kind, X, hd, :], so, [rso], [R_dram["nS"]], sso)
                    if filler is not None:
                        for _ in filler:
                            pass
                    if full and X == 0:
                        for tb in range(NTB):
                            CP("dve", on[:, hd, tbs(tb)], psb[4 + tb][:, :], [R_ps[4 + tb]], [R_on[hd][tb]])
                    if full and X == 1:
                        for tb in range(NTB):
                            ob, rob = psb[4 + tb], R_ps[4 + tb]
                            ont = on[:, hd, tbs(tb)]
                            TT("dve", ont, ob[:, :], ont, ALU.add, [rob, R_on[hd][tb]], [R_on[hd][tb]])
                            sq, rsq = b.sq.get()
                            ACT(sq, ont, AF.Square, [R_on[hd][tb]], [rsq])
                            pn, rpn = gen.get()
                            MM(pn[:, :], ones, sq, True, True, [rsq, R_cb], rpn)
                            ACT(b.rstd[:, tb, :], pn[:, :], AF.Ln, [rpn], [b.R_rstd[tb]], scale=1.0 / 128.0, bias=EPS)
                            ACT(b.rstd[:, tb, :], b.rstd[:, tb, :], AF.Exp, [b.R_rstd[tb]], [b.R_rstd[tb]], scale=-0.5)
                        for tb in range(NTB):
                            ont = on[:, hd, tbs(tb)]
                            STT(ont, ont, gn, b.rstd[:, tb, :], ALU.mult, ALU.mult, [R_on[hd][tb], R_cst, b.R_rstd[tb]], [R_on[hd][tb]])
                            TT("dve", ont, ont, sg[:, tbs(tb)], ALU.mult, [R_on[hd][tb], R_sg], [R_on[hd][tb]])
                    if X == 0 and hd == 7:
                        S.op("pool", lambda e: e.collective_compute(
                            "AllGather", ALU.bypass, replica_groups=RG,
                            ins=[cin2.ap().opt()], outs=[cout2.ap().opt()]),
                            reads=[R_dram["cin2"]], writes=[R_dram["cout2"]], dsem=sem_cc, inc=1)
                        c2v = cout2.ap().rearrange("(r h p) n -> p r h n", r=2, p=128)
                        DMA("sp", SinB[:, :, :], c2v[:, 0, :, :], [R_dram["cout2"]], [R_SinB], sem_in[6])
                        DMA("sp", Srv[:, :, :], c2v[:, 1, :, :], [R_dram["cout2"]], [R_Srv], sem_in[7])
                        TS("dve", SinB[:, :, :], SinB[:, :, :], cst[:, CST_M:CST_M + 1], None, ALU.mult, ALU.bypass, [R_cst, R_SinB], [R_SinB])
                        STT(SinB[:, :, :], Srv[:, :, :], cst[:, CST_M + 1:CST_M + 2], SinB[:, :, :], ALU.mult, ALU.add,
                            [R_Srv, R_cst, R_SinB], [R_SinB])

                passes = []
                for hd in range(8):
                    passes.append((hd, 0, True))
                for hd in range(8):
                    passes.append((hd, 1, True))
                plist = []
                prev = None
                for n_, (hd, X, full) in enumerate(passes):
                    p = P()
                    p.hd, p.X, p.full, p.slot = hd, X, full, n_ % 2
                    plist.append(p)
                    prev = p
                for n_, p in enumerate(plist):
                    if n_ == 0:
                        for _ in fp1(p):
                            pass
                        fp2(p)
                    fil = None
                    if n_ + 1 < len(plist):
                        fil = fp1(plist[n_ + 1])
                        next(fil, None)
                    back(p, fil)
                    if n_ + 1 < len(plist):
                        fp2(plist[n_ + 1])
                for s in range(2):
                    wv, wres = w_next(whout[s], (8, 512))
                    for ff in range(4):
                        fo = 4 * s + ff
                        for tb in range(NTB):
                            pa, rpa = gen.get()
                            for hd in range(8):
                                MM(pa[:, :], wv[:, hd, ff * 128:(ff + 1) * 128], on[:, hd, tbs(tb)], hd == 0, hd == 7,
                                   [wres, R_on[hd][tb]], rpa)
                            post_chunk(b, pa, rpa, fo, tb, hT, R_h)
                post_finish(b, l, j, hT, R_h, nxt)

            stages = ["none", "mod", "ffn00", "attn", "ffn01", "ffn10", "hgrn", "full"]
            lim = stages.index(stop) - 2
            def mk(l, s):
                return lambda: mod_slab(l, s)

            def fin(l, js):
                def f():
                    for j_ in js:
                        mod_fin(l, j_)
                return f
            if lim >= -1:
                for s_ in range(4):
                    mod_slab(0, s_)
                mod_fin(0, 0, "pre")
            if lim >= 0:
                prenorm(NBP, 0, 0)
                ffn(0, 0, 0, nxt=(0, 1) if lim >= 1 else None,
                    hooks=[mk(0, 4), mk(0, 5), lambda: mod_fin(0, 0, "post")] + [mk(0, s_) for s_ in range(6, 18)] + [fin(0, [1, 2])])
            if lim >= 1:
                attention(nxt=(0, 2) if lim >= 2 else None)
            if lim >= 2:
                ffn(0, 1, 2, nxt=(1, 0) if lim >= 3 else None,
                    hooks=[mk(1, s_) for s_ in range(18)] + [fin(1, [0, 1, 2])])
            if lim >= 3:
                ffn(1, 0, 0, nxt=(1, 1) if lim >= 4 else None)
            if lim >= 4:
                hgrn(nxt=(1, 2) if lim >= 5 else None)
            if lim >= 5:
                ffn(1, 1, 2)
            for tb in range(NTB):
                DMA("sp", yT[:, :, tbs(tb)], xT[:, :, tbs(tb)], [R_x[fo][tb] for fo in range(8)], [R_dram["yT"]], outsem.get())

        S.dry = True
        body()
        S.dry = False
        ar.off = 0
        ar.cur = []
        ar.prev = []
        body()
        with nc.Block() as block:
            S.emit(block, final_waits=sem_out + sem_misc)
    return nc


def _fm(v):
    v = np.asarray(v)
    n = v.shape[-1] // 128
    w = v.reshape(v.shape[:-1] + (n, 128))
    return np.moveaxis(w, -1, 0)


def _slab(wmat, cols):
    sub = wmat[:, cols]
    return np.ascontiguousarray(sub.reshape(8, 128, -1).transpose(1, 0, 2))


_NC_CACHE = {}


def _get_nc(stop):
    if stop not in _NC_CACHE:
        _NC_CACHE[stop] = build_program(stop)
    return _NC_CACHE[stop]


def kernel(x_prompt, x_sample, cache_k, cache_v, state_hgrn, c, c_ctx, w_mod, b_mod, norm_g,
           ffn_w_in, ffn_w_out, attn_w_in, attn_w_out, attn_lambda, attn_subln,
           hgrn_w_in, hgrn_w_out, hgrn_lower_bounds, hgrn_gnorm, _stop="full"):
    f32 = np.float32
    A = lambda a: np.asarray(a, dtype=f32)
    x_prompt, x_sample, cache_k, cache_v, state_hgrn = A(x_prompt), A(x_sample), A(cache_k), A(cache_v), A(state_hgrn)
    c, c_ctx, w_mod, b_mod, norm_g = A(c), A(c_ctx), A(w_mod), A(b_mod), A(norm_g)
    ffn_w_in, ffn_w_out, attn_w_in, attn_w_out = A(ffn_w_in), A(ffn_w_out), A(attn_w_in), A(attn_w_out)
    attn_lambda, attn_subln, hgrn_w_in, hgrn_w_out = A(attn_lambda), A(attn_subln), A(hgrn_w_in), A(hgrn_w_out)
    hgrn_lower_bounds, hgrn_gnorm = A(hgrn_lower_bounds), A(hgrn_gnorm)

    wmod_h = np.empty((2, 18, 128, 8, 512), f32)
    for l in range(2):
        for s in range(18):
            wmod_h[l, s] = _slab(w_mod[l], np.arange(s * 512, (s + 1) * 512))
    wfin_h = np.empty((2, 2, 11, 128, 8, 512), f32)
    wfout_h = np.empty((2, 2, 8, 128, 22, 128), f32)
    for l in range(2):
        for i in range(2):
            for s in range(11):
                cols = np.concatenate([np.arange(2 * s * 128, (2 * s + 2) * 128), FH + np.arange(2 * s * 128, (2 * s + 2) * 128)])
                wfin_h[l, i, s] = _slab(ffn_w_in[l, i], cols)
            wo = ffn_w_out[l, i].reshape(22, 128, 8, 128)
            wfout_h[l, i] = wo.transpose(2, 1, 0, 3)
    idx = np.arange(1024).reshape(8, 2, 2, 2, 16)
    swp = idx[:, :, :, ::-1, :].reshape(-1)
    wain_h = np.empty((10, 128, 8, 512), f32)
    for which in range(2):
        for s in range(2):
            cols = which * 1024 + np.arange(s * 512, (s + 1) * 512)
            wain_h[which * 4 + s] = _slab(attn_w_in[0], cols)
            cols2 = which * 1024 + swp[s * 512:(s + 1) * 512]
            wain_h[which * 4 + 2 + s] = _slab(attn_w_in[0], cols2)
    for s in range(2):
        wain_h[8 + s] = _slab(attn_w_in[0], 2048 + np.arange(s * 512, (s + 1) * 512))
    waout_h = np.stack([_slab(attn_w_out[0], np.arange(s * 512, (s + 1) * 512)) for s in range(2)])
    whout_h = np.stack([_slab(hgrn_w_out[0], np.arange(s * 512, (s + 1) * 512)) for s in range(2)])
    whi_h = np.stack([_slab(hgrn_w_in[0], 3072 + np.arange(hd * 128, (hd + 1) * 128)) for hd in range(8)])
    whin_par = []
    for par in range(2):
        arr = np.empty((8, 128, 8, 512), f32)
        for hd in range(8):
            r = np.arange(hd * 128, (hd + 1) * 128)
            gA, gB = (1024, 2048) if par == 0 else (2048, 1024)
            arr[hd] = _slab(hgrn_w_in[0], np.concatenate([r, gA + r, gB + r, 4096 + r]))
        whin_par.append(arr)
    cbh = np.zeros((128, CB_N), f32)
    cbh[:, CB_ID:CB_ID + 128] = np.eye(128, dtype=f32)
    cbh[:, CB_ONES:CB_ONES + 128] = 1.0
    pp = np.arange(128)[:, None]
    tt = np.arange(128)[None, :]
    same = (pp // 64) == (tt // 64)
    cbh[:, CB_MA:CB_MA + 128] = same & ((pp % 64) <= (tt % 64))
    cbh[:, CB_MB:CB_MB + 128] = same & ((pp % 64) >= (tt % 64))
    nf = 16
    inv = (np.float32(10000.0) ** (-np.arange(nf, dtype=f32) / np.float32(nf))).astype(f32)

    in_maps = []
    for core in range(8):
        pr, par = core // 2, core % 2
        segs = [x_prompt[2 * core], x_prompt[2 * core + 1], x_sample[pr, par * 1024:(par + 1) * 1024]]
        if par:
            segs = [s_[::-1] for s_ in segs]
        X = np.concatenate(segs, 0)
        xT_h = np.ascontiguousarray(X.T.reshape(8, 128, T).transpose(1, 0, 2))
        cst_h = np.zeros((128, CST_N), f32)
        cst_h[:, CST_LAM:CST_LAM + 256] = attn_lambda[0].reshape(1, 256)
        lbm = hgrn_lower_bounds[:, ::-1, :] if par else hgrn_lower_bounds
        cst_h[:, CST_LB:CST_LB + 32] = _fm(lbm).reshape(128, 32)
        cst_h[:, CST_BMOD:CST_BMOD + 144] = _fm(b_mod).reshape(128, 144)
        cst_h[:, CST_NG:CST_NG + 96] = _fm(norm_g).reshape(128, 96)
        cst_h[:, CST_GN] = hgrn_gnorm[0]
        cst_h[:, CST_SUB] = attn_subln[0]
        cst_h[:, CST_C:CST_C + 16] = np.stack([_fm(c_ctx), _fm(c[pr])], -1).reshape(128, 16)
        cst_h[:, CST_M] = 1.0 if par else 0.0
        cst_h[:, CST_M + 1] = 0.0 if par else 1.0
        cst_h[:64, CST_RM] = 1.0
        cst_h[64:, CST_RM + 1] = 1.0
        pos = par * 1024 + np.arange(1024)
        if par:
            pos = pos[::-1]
        row = (pos // 64).astype(f32)
        col = (pos % 64).astype(f32)
        p = np.arange(128) % 64
        axis, half, fr = p // 32, (p % 32) // 16, p % 16
        ang = np.where(axis[:, None] == 0, row[None, :], col[None, :]).astype(f32) * inv[fr][:, None]
        rope_h = np.stack([np.cos(ang), np.where(half[:, None] == 0, -np.sin(ang), np.sin(ang))], 1).astype(f32)
        ck = cache_k[pr, 0]
        ck_h = np.ascontiguousarray(ck.transpose(2, 3, 1, 0).reshape(128, 8, 256))
        cv_h = np.ascontiguousarray(cache_v[pr, 0].reshape(2, 128, 1024).transpose(1, 0, 2))
        s0_h = np.ascontiguousarray(state_hgrn[pr, 0, par].transpose(1, 0, 2))
        in_maps.append(dict(xT_in=xT_h, cst=cst_h, cb=cbh, rope=np.ascontiguousarray(rope_h), ckT=ck_h, cv=cv_h, s0=s0_h,
                            wmod=wmod_h, wfin=wfin_h, wfout=wfout_h, wain=wain_h, waout=waout_h,
                            whin=whin_par[par], whi=whi_h, whout=whout_h))
    nc = _get_nc(_stop)
    ncores = int(os.environ.get("K_CORES", "8"))
    res = run_bass_kernel_spmd(nc, in_maps[:ncores], core_ids=list(range(ncores)))
    return _assemble(res.results, ncores)


def _assemble(results, ncores=8):
    f32 = np.float32

    y_prompt = np.zeros((16, 256, 1024), f32)
    y_sample = np.zeros((4, 2048, 1024), f32)
    nck = np.zeros((16, 1, 256, 8, 2, 64), f32)
    ncv = np.zeros((16, 1, 256, 8, 128), f32)
    nst = np.zeros((16, 1, 2, 8, 128, 128), f32)
    for core in range(ncores):
        r = results[core]
        pr, par = core // 2, core % 2
        Y = np.asarray(r["yT"]).reshape(128, 8, T).transpose(2, 1, 0).reshape(T, 1024)
        K = np.asarray(r["nkT"]).reshape(128, 8, 512).transpose(2, 1, 0).reshape(512, 8, 2, 64)
        V = np.asarray(r["nv"]).reshape(128, 4, 1024).transpose(1, 0, 2).reshape(512, 8, 128)
        St = np.asarray(r["nS"]).reshape(128, 2, 2, 8, 128)
        for sq in range(2):
            ys, ks, vs = Y[sq * 256:(sq + 1) * 256], K[sq * 256:(sq + 1) * 256], V[sq * 256:(sq + 1) * 256]
            if par:
                ys, ks, vs = ys[::-1], ks[::-1], vs[::-1]
            y_prompt[2 * core + sq] = ys
            nck[2 * core + sq, 0] = ks
            ncv[2 * core + sq, 0] = vs
            for X in range(2):
                d = X if par == 0 else 1 - X
                nst[2 * core + sq, 0, d] = St[:, sq, X].transpose(1, 0, 2)
        ysm = Y[512:]
        if par:
            ysm = ysm[::-1]
        y_sample[pr, par * 1024:(par + 1) * 1024] = ysm
    return (y_prompt, y_sample, nck, ncv, nst)
```
